# Optimizing a Trainium2 kernel written in Bass

```python
import jax, jax.numpy as jnp
from jax import lax
import numpy as np

D_MODEL = 1024
BATCH = 8
SEQ = 2048
DEPTH = 4

CHUNK = 64
N_MIXERS = 3
N_GLA = (DEPTH + 2) // 3
N_MLA = (DEPTH + 1) // 3
N_CONV = DEPTH // 3
ALPHA = (2 * DEPTH) ** 0.25
BETA = (8 * DEPTH) ** -0.25
LN_EPS = 1e-5
RMS_EPS = 1e-6
PLE_DIM = 256
D_FF = 4 * D_MODEL
MAX_OFFSET = 4096

GLA_HEADS = 4
GLA_DK = D_MODEL // 2 // GLA_HEADS
GLA_DV = D_MODEL // GLA_HEADS
GLA_GATE_RANK = 16
GLA_TAU = 16.0
GLA_HK = GLA_HEADS * GLA_DK
GLA_HV = GLA_HEADS * GLA_DV
GLA_SPLITS = [GLA_HK, 2 * GLA_HK, 2 * GLA_HK + GLA_HV, 2 * GLA_HK + GLA_HV + D_MODEL]
GLA_IN = 2 * GLA_HK + GLA_HV + D_MODEL + GLA_GATE_RANK

MLA_HEADS = 8
MLA_NOPE = 128
MLA_ROPE = 64
MLA_V = 128
MLA_Q_RANK = 256
MLA_KV_RANK = 256
MLA_IN = MLA_Q_RANK + MLA_KV_RANK + MLA_ROPE
ROPE_BASE = 10000.0
Q_BLOCK = 128

CONV_WIDTH = 3

kernel_name = 'hybrid_gla_mla_shortconv_deepnorm_trunk'


def layer_norm(x, g, b):
    xf = x.astype(jnp.float32)
    mu = jnp.mean(xf, -1, keepdims=True)
    var = jnp.mean(jnp.square(xf - mu), -1, keepdims=True)
    return ((xf - mu) * lax.rsqrt(var + LN_EPS) * g + b).astype(x.dtype)


def rms_norm(x, g):
    xf = x.astype(jnp.float32)
    return (xf * lax.rsqrt(jnp.mean(xf * xf, -1, keepdims=True) + RMS_EPS) * g).astype(x.dtype)


def rope(x, cos, sin):
    x1, x2 = jnp.split(x, 2, axis=-1)
    return jnp.concatenate([x1 * cos - x2 * sin, x2 * cos + x1 * sin], axis=-1)


def gla_mixer(x, w_in, w_gate_up, b_gate, norm_g, w_out):
    B_, S_, _ = x.shape
    nc = S_ // CHUNK
    q, k, v, r, g_lr = jnp.split(x @ w_in, GLA_SPLITS, axis=-1)
    log_a = jax.nn.log_sigmoid((g_lr @ w_gate_up + b_gate).astype(jnp.float32)) / GLA_TAU

    def to_chunks(t, d):
        return t.astype(jnp.float32).reshape(B_, nc, CHUNK, GLA_HEADS, d).transpose(1, 0, 3, 2, 4)

    qc = to_chunks(q, GLA_DK) * GLA_DK ** -0.5
    kc = to_chunks(k, GLA_DK)
    vc = to_chunks(v, GLA_DV)
    lc = to_chunks(log_a, GLA_DK)

    def step(state, inp):
        q_, k_, v_, la = inp
        L = jnp.cumsum(la, axis=2)
        decay = jnp.exp(-jnp.abs(L[:, :, :, None, :] - L[:, :, None, :, :]))
        scores = jnp.einsum('bhtd,bhsd,bhtsd->bhts', q_, k_, decay)
        o = scores @ v_ + (q_ * jnp.exp(L)) @ state
        L_end = L[:, :, -1:, :]
        state = (jnp.exp(L_end[:, :, 0, :, None]) * state
                 + jnp.einsum('bhsd,bhse->bhde', k_ * jnp.exp(L_end - L), v_))
        return state, o

    s0 = jnp.zeros((B_, GLA_HEADS, GLA_DK, GLA_DV), jnp.float32)
    _, o = lax.scan(step, s0, (qc, kc, vc, lc))
    o = o.transpose(1, 0, 3, 2, 4).reshape(B_, S_, GLA_HEADS, GLA_DV)
    o = rms_norm(o, norm_g).reshape(B_, S_, GLA_HV) * jax.nn.silu(r.astype(jnp.float32))
    return o.astype(x.dtype) @ w_out


def mla_mixer(x, cos, sin, w_in, q_norm, kv_norm, w_uq, w_ukv, w_out):
    B_, S_, _ = x.shape
    c_q, c_kv, k_rope = jnp.split(x @ w_in, [MLA_Q_RANK, MLA_Q_RANK + MLA_KV_RANK], axis=-1)
    q = (rms_norm(c_q, q_norm) @ w_uq).reshape(B_, S_, MLA_HEADS, MLA_NOPE + MLA_ROPE)
    kv = (rms_norm(c_kv, kv_norm) @ w_ukv).reshape(B_, S_, MLA_HEADS, MLA_NOPE + MLA_V)
    q_nope, q_rope = jnp.split(q, [MLA_NOPE], axis=-1)
    k_nope, v = jnp.split(kv, [MLA_NOPE], axis=-1)
    q_rope = rope(q_rope, cos[:, :, None, :], sin[:, :, None, :])
    k_rope = rope(k_rope, cos, sin)
    qf = jnp.concatenate([q_nope.astype(jnp.float32), q_rope.astype(jnp.float32)], axis=-1)
    qf = qf * (MLA_NOPE + MLA_ROPE) ** -0.5
    kf = jnp.concatenate([k_nope.astype(jnp.float32),
                          jnp.broadcast_to(k_rope.astype(jnp.float32)[:, :, None, :],
                                           (B_, S_, MLA_HEADS, MLA_ROPE))], axis=-1)
    n_qb = S_ // Q_BLOCK
    q_blocks = qf.reshape(B_, n_qb, Q_BLOCK, MLA_HEADS, MLA_NOPE + MLA_ROPE).transpose(1, 0, 2, 3, 4)
    key_chunk = jnp.arange(S_) // CHUNK

    def attend(args):
        qb, bi = args
        q_chunk = (bi * Q_BLOCK + jnp.arange(Q_BLOCK)) // CHUNK
        s = jnp.einsum('bqhd,bkhd->bhqk', qb, kf)
        s = jnp.where(key_chunk[None, :] <= q_chunk[:, None], s, -jnp.inf)
        pr = jax.nn.softmax(s, axis=-1)
        return jnp.einsum('bhqk,bkhd->bqhd', pr.astype(v.dtype), v)

    o = lax.map(attend, (q_blocks, jnp.arange(n_qb)))
    o = o.transpose(1, 0, 2, 3, 4).reshape(B_, S_, MLA_HEADS * MLA_V)
    return o.astype(x.dtype) @ w_out


def conv_mixer(x, w_in, conv_w, w_out):
    b, c, u = jnp.split(x @ w_in, 3, axis=-1)
    z = lax.conv_general_dilated(c * u, conv_w[:, None, :], window_strides=(1,),
                                 padding=[(CONV_WIDTH - 1, 0)],
                                 dimension_numbers=('NWC', 'WIO', 'NWC'),
                                 feature_group_count=D_MODEL)
    return (b * z) @ w_out


def sq_relu_mlp(x, w1, w2):
    return jnp.square(jax.nn.relu(x @ w1)) @ w2


def setup_inputs(seed: int = 0) -> dict:
    key = jax.random.key(seed)
    ks = jax.random.split(key, 24)

    def nrm(i, shape, scale):
        return jax.random.normal(ks[i], shape, jnp.float32) * scale

    x = nrm(0, (BATCH, SEQ, D_MODEL), 1.0)
    p = nrm(1, (DEPTH, BATCH, SEQ, PLE_DIM), 1.0)
    offsets = jax.random.randint(ks[2], (BATCH, 1), 0, MAX_OFFSET, dtype=jnp.int32)
    positions = (offsets + jnp.arange(SEQ, dtype=jnp.int32)[None, :]).astype(jnp.int32)
    return {
        'x': x,
        'p': p,
        'positions': positions,
        'gla_w_in': nrm(3, (N_GLA, D_MODEL, GLA_IN), D_MODEL ** -0.5),
        'gla_w_gate_up': nrm(4, (N_GLA, GLA_GATE_RANK, GLA_HK), GLA_GATE_RANK ** -0.5),
        'gla_b_gate': nrm(5, (N_GLA, GLA_HK), 0.1),
        'gla_norm_g': 1.0 + nrm(6, (N_GLA, GLA_DV), 0.01),
        'gla_w_out': nrm(7, (N_GLA, GLA_HV, D_MODEL), GLA_HV ** -0.5 * BETA),
        'mla_w_in': nrm(8, (N_MLA, D_MODEL, MLA_IN), D_MODEL ** -0.5),
        'mla_q_norm': 1.0 + nrm(9, (N_MLA, MLA_Q_RANK), 0.01),
        'mla_kv_norm': 1.0 + nrm(10, (N_MLA, MLA_KV_RANK), 0.01),
        'mla_w_uq': nrm(11, (N_MLA, MLA_Q_RANK, MLA_HEADS * (MLA_NOPE + MLA_ROPE)), MLA_Q_RANK ** -0.5),
        'mla_w_ukv': nrm(12, (N_MLA, MLA_KV_RANK, MLA_HEADS * (MLA_NOPE + MLA_V)), MLA_KV_RANK ** -0.5),
        'mla_w_out': nrm(13, (N_MLA, MLA_HEADS * MLA_V, D_MODEL), (MLA_HEADS * MLA_V) ** -0.5 * BETA),
        'conv_w_in': nrm(14, (N_CONV, D_MODEL, 3 * D_MODEL), D_MODEL ** -0.5),
        'conv_w': nrm(15, (N_CONV, CONV_WIDTH, D_MODEL), CONV_WIDTH ** -0.5),
        'conv_w_out': nrm(16, (N_CONV, D_MODEL, D_MODEL), D_MODEL ** -0.5 * BETA),
        'ln_g': 1.0 + nrm(17, (DEPTH, 2, D_MODEL), 0.01),
        'ln_b': nrm(18, (DEPTH, 2, D_MODEL), 0.01),
        'mlp_w1': nrm(19, (DEPTH, D_MODEL, D_FF), D_MODEL ** -0.5),
        'mlp_w2': nrm(20, (DEPTH, D_FF, D_MODEL), D_FF ** -0.5 * BETA),
        'ple_w_gate': nrm(21, (DEPTH, D_MODEL, D_MODEL), D_MODEL ** -0.5),
        'ple_w_proj': nrm(22, (DEPTH, PLE_DIM, D_MODEL), PLE_DIM ** -0.5),
    }


def reference(x, p, positions, gla_w_in, gla_w_gate_up, gla_b_gate, gla_norm_g, gla_w_out,
              mla_w_in, mla_q_norm, mla_kv_norm, mla_w_uq, mla_w_ukv, mla_w_out,
              conv_w_in, conv_w, conv_w_out, ln_g, ln_b, mlp_w1, mlp_w2,
              ple_w_gate, ple_w_proj):
    inv_freq = ROPE_BASE ** (-jnp.arange(0, MLA_ROPE // 2, dtype=jnp.float32) * (2.0 / MLA_ROPE))
    ang = positions.astype(jnp.float32)[..., None] * inv_freq
    cos, sin = jnp.cos(ang), jnp.sin(ang)
    for i in range(DEPTH):
        j = i // N_MIXERS
        kind = i % N_MIXERS
        if kind == 0:
            h = gla_mixer(x, gla_w_in[j], gla_w_gate_up[j], gla_b_gate[j], gla_norm_g[j], gla_w_out[j])
        elif kind == 1:
            h = mla_mixer(x, cos, sin, mla_w_in[j], mla_q_norm[j], mla_kv_norm[j],
                          mla_w_uq[j], mla_w_ukv[j], mla_w_out[j])
        else:
            h = conv_mixer(x, conv_w_in[j], conv_w[j], conv_w_out[j])
        x = layer_norm(ALPHA * x + h, ln_g[i, 0], ln_b[i, 0])
        x = layer_norm(ALPHA * x + sq_relu_mlp(x, mlp_w1[i], mlp_w2[i]), ln_g[i, 1], ln_b[i, 1])
        x = x + jax.nn.sigmoid(x @ ple_w_gate[i]) * (p[i] @ ple_w_proj[i])
    return x
```

```python
import numpy as np
from contextlib import ExitStack
import concourse.bass as bass
import concourse.mybir as mybir
from concourse.bass_utils import run_bass_kernel_spmd

F32 = mybir.dt.float32
BF16 = mybir.dt.bfloat16
I32 = mybir.dt.int32
AF = mybir.ActivationFunctionType
ALU = mybir.AluOpType

ENGS = ("pe", "act", "dve", "pool", "sp")

S = 2048
D = 1024
NT = 16
KC = 8
DEPTH = 4
ALPHA = (2 * DEPTH) ** 0.25
LN_EPS = 1e-5
RMS_EPS = 1e-6
SAME_ENGINE_SYNC = False
RSTD_LATE = True
ST_DVE = 0
PREP_SPLIT = True
C_SKEW = True


class Tl:
    __slots__ = ("ap", "name", "lw", "rd", "small", "owner")

    def __init__(self, ap, name="", small=False):
        self.ap = ap
        self.name = name
        self.lw = None
        self.rd = []
        self.small = small
        self.owner = None

    def __getitem__(self, k):
        return self.ap[k]


class WPiece(list):
    oid = None


class Ins:
    __slots__ = ("eng", "fn", "deps", "pos", "sig", "sigval", "dma", "dsem", "dval")

    def __init__(self, eng, fn, dma=False):
        self.eng = eng
        self.fn = fn
        self.deps = []
        self.pos = -1
        self.sig = False
        self.sigval = 0
        self.dma = dma
        self.dsem = -1
        self.dval = 0


class Prog:
    def __init__(self, nc, n_hw_sems=6, n_sw_sems=8):
        self.nc = nc
        self.streams = {e: [] for e in ENGS}
        self.n_hw = n_hw_sems
        self.n_dma_sems = n_hw_sems + n_sw_sems
        self.rr_hw = 0
        self.rr_sw = 0
        self.dma_last = [None] * self.n_dma_sems
        self.dma_cnt = [0] * self.n_dma_sems

    @staticmethod
    def _flat(lst):
        out = []
        for t in lst:
            if isinstance(t, (list, tuple)):
                oid = getattr(t, "oid", None)
                for u in t:
                    if oid is not None and u.owner != oid:
                        raise RuntimeError("weight piece clobbered before use: %s" % u.name)
                    out.append(u)
            else:
                out.append(t)
        return out

    def op(self, eng, fn, reads=(), writes=(), dma=False):
        reads = self._flat(reads)
        writes = self._flat(writes)
        ins = Ins(eng, fn, dma)
        deps = []
        for t in reads:
            if t.lw is not None:
                deps.append((t.lw, t.small))
        for t in writes:
            if t.lw is not None:
                deps.append((t.lw, t.small))
            for r in t.rd:
                deps.append((r, t.small))
        if dma:
            if eng == "pool":
                s = self.n_hw + self.rr_sw
                self.rr_sw = (self.rr_sw + 1) % (self.n_dma_sems - self.n_hw)
            else:
                s = self.rr_hw
                self.rr_hw = (self.rr_hw + 1) % self.n_hw
            prev = self.dma_last[s]
            if prev is not None:
                deps.append((prev, True))
            self.dma_cnt[s] += 16
            ins.dsem = s
            ins.dval = self.dma_cnt[s]
            self.dma_last[s] = ins
        seen = set()
        best = {}
        for d, force in deps:
            if d is ins:
                continue
            if d.dma:
                if id(d) not in seen:
                    seen.add(id(d))
                    ins.deps.append(d)
                continue
            if d.eng == eng and not dma:
                if eng == "pe":
                    continue
                if not (force or SAME_ENGINE_SYNC):
                    continue
            if d.eng not in best or best[d.eng].pos < d.pos:
                best[d.eng] = d
        ins.deps.extend(best.values())
        ins.pos = len(self.streams[eng])
        self.streams[eng].append(ins)
        for t in reads:
            t.rd.append(ins)
        for t in writes:
            t.lw = ins
            t.rd = []
        return ins

    def pe(self, fn, reads=(), writes=()):
        return self.op("pe", fn, reads, writes)

    def act(self, fn, reads=(), writes=()):
        return self.op("act", fn, reads, writes)

    def dve(self, fn, reads=(), writes=()):
        return self.op("dve", fn, reads, writes)

    def dma(self, q, out_ap, in_ap, reads=(), writes=()):
        return self.op(q, lambda e: e.dma_start(out=out_ap, in_=in_ap), reads, writes, dma=True)

    def emit(self, stack):
        nc = self.nc
        esem = {e: stack.enter_context(nc.semaphore("s_" + e)) for e in ENGS}
        dsem = [stack.enter_context(nc.semaphore("d_%d" % i)) for i in range(self.n_dma_sems)]
        for e in ENGS:
            for ins in self.streams[e]:
                for d in ins.deps:
                    if not d.dma:
                        d.sig = True
        for e in ENGS:
            c = 0
            for ins in self.streams[e]:
                if ins.sig and not ins.dma:
                    c += 1
                    ins.sigval = c
        final_waits = [(i, self.dma_cnt[i]) for i in range(self.n_dma_sems) if self.dma_cnt[i] > 0]
        block = stack.enter_context(nc.Block())
        engobj = {"pe": "tensor", "act": "scalar", "dve": "vector", "pool": "gpsimd", "sp": "sync"}

        def make(e):
            def body(eng):
                seen_eng = {x: 0 for x in ENGS}
                seen_dma = [0] * self.n_dma_sems
                for ins in self.streams[e]:
                    for d in ins.deps:
                        if d.dma:
                            if seen_dma[d.dsem] < d.dval:
                                eng.wait_ge(dsem[d.dsem], d.dval)
                                seen_dma[d.dsem] = d.dval
                        else:
                            if seen_eng[d.eng] < d.sigval:
                                eng.wait_ge(esem[d.eng], d.sigval)
                                seen_eng[d.eng] = d.sigval
                    r = ins.fn(eng)
                    if ins.dma:
                        r.then_inc(dsem[ins.dsem], 16)
                    elif ins.sig:
                        r.then_inc(esem[e], 1)
                if e == "sp":
                    for i, v in final_waits:
                        if seen_dma[i] < v:
                            eng.wait_ge(dsem[i], v)
            return body

        for e in ENGS:
            getattr(block, engobj[e])(make(e))


WUNIT = 1024
NWUNIT = 16


class Ctx:
    pass


def build_nc(layers, dbg=None):
    nc = bass.Bass("TRN2", target_bir_lowering=False)
    dt = lambda name, shape, dtype=F32, kind="ExternalInput": nc.dram_tensor(name, shape, dtype, kind=kind).ap()
    d_x = dt("x", [S, D])
    d_p = dt("p", [DEPTH, S, 256])
    d_pos = dt("pos", [128, NT], I32)
    d_glaw = dt("gla_w_in", [2, D, 3088])
    d_glagu = dt("gla_w_gate_up", [2, 16, 512])
    d_glab = dt("gla_b_gate_t", [2, 128, 4])
    d_glang = dt("gla_norm_g_t", [2, 128, 2])
    d_glawo = dt("gla_w_out", [2, D, D])
    d_mlaw = dt("mla_w_in", [1, D, 576])
    d_mlan = dt("mla_norms", [1, 512])
    d_mlauq = dt("mla_w_uq", [1, 256, 1536])
    d_mlaukv = dt("mla_w_ukv", [1, 256, 2048])
    d_mlawo = dt("mla_w_out", [1, D, D])
    d_cwin = dt("conv_w_in", [1, D, 3072])
    d_cw = dt("conv_w_t", [1, 128, 8, 3])
    d_cwo = dt("conv_w_out", [1, D, D])
    d_lng = dt("ln_g", [DEPTH, 2, D])
    d_lnb = dt("ln_b", [DEPTH, 2, D])
    d_w1 = dt("mlp_w1", [DEPTH, D, 4 * D])
    d_w2 = dt("mlp_w2", [DEPTH, 4 * D, D])
    d_pwg = dt("ple_w_gate", [DEPTH, D, D])
    d_pwp = dt("ple_w_proj", [DEPTH, 256, D])
    d_ident = dt("c_ident", [128, 128])
    d_gmask = dt("c_gmask", [128, 256])
    d_rmask = dt("c_rmask", [128, 512])
    d_invf = dt("c_invf", [128, 32])
    d_out = dt("out", [S, D], F32, "ExternalOutput")
    d_dbg = dt("dbg", [S, D], F32, "ExternalOutput") if dbg else None

    with ExitStack() as st:
        P = Prog(nc)
        sb = lambda name, shape, dtype: st.enter_context(nc.sbuf_tensor(name, shape, dtype))

        Xb = sb("X", [128, NT, D], F32)
        X = [Tl(Xb[:, t, :], "X%d" % t) for t in range(NT)]
        XTb = sb("XT", [128, KC, S], BF16)
        XT = [Tl(XTb[:, :, t * 128:(t + 1) * 128], "XT%d" % t) for t in range(NT)]
        Hb = sb("H", [128, 16384], BF16)
        SCb = sb("SC", [128, 11264], BF16)
        Wb = sb("W", [128, NWUNIT * WUNIT], BF16)
        Wt = [Tl(Wb[:, i * WUNIT:(i + 1) * WUNIT], "W%d" % i) for i in range(NWUNIT)]
        C_w = {"rr": 0, "oid": 0}
        LNg = Tl(sb("LNg", [128, D], F32), "LNg")
        LNb = Tl(sb("LNb", [128, D], F32), "LNb")
        identf = Tl(sb("identf", [128, 128], F32))
        ident = Tl(sb("ident", [128, 128], BF16))
        xbf = [Tl(sb("xbf%d" % i, [128, D], BF16)) for i in range(2)]
        stg = [Tl(sb("stg%d" % i, [128, 512], F32)) for i in range(2)]
        st6s = [Tl(sb("st6_%d" % i, [128, 2, 6], F32), small=True) for i in range(4)]
        mvs = [Tl(sb("mv_%d" % i, [128, 8], F32), small=True) for i in range(4)]
        nhalf = Tl(sb("nhalf", [128, 8], F32), "nhalf")
        PSt = st.enter_context(nc.psum_tensor("ps", [128, 8, 512], F32))
        PS = [Tl(PSt[:, i, :], "PS%d" % i) for i in range(8)]
        C = Ctx()
        C.ps_rr = 0
        C.xb_rr = 0
        C.mv_rr = 0
        C.stg_rr = 0

        C.resv = set()

        def ps1():
            while C.ps_rr in C.resv:
                C.ps_rr = (C.ps_rr + 1) % 8
            i = C.ps_rr
            C.ps_rr = (C.ps_rr + 1) % 8
            return i

        def ps2():
            if C.ps_rr % 2:
                C.ps_rr = (C.ps_rr + 1) % 8
            while C.ps_rr in C.resv or (C.ps_rr + 1) in C.resv:
                C.ps_rr = (C.ps_rr + 2) % 8
            i = C.ps_rr
            C.ps_rr = (C.ps_rr + 2) % 8
            return i

        ARENA = {"H": [], "SC": []}

        def fence(arena, new_tiles):
            prior = []
            for t in ARENA[arena]:
                if t.lw is not None:
                    prior.append(t.lw)
                prior.extend(t.rd)
            for t in new_tiles:
                t.rd = list(prior)
            ARENA[arena] = list(new_tiles)

        def nstg():
            i = C.stg_rr
            C.stg_rr = (C.stg_rr + 1) % 2
            return stg[i]

        def walloc(n):
            nu = (n + WUNIT - 1) // WUNIT
            if C_w["rr"] + nu > NWUNIT:
                C_w["rr"] = 0
            u0 = C_w["rr"]
            C_w["rr"] = (u0 + nu) % NWUNIT
            C_w["oid"] += 1
            wp = WPiece(Wt[u0:u0 + nu])
            wp.oid = C_w["oid"]
            for u in wp:
                u.owner = wp.oid
            return wp, Wb[:, u0 * WUNIT:u0 * WUNIT + n]

        def wload(src_ap, shape):
            n = 1
            for s_ in shape[1:]:
                n *= s_
            wp, view = walloc(n)
            if len(shape) == 3:
                view = view.rearrange("p (a b) -> p a b", b=shape[2])
            elif len(shape) == 4:
                view = view.rearrange("p (a b c) -> p a b c", b=shape[2], c=shape[3])
            P.dma("pool", view, src_ap, writes=[wp])
            return wp, view

        def wview(dram2d, c0, ncols):
            return dram2d[:, c0:c0 + ncols].rearrange("(kc k) n -> k kc n", k=128)

        P.op("pool", lambda e: e.memset(nhalf[:], -0.5), writes=[nhalf])
        P.dma("sp", identf[:], d_ident[:, :], writes=[identf])
        P.dve(lambda e: e.tensor_copy(out=ident[:], in_=identf[:]), reads=[identf], writes=[ident])

        gsm_b = sb("gsm_b", [128, 1024], BF16)
        gsm_f = sb("gsm_f", [128, 64], F32)
        G_SM = (Tl(gsm_b[0:16, 0:512], "glrT"), Tl(gsm_b[0:16, 512:1024], "wgu"),
                Tl(gsm_f[:, 0:4], "bgt", True), Tl(gsm_f[:, 4:8], "nbg", True), Tl(gsm_f[:, 8:10], "ngt", True),
                Tl(gsm_f[:, 16:32], "eend", True), Tl(gsm_f[:, 32:40], "ssq", True), Tl(gsm_f[:, 40:48], "rstd", True),
                Tl(sb("k3T", [128, 512], BF16), "k3T"), Tl(sb("gmask", [128, 256], F32), "gmask"),
                Tl(sb("rmask", [128, 512], BF16), "rmask"), Tl(sb("wglr", [128, 8, 16], BF16), "wglr"))
        mla_i = sb("mla_i", [128, 16], I32)
        mla_f = sb("mla_f", [128, 64], F32)
        MLA_SM = (Tl(mla_i[:, :], "posi", True), Tl(mla_f[:, 0:16], "posf", True), Tl(mla_f[:, 16:48], "invf"),
                  [Tl(mla_f[:, 48 + i:49 + i], "rec%d" % i, True) for i in range(4)])
        P.dma("sp", G_SM[9][:], d_gmask[:, :], writes=[G_SM[9]])
        P.dma("pool", G_SM[10][:], d_rmask[:, :], writes=[G_SM[10]])

        def to_xt(tt, on_act=True, evac_act=False):
            xb = xbf[C.xb_rr]
            C.xb_rr ^= 1
            if on_act:
                P.act(lambda e: e.copy(out=xb[:], in_=X[tt][:]), reads=[X[tt]], writes=[xb])
            else:
                P.dve(lambda e: e.tensor_copy(out=xb[:], in_=X[tt][:]), reads=[X[tt]], writes=[xb])
            b = ps1()
            pv = PSt[:, b, :].bitcast(BF16)
            for kc in range(KC):
                P.pe(lambda e, kc=kc: e.transpose(out=pv[:, kc * 128:(kc + 1) * 128], in_=xb[:, kc * 128:(kc + 1) * 128],
                                                  identity=ident[:]), reads=[xb, ident], writes=[PS[b]])
            if evac_act:
                P.act(lambda e: e.copy(out=XT[tt][:], in_=pv.rearrange("p (a b) -> p a b", b=128)), reads=[PS[b]], writes=[XT[tt]])
            else:
                P.dve(lambda e: e.tensor_copy(out=XT[tt][:], in_=pv.rearrange("p (a b) -> p a b", b=128)),
                      reads=[PS[b]], writes=[XT[tt]])

        class XtQ:
            def __init__(self, delay=2):
                self.q = []
                self.delay = delay

            def push(self, tt):
                self.q.append(tt)
                while len(self.q) > self.delay:
                    to_xt(self.q.pop(0))

            def flush(self):
                while self.q:
                    to_xt(self.q.pop(0))

        def load_ln(layer, which):
            P.dma("sp", LNg[:], d_lng[layer, which, :].partition_broadcast(128), writes=[LNg])
            P.dma("sp", LNb[:], d_lnb[layer, which, :].partition_broadcast(128), writes=[LNb])

        def ln_inplace(tt):
            k = C.mv_rr
            C.mv_rr = (C.mv_rr + 1) % 4
            m, s6 = mvs[k], st6s[k]
            P.dve(lambda e: e.bn_stats(out=s6[:, 0, :], in_=X[tt][:, 0:512]), reads=[X[tt]], writes=[s6])
            P.dve(lambda e: e.bn_stats(out=s6[:, 1, :], in_=X[tt][:, 512:1024]), reads=[X[tt]], writes=[s6])
            P.dve(lambda e: e.bn_aggr(out=m[:, 0:2], in_=s6[:]), reads=[s6], writes=[m])
            P.op("pool", lambda e: e.tensor_scalar_add(out=m[:, 2:3], in0=m[:, 1:2], scalar1=LN_EPS), reads=[m], writes=[m])
            P.op("pool", lambda e: e.tensor_tensor(out=m[:, 3:4], in0=m[:, 2:3], in1=nhalf[:, 0:1], op=ALU.pow), reads=[m, nhalf], writes=[m])
            P.dve(lambda e: e.scalar_tensor_tensor(out=X[tt][:], in0=X[tt][:], scalar=m[:, 0:1], in1=LNg[:], op0=ALU.subtract, op1=ALU.mult),
                  reads=[X[tt], m, LNg], writes=[X[tt]])
            P.dve(lambda e: e.scalar_tensor_tensor(out=X[tt][:], in0=X[tt][:], scalar=m[:, 3:4], in1=LNb[:], op0=ALU.mult, op1=ALU.add),
                  reads=[X[tt], m, LNb], writes=[X[tt]])

        def resid_ln(tt, b2):
            P.dve(lambda e: e.scalar_tensor_tensor(out=X[tt][:], in0=X[tt][:], scalar=ALPHA, in1=PSt[:, b2:b2 + 2, :].rearrange("p a b -> p (a b)"),
                                                   op0=ALU.mult, op1=ALU.add), reads=[X[tt], PS[b2], PS[b2 + 1]], writes=[X[tt]])
            ln_inplace(tt)

        def dense_tok(tt, wv0, wv1, wt0, wt1, lhs_fn, nk, lhs_tiles):
            b2 = ps2()
            for half, (wv, wt) in enumerate(((wv0, wt0), (wv1, wt1))):
                for k in range(nk):
                    P.pe(lambda e, k=k, wv=wv, half=half: e.matmul(PSt[:, b2 + half, :], lhsT=lhs_fn(k), rhs=wv[:, k, :],
                                                                   start=(k == 0), stop=(k == nk - 1)),
                         reads=list(lhs_tiles) + [wt], writes=[PS[b2 + half]])
            return b2

        def mlp(layer):
            hT = Hb[:, :].rearrange("p (f t) -> p f t", t=S)
            hTl = [Tl(hT[:, f, :], "hT%d" % f) for f in range(8)]
            fence("H", hTl)
            w1 = d_w1[layer]
            w2 = d_w2[layer]
            load_ln(layer, 1)
            for fb in range(4):
                for half in range(2):
                    wt, wv = wload(wview(w1, fb * 1024 + half * 512, 512), [128, 8, 512])
                    for f4 in range(4):
                        f = half * 4 + f4
                        for g in range(4):
                            b = ps1()
                            for kc in range(KC):
                                P.pe(lambda e, kc=kc, f4=f4, g=g, wv=wv, b=b: e.matmul(
                                    PSt[:, b, :], lhsT=wv[:, kc, f4 * 128:(f4 + 1) * 128], rhs=XTb[:, kc, g * 512:(g + 1) * 512],
                                    start=(kc == 0), stop=(kc == KC - 1)),
                                    reads=[wt] + XT[4 * g:4 * g + 4], writes=[PS[b]])
                            r = nstg()
                            P.act(lambda e, r=r, b=b: e.activation(out=r[:], in_=PSt[:, b, :], func=AF.Relu), reads=[PS[b]], writes=[r])
                            P.dve(lambda e, r=r, f=f, g=g: e.tensor_tensor(out=hT[:, f, g * 512:(g + 1) * 512], in0=r[:], in1=r[:], op=ALU.mult),
                                  reads=[r], writes=[hTl[f]])
                wts = []
                for half in range(2):
                    wts.append(wload(w2[fb * 1024 + half * 512: fb * 1024 + (half + 1) * 512, :].rearrange("(fc f) n -> f fc n", f=128),
                                     [128, 4, 1024]))

                def down(tt, fb=fb, wts=wts):
                    b2 = ps2()
                    for nh in range(2):
                        for f in range(8):
                            wt, wv = wts[f // 4]
                            P.pe(lambda e, f=f, nh=nh, wv=wv, tt=tt, b2=b2: e.matmul(
                                PSt[:, b2 + nh, :], lhsT=hT[:, f, tt * 128:(tt + 1) * 128], rhs=wv[:, f % 4, nh * 512:(nh + 1) * 512],
                                start=(f == 0), stop=(f == 7)), reads=[hTl[f], wt], writes=[PS[b2 + nh]])
                    pin = PSt[:, b2:b2 + 2, :].rearrange("p a b -> p (a b)")
                    if fb == 0:
                        P.dve(lambda e, tt=tt, pin=pin: e.scalar_tensor_tensor(out=X[tt][:], in0=X[tt][:], scalar=ALPHA, in1=pin,
                                                                             op0=ALU.mult, op1=ALU.add),
                              reads=[X[tt], PS[b2], PS[b2 + 1]], writes=[X[tt]])
                    else:
                        P.dve(lambda e, tt=tt, pin=pin: e.tensor_tensor(out=X[tt][:], in0=X[tt][:], in1=pin, op=ALU.add),
                              reads=[X[tt], PS[b2], PS[b2 + 1]], writes=[X[tt]])
                    if fb == 3:
                        ln_inplace(tt)

                if fb < 3:
                    for tt in range(NT):
                        down(tt)
            return down

        def ple(layer, last, down):
            pbf = SCb[:, 0:4096].rearrange("p (t k) -> p t k", k=256)
            pbt = Tl(pbf, "pbf")
            pT = SCb[:, 4096:8192].rearrange("p (c t) -> p c t", t=S)
            pTt = Tl(pT, "pT")
            wpv = SCb[:, 8192:10240].rearrange("p (c n) -> p c n", n=1024)
            wpt = Tl(wpv, "wp")
            fence("SC", [pbt, pTt, wpt])
            P.dma("pool", pbf, d_p[layer].rearrange("(t q) k -> q t k", q=128), writes=[pbt])
            P.dma("pool", wpv, wview(d_pwp[layer], 0, 1024), writes=[wpt])
            for tt in range(NT):
                b = ps1()
                pv = PSt[:, b, :].bitcast(BF16)
                for c in range(2):
                    P.pe(lambda e, c=c, tt=tt, pv=pv: e.transpose(out=pv[:, c * 128:(c + 1) * 128], in_=pbf[:, tt, c * 128:(c + 1) * 128],
                                                             identity=ident[:]), reads=[pbt, ident], writes=[PS[b]])
                P.act(lambda e, tt=tt, pv=pv: e.copy(out=pT[:, :, tt * 128:(tt + 1) * 128], in_=pv[:, 0:256].rearrange("p (a b) -> p a b", b=128)),
                      reads=[PS[b]], writes=[pTt])
            wg = [wload(wview(d_pwg[layer], h * 512, 512), [128, 8, 512]) for h in range(2)]

            def ple_tile(tt):
                for half in range(2):
                    wt, wv = wg[half]
                    ba = ps1()
                    for kc in range(KC):
                        P.pe(lambda e, kc=kc, wv=wv, ba=ba, tt=tt: e.matmul(PSt[:, ba, :], lhsT=XTb[:, kc, tt * 128:(tt + 1) * 128], rhs=wv[:, kc, :],
                                                                        start=(kc == 0), stop=(kc == KC - 1)),
                             reads=[XT[tt], wt], writes=[PS[ba]])
                    bb = ps1()
                    for c in range(2):
                        P.pe(lambda e, c=c, bb=bb, tt=tt, half=half: e.matmul(PSt[:, bb, :], lhsT=pT[:, c, tt * 128:(tt + 1) * 128],
                                                                          rhs=wpv[:, c, half * 512:(half + 1) * 512], start=(c == 0), stop=(c == 1)),
                             reads=[pTt, wpt], writes=[PS[bb]])
                    sg = nstg()
                    P.act(lambda e, sg=sg, ba=ba: e.activation(out=sg[:], in_=PSt[:, ba, :], func=AF.Sigmoid), reads=[PS[ba]], writes=[sg])
                    P.dve(lambda e, sg=sg, bb=bb: e.tensor_tensor(out=sg[:], in0=sg[:], in1=PSt[:, bb, :], op=ALU.mult), reads=[sg, PS[bb]], writes=[sg])
                    P.dve(lambda e, sg=sg, tt=tt, half=half: e.tensor_tensor(out=X[tt][:, half * 512:(half + 1) * 512],
                                                                             in0=X[tt][:, half * 512:(half + 1) * 512], in1=sg[:], op=ALU.add),
                          reads=[sg, X[tt]], writes=[X[tt]])

            D1, D2, D3 = 3, 4, 7
            for i in range(NT + D3):
                if i < NT:
                    down(i)
                if 0 <= i - D1 < NT:
                    to_xt(i - D1, evac_act=True)
                if 0 <= i - D2 < NT:
                    ple_tile(i - D2)
                if not last and 0 <= i - D3 < NT:
                    to_xt(i - D3, evac_act=True)

        def conv_mixer(j, layer):
            YT = Hb[:, :].rearrange("p (c t) -> p c t", t=S)
            YTl = [Tl(YT[:, c, :], "YT%d" % c) for c in range(8)]
            CUf = SCb[:, 0:4352].bitcast(F32)
            CU = Tl(CUf, "CU")
            Zf = SCb[:, 4352:8448].bitcast(F32)
            Z = Tl(Zf, "Z")
            Bs = Tl(SCb[:, 8448:10496], "Bs")
            cw = Tl(SCb[:, 10496:10544].bitcast(F32).rearrange("p (c j) -> p c j", j=3), "cw", small=True)
            fence("H", YTl)
            fence("SC", [CU, Z, Bs, cw])
            P.dma("sp", cw[:], d_cw[j], writes=[cw])
            P.dve(lambda e: e.memset(CUf[:, 0:2], 0.0), writes=[CU])
            load_ln(layer, 0)
            win = d_cwin[j]
            for cg in range(4):
                wp = [wload(wview(win, sec * 1024 + cg * 256, 256), [128, 8, 256]) for sec in range(3)]
                for c4 in range(2):
                    cc = cg * 2 + c4
                    for g in range(4):
                        bs = []
                        for sec in range(3):
                            wt, wv = wp[sec]
                            b = ps1()
                            bs.append(b)
                            for kc in range(KC):
                                P.pe(lambda e, kc=kc, wv=wv, b=b, c4=c4, g=g: e.matmul(
                                    PSt[:, b, :], lhsT=wv[:, kc, c4 * 128:(c4 + 1) * 128], rhs=XTb[:, kc, g * 512:(g + 1) * 512],
                                    start=(kc == 0), stop=(kc == KC - 1)), reads=[wt] + XT[4 * g:4 * g + 4], writes=[PS[b]])
                        P.act(lambda e, b=bs[0], g=g: e.copy(out=Bs[:, g * 512:(g + 1) * 512], in_=PSt[:, b, :]), reads=[PS[bs[0]]], writes=[Bs])
                        cs = nstg()
                        P.act(lambda e, b=bs[1], cs=cs: e.copy(out=cs[:], in_=PSt[:, b, :]), reads=[PS[bs[1]]], writes=[cs])
                        P.dve(lambda e, b=bs[2], cs=cs, g=g: e.tensor_tensor(out=CUf[:, 2 + g * 512: 2 + (g + 1) * 512], in0=PSt[:, b, :], in1=cs[:], op=ALU.mult),
                              reads=[PS[bs[2]], cs], writes=[CU])
                    P.dve(lambda e, cc=cc: e.tensor_scalar(out=Zf[:, :], in0=CUf[:, 2:2050], scalar1=cw[:, cc, 2:3], scalar2=None, op0=ALU.mult),
                          reads=[CU, cw], writes=[Z])
                    P.dve(lambda e, cc=cc: e.scalar_tensor_tensor(out=Zf[:, :], in0=CUf[:, 1:2049], scalar=cw[:, cc, 1:2], in1=Zf[:, :], op0=ALU.mult, op1=ALU.add),
                          reads=[CU, cw, Z], writes=[Z])
                    P.dve(lambda e, cc=cc: e.scalar_tensor_tensor(out=Zf[:, :], in0=CUf[:, 0:2048], scalar=cw[:, cc, 0:1], in1=Zf[:, :], op0=ALU.mult, op1=ALU.add),
                          reads=[CU, cw, Z], writes=[Z])
                    P.dve(lambda e, cc=cc: e.tensor_tensor(out=YT[:, cc, :], in0=Zf[:, :], in1=Bs[:], op=ALU.mult), reads=[Z, Bs], writes=[YTl[cc]])
            wo = [wload(wview(d_cwo[j], h * 512, 512), [128, 8, 512]) for h in range(2)]
            xq = XtQ(2)
            for tt in range(NT):
                b2 = dense_tok(tt, wo[0][1], wo[1][1], wo[0][0], wo[1][0], lambda k, tt=tt: YT[:, k, tt * 128:(tt + 1) * 128], 8, YTl)
                resid_ln(tt, b2)
                xq.push(tt)
            xq.flush()

        def gla_mixer(j, layer):
            win = d_glaw[j]
            o = [0]

            def take(n):
                r = o[0]
                o[0] += n
                return r
            a0 = take(2048); Vh = [Hb[:, a0 + i * 1024: a0 + (i + 1) * 1024].rearrange("p (t n) -> p t n", n=256) for i in range(2)]
            Vht = [Tl(Vh[i], "Vh%d" % i) for i in range(2)]
            a0 = take(2048); Gh = [Hb[:, a0 + i * 1024: a0 + (i + 1) * 1024].rearrange("p (t n) -> p t n", n=256) for i in range(2)]
            Ght = [Tl(Gh[i], "Gh%d" % i) for i in range(2)]
            a0 = take(1024); K3 = [Hb[:, a0 + i * 512: a0 + (i + 1) * 512].rearrange("p (q d) -> p q d", d=128) for i in range(2)]
            K3t = [Tl(K3[i], "K3_%d" % i) for i in range(2)]
            a0 = take(512); STf = Tl(Hb[:, a0:a0 + 512].bitcast(F32), "STf")
            a0 = take(256); STb = Tl(Hb[:, a0:a0 + 256], "STb")
            a0 = take(768); OGh = [Tl(Hb[:, a0 + i * 256: a0 + (i + 1) * 256], "OGh%d" % i) for i in range(3)]
            a0 = take(512); OGT = [Tl(Hb[:, a0 + i * 256: a0 + (i + 1) * 256].rearrange("p (c t) -> p c t", t=128), "OGT%d" % i) for i in range(2)]
            a0 = take(768); S12 = [Tl(Hb[:, a0 + i * 256: a0 + (i + 1) * 256], "S12_%d" % i) for i in range(3)]
            a0 = take(2048); glrA = Tl(Hb[0:16, a0:a0 + 2048], "glrA")
            a0 = take(512); k3T = Tl(Hb[:, a0:a0 + 512], "k3T")
            assert o[0] <= 16384
            PR = [SCb[:, i * 2048:(i + 1) * 2048].rearrange("p (k t) -> p k t", t=512) for i in range(2)]
            PRt = [Tl(PR[i], "PR%d" % i) for i in range(2)]
            LT = Tl(SCb[:, 4096:5120].bitcast(F32), "LT")
            EP = Tl(SCb[:, 5120:6144].bitcast(F32), "EP")
            EM = Tl(SCb[:, 6144:7168].bitcast(F32), "EM")
            _glrT, wgu, bgt, nbg, ngt, eend, ssq, rstd, _k3T, gmask, rmask, wglr = G_SM
            ssq_l = [Tl(gsm_f[:, 32 + i:33 + i], "ssq%d" % i, True) for i in range(4)]
            sq_l = [Tl(gsm_f[:, 36 + i:37 + i], "sq%d" % i, True) for i in range(4)]
            rs_l = [Tl(gsm_f[:, 40 + i:41 + i], "rs%d" % i, True) for i in range(4)]
            if RSTD_LATE:
                for tt in range(NT):
                    P.act(lambda e, tt=tt: e.activation(out=X[tt][:], in_=X[tt][:], func=AF.Copy, scale=ALPHA), reads=[X[tt]], writes=[X[tt]])
            fence("H", Vht + Ght + K3t + [STf, STb] + OGh + OGT + S12 + [glrA, k3T])
            fence("SC", PRt + [LT, EP, EM])
            P.dma("pool", wgu[:], d_glagu[j], writes=[wgu])
            P.dma("sp", bgt[:], d_glab[j], writes=[bgt])
            P.dma("sp", ngt[:], d_glang[j], writes=[ngt])
            P.dma("pool", wglr[:], win[:, 3072:3088].rearrange("(kc k) n -> k kc n", k=128), writes=[wglr])
            P.dve(lambda e: e.tensor_scalar(out=nbg[:], in0=bgt[:], scalar1=-1.0, scalar2=None, op0=ALU.mult), reads=[bgt], writes=[nbg])
            load_ln(layer, 0)
            qs = 128.0 ** -0.5
            for g in range(4):
                b = ps1()
                for kc in range(KC):
                    P.pe(lambda e, kc=kc, b=b, g=g: e.matmul(PSt[0:16, b, :], lhsT=wglr[:, kc, :], rhs=XTb[:, kc, g * 512:(g + 1) * 512],
                                                         start=(kc == 0), stop=(kc == KC - 1)), reads=[wglr] + XT[4 * g:4 * g + 4], writes=[PS[b]])
                P.act(lambda e, b=b, g=g: e.copy(out=glrA[:, g * 512:(g + 1) * 512], in_=PSt[0:16, b, :]), reads=[PS[b]], writes=[glrA])
            gxq = XtQ(2)

            for h in range(4):
                wq = wload(wview(win, h * 128, 128), [128, 8, 128])
                wk = wload(wview(win, 512 + h * 128, 128), [128, 8, 128])
                wv = wload(wview(win, 1024 + h * 256, 256), [128, 8, 256])
                wr = wload(wview(win, 2048 + h * 256, 256), [128, 8, 256])
                wo = wload(d_glawo[j][h * 256:(h + 1) * 256, :].rearrange("(c k) n -> k c n", k=128), [128, 2, 1024])
                for c in range(2):
                    P.dve(lambda e, c=c, wo=wo: e.tensor_scalar(out=wo[1][:, c, :], in0=wo[1][:, c, :], scalar1=ngt[:, c:c + 1], scalar2=None, op0=ALU.mult),
                          reads=[wo[0], ngt], writes=[wo[0]])
                P.dve(lambda e: e.memset(STf[:], 0.0), writes=[STf])
                P.dve(lambda e: e.memset(STb[:], 0.0), writes=[STb])

                def prepA(g, h=h):
                    gb = g % 2
                    b = ps1()
                    P.pe(lambda e, b=b, h=h, g=g: e.matmul(PSt[:, b, :], lhsT=wgu[:, h * 128:(h + 1) * 128], rhs=glrA[:, g * 512:(g + 1) * 512], start=True, stop=True),
                         reads=[wgu, glrA], writes=[PS[b]])
                    P.act(lambda e, b=b, h=h: e.activation(out=LT[:], in_=PSt[:, b, :], func=AF.Exp, scale=-1.0, bias=nbg[:, h:h + 1]),
                          reads=[PS[b], nbg], writes=[LT])
                    P.act(lambda e: e.activation(out=LT[:], in_=LT[:], func=AF.Ln, bias=1.0), reads=[LT], writes=[LT])
                    P.dve(lambda e: e.tensor_scalar(out=LT[:], in0=LT[:], scalar1=-1.0 / 16.0, scalar2=None, op0=ALU.mult), reads=[LT], writes=[LT])
                    P.dve(lambda e: e.tensor_tensor_scan(out=LT[:], data0=rmask[:], data1=LT[:], initial=0.0, op0=ALU.mult, op1=ALU.add),
                          reads=[LT, rmask], writes=[LT])
                    P.act(lambda e: e.activation(out=EP[:], in_=LT[:], func=AF.Exp), reads=[LT], writes=[EP])
                    P.act(lambda e: e.activation(out=EM[:], in_=LT[:], func=AF.Exp, scale=-1.0), reads=[LT], writes=[EM])
                    P.dve(lambda e, gb=gb: e.tensor_copy(out=eend[:, gb * 4:(gb + 1) * 4], in_=EP[:].rearrange("p (q t) -> p q t", t=128)[:, :, 127]),
                          reads=[EP], writes=[eend])

                def prepB(g, h=h, wq=wq, wk=wk):
                    gb = g % 2
                    xts = XT[4 * g:4 * g + 4]
                    pr, prt = PR[gb], PRt[gb]
                    bq = ps1()
                    for kc in range(KC):
                        P.pe(lambda e, kc=kc, bq=bq, g=g, wq=wq: e.matmul(PSt[:, bq, :], lhsT=wq[1][:, kc, :], rhs=XTb[:, kc, g * 512:(g + 1) * 512],
                                                                      start=(kc == 0), stop=(kc == KC - 1)), reads=[wq[0]] + xts, writes=[PS[bq]])
                    bk = ps1()
                    for kc in range(KC):
                        P.pe(lambda e, kc=kc, bk=bk, g=g, wk=wk: e.matmul(PSt[:, bk, :], lhsT=wk[1][:, kc, :], rhs=XTb[:, kc, g * 512:(g + 1) * 512],
                                                                      start=(kc == 0), stop=(kc == KC - 1)), reads=[wk[0]] + xts, writes=[PS[bk]])
                    P.dve(lambda e, bq=bq, pr=pr: e.scalar_tensor_tensor(out=pr[:, 0, :], in0=PSt[:, bq, :], scalar=qs, in1=EP[:], op0=ALU.mult, op1=ALU.mult),
                          reads=[PS[bq], EP], writes=[prt])
                    P.dve(lambda e, bq=bq, pr=pr: e.scalar_tensor_tensor(out=pr[:, 1, :], in0=PSt[:, bq, :], scalar=qs, in1=EM[:], op0=ALU.mult, op1=ALU.mult),
                          reads=[PS[bq], EM], writes=[prt])
                    P.dve(lambda e, bk=bk, pr=pr: e.tensor_tensor(out=pr[:, 2, :], in0=PSt[:, bk, :], in1=EM[:], op=ALU.mult), reads=[PS[bk], EM], writes=[prt])
                    P.dve(lambda e, bk=bk, pr=pr: e.tensor_tensor(out=pr[:, 3, :], in0=PSt[:, bk, :], in1=EP[:], op=ALU.mult), reads=[PS[bk], EP], writes=[prt])
                    for q in range(4):
                        P.dve(lambda e, q=q, pr=pr, gb=gb: e.tensor_scalar(out=k3T[:, q * 128:(q + 1) * 128], in0=pr[:, 2, q * 128:(q + 1) * 128],
                                                                      scalar1=eend[:, gb * 4 + q:gb * 4 + q + 1], scalar2=None, op0=ALU.mult),
                              reads=[prt, eend], writes=[k3T])

                def prep_vr(g, t4, h=h, wv=wv, wr=wr):
                    gb = g % 2
                    vh, vht, gh, ght = Vh[gb], Vht[gb], Gh[gb], Ght[gb]
                    if True:
                        tt = 4 * g + t4
                        b = ps1()
                        for sec, w_ in ((0, wv), (1, wr)):
                            for kc in range(KC):
                                P.pe(lambda e, kc=kc, b=b, tt=tt, sec=sec, w_=w_: e.matmul(
                                    PSt[:, b, sec * 256:(sec + 1) * 256], lhsT=XTb[:, kc, tt * 128:(tt + 1) * 128], rhs=w_[1][:, kc, :],
                                    start=(kc == 0), stop=(kc == KC - 1)), reads=[XT[tt], w_[0]], writes=[PS[b]])
                        P.act(lambda e, b=b, t4=t4, vh=vh: e.copy(out=vh[:, t4, :], in_=PSt[:, b, 0:256]), reads=[PS[b]], writes=[vht])
                        P.act(lambda e, b=b, t4=t4, gh=gh: e.activation(out=gh[:, t4, :], in_=PSt[:, b, 256:512], func=AF.Silu), reads=[PS[b]], writes=[ght])

                def prep2(g):
                    k3, k3t = K3[g % 2], K3t[g % 2]
                    bt = ps1()
                    pv = PSt[:, bt, :].bitcast(BF16)
                    for q in range(4):
                        P.pe(lambda e, q=q, pv=pv: e.transpose(out=pv[:, q * 128:(q + 1) * 128], in_=k3T[:, q * 128:(q + 1) * 128], identity=ident[:]),
                             reads=[k3T, ident], writes=[PS[bt]])
                    P.act(lambda e, pv=pv, k3=k3: e.copy(out=k3[:, :, :], in_=pv[:, 0:512].rearrange("p (q d) -> p q d", d=128)), reads=[PS[bt]], writes=[k3t])

                def stageA(p):
                    g, q = p // 4, p % 4
                    pr, prt = PR[g % 2], PRt[g % 2]
                    s12 = S12[p % 3]
                    bab = ps1()
                    P.pe(lambda e, q=q, bab=bab, pr=pr: e.matmul(PSt[:, bab, 0:128], lhsT=pr[:, 2, q * 128:(q + 1) * 128], rhs=pr[:, 0, q * 128:(q + 1) * 128],
                                                             start=True, stop=True), reads=[prt], writes=[PS[bab]])
                    P.pe(lambda e, q=q, bab=bab, pr=pr: e.matmul(PSt[:, bab, 128:256], lhsT=pr[:, 3, q * 128:(q + 1) * 128], rhs=pr[:, 1, q * 128:(q + 1) * 128],
                                                             start=True, stop=True), reads=[prt], writes=[PS[bab]])
                    P.dve(lambda e, bab=bab, s12=s12: e.tensor_tensor(out=s12[:], in0=PSt[:, bab, 0:256], in1=gmask[:], op=ALU.mult),
                          reads=[PS[bab], gmask], writes=[s12])

                def stageB(p, h=h):
                    g, q = p // 4, p % 4
                    gb = g % 2
                    pr, prt, vh, vht, gh, ght, k3, k3t = PR[gb], PRt[gb], Vh[gb], Vht[gb], Gh[gb], Ght[gb], K3[gb], K3t[gb]
                    s12 = S12[p % 3]
                    og = OGh[p % 3]
                    bo = ps1()
                    oreg = PSt[:, bo, 0:256]
                    vq = vh[:, q, :]
                    P.pe(lambda e, s12=s12, oreg=oreg, vq=vq: e.matmul(oreg, lhsT=s12[:, 0:128], rhs=vq, start=True, stop=False), reads=[s12, vht], writes=[PS[bo]])
                    P.pe(lambda e, s12=s12, oreg=oreg, vq=vq: e.matmul(oreg, lhsT=s12[:, 128:256], rhs=vq, start=False, stop=False), reads=[s12, vht], writes=[PS[bo]])
                    P.pe(lambda e, q=q, oreg=oreg, pr=pr: e.matmul(oreg, lhsT=pr[:, 0, q * 128:(q + 1) * 128], rhs=STb[:], start=False, stop=True),
                         reads=[prt, STb], writes=[PS[bo]])
                    sreg = PSt[:, bo, 256:512]
                    P.pe(lambda e, q=q, sreg=sreg, vq=vq, k3=k3: e.matmul(sreg, lhsT=k3[:, q, :], rhs=vq, start=True, stop=True), reads=[k3t, vht], writes=[PS[bo]])
                    if ST_DVE == 2:
                        P.dve(lambda e, q=q, gb=gb, sreg=sreg: e.scalar_tensor_tensor(out=STb[:], in0=STf[:], scalar=eend[:, gb * 4 + q:gb * 4 + q + 1], in1=sreg,
                                                                                 op0=ALU.mult, op1=ALU.add), reads=[STf, eend, PS[bo]], writes=[STb])
                    P.dve(lambda e, q=q, gb=gb, sreg=sreg: e.scalar_tensor_tensor(out=STf[:], in0=STf[:], scalar=eend[:, gb * 4 + q:gb * 4 + q + 1], in1=sreg,
                                                                             op0=ALU.mult, op1=ALU.add), reads=[STf, eend, PS[bo]], writes=[STf])
                    if ST_DVE == 1:
                        P.dve(lambda e: e.tensor_copy(out=STb[:], in_=STf[:]), reads=[STf], writes=[STb])
                    elif ST_DVE == 0:
                        P.act(lambda e: e.copy(out=STb[:], in_=STf[:]), reads=[STf], writes=[STb])
                    if RSTD_LATE:
                        P.dve(lambda e, oreg=oreg, og=og, gh=gh, q=q: e.tensor_tensor(out=og[:], in0=oreg, in1=gh[:, q, :], op=ALU.mult),
                              reads=[PS[bo], ght], writes=[og])
                    jk = nstg()
                    sq_, sr_, rs_ = ssq_l[p % 4], sq_l[p % 4], rs_l[p % 4]
                    P.act(lambda e, oreg=oreg, jk=jk, sq_=sq_: e.activation(out=jk[:, 0:256], in_=oreg, func=AF.Square, accum_out=sq_[:, 0:1]),
                          reads=[PS[bo]], writes=[jk, sq_])
                    if not RSTD_LATE:
                        P.act(lambda e, sq_=sq_, sr_=sr_: e.activation(out=sr_[:, 0:1], in_=sq_[:, 0:1], func=AF.Sqrt, scale=1.0 / 256.0, bias=RMS_EPS),
                              reads=[sq_], writes=[sr_])
                    if not RSTD_LATE:
                        P.dve(lambda e, sr_=sr_, rs_=rs_: e.reciprocal(out=rs_[:, 0:1], in_=sr_[:, 0:1]), reads=[sr_], writes=[rs_])
                        P.dve(lambda e, oreg=oreg, og=og, gh=gh, q=q, rs_=rs_: e.scalar_tensor_tensor(out=og[:], in0=oreg, scalar=rs_[:, 0:1], in1=gh[:, q, :],
                                                                                               op0=ALU.mult, op1=ALU.mult),
                              reads=[PS[bo], rs_, ght], writes=[og])

                def stageC1(p):
                    og = OGh[p % 3]
                    ogt = OGT[p % 2]
                    if RSTD_LATE:
                        sq_, sr_, rs_ = ssq_l[p % 4], sq_l[p % 4], rs_l[p % 4]
                        P.dve(lambda e, sq_=sq_, sr_=sr_: e.tensor_scalar(out=sr_[:, 0:1], in0=sq_[:, 0:1], scalar1=1.0 / 256.0, scalar2=RMS_EPS,
                                                                       op0=ALU.mult, op1=ALU.add), reads=[sq_], writes=[sr_])
                        P.op("pool", lambda e, sr_=sr_, rs_=rs_: e.tensor_tensor(out=rs_[:, 0:1], in0=sr_[:, 0:1], in1=nhalf[:, 0:1], op=ALU.pow),
                             reads=[sr_, nhalf], writes=[rs_])
                    bt = ps1()
                    pv = PSt[:, bt, :].bitcast(BF16)
                    for c in range(2):
                        P.pe(lambda e, c=c, pv=pv, og=og: e.transpose(out=pv[:, c * 128:(c + 1) * 128], in_=og[:, c * 128:(c + 1) * 128], identity=ident[:]),
                             reads=[og, ident], writes=[PS[bt]])
                    P.act(lambda e, pv=pv, ogt=ogt: e.copy(out=ogt[:, :, :], in_=pv[:, 0:256].rearrange("p (a b) -> p a b", b=128)), reads=[PS[bt]], writes=[ogt])

                def stageC2(p, h=h, wo=wo):
                    tt = p
                    ogt = OGT[p % 2]
                    rs_ = rs_l[p % 4]
                    b2 = ps2()
                    for nh in range(2):
                        for c in range(2):
                            P.pe(lambda e, c=c, nh=nh, b2=b2, ogt=ogt, wo=wo: e.matmul(PSt[:, b2 + nh, :], lhsT=ogt[:, c, :], rhs=wo[1][:, c, nh * 512:(nh + 1) * 512],
                                                                                start=(c == 0), stop=(c == 1)), reads=[ogt, wo[0]], writes=[PS[b2 + nh]])
                    pin = PSt[:, b2:b2 + 2, :].rearrange("p a b -> p (a b)")
                    sr_ = sq_l[p % 4]
                    if RSTD_LATE:
                        P.dve(lambda e, tt=tt, pin=pin, rs_=rs_: e.scalar_tensor_tensor(out=X[tt][:], in0=pin, scalar=rs_[:, 0:1], in1=X[tt][:], op0=ALU.mult, op1=ALU.add),
                              reads=[X[tt], PS[b2], PS[b2 + 1], rs_], writes=[X[tt]])
                    elif h == 0:
                        P.dve(lambda e, tt=tt, pin=pin: e.scalar_tensor_tensor(out=X[tt][:], in0=X[tt][:], scalar=ALPHA, in1=pin, op0=ALU.mult, op1=ALU.add),
                              reads=[X[tt], PS[b2], PS[b2 + 1]], writes=[X[tt]])
                    else:
                        P.dve(lambda e, tt=tt, pin=pin: e.tensor_tensor(out=X[tt][:], in0=X[tt][:], in1=pin, op=ALU.add),
                              reads=[X[tt], PS[b2], PS[b2 + 1]], writes=[X[tt]])
                    if h == 3:
                        ln_inplace(tt)
                        gxq.push(tt)

                prepA(0)
                for t4 in range(4):
                    prep_vr(0, t4)
                prepB(0)
                prep2(0)
                stageA(0)
                stageA(1)
                for p in range(16):
                    gn = p // 4 + 1
                    if p + 4 < 16:
                        prep_vr(gn, p % 4)
                    if p + 2 < 16:
                        stageA(p + 2)
                    stageB(p)
                    if p >= 2:
                        stageC1(p - 2)
                    if p >= 3:
                        stageC2(p - 3)
                    if gn < 4:
                        if p % 4 == 0:
                            prepA(gn)
                        elif p % 4 == 1:
                            prepB(gn)
                        elif p % 4 == 2:
                            prep2(gn)
                stageC1(14)
                stageC2(13)
                stageC1(15)
                stageC2(14)
                stageC2(15)
            gxq.flush()

        def mla_mixer(j, layer):
            XTf = XTb[:, :, :].rearrange("p a b -> p (a b)")
            QR = XTf[:, 0:8192].rearrange("p (a t) -> p a t", t=S)
            QRt = Tl(QR, "QR")
            OA = XTf[:, 8192:16384].rearrange("p (t n) -> p t n", n=512)
            OAt = Tl(OA, "OA")
            cqT = Hb[:, 0:4096].rearrange("p (c t) -> p c t", t=S)
            cqTt = Tl(cqT, "cqT")
            ckT = Hb[:, 4096:8192].rearrange("p (c t) -> p c t", t=S)
            ckTt = Tl(ckT, "ckT")
            krT = Hb[:, 8192:10240]
            krTt = Tl(krT, "krT")
            VH = [Hb[:, 10240 + i * 2080: 10240 + (i + 1) * 2080].rearrange("p (t n) -> p t n", n=130) for i in range(2)]
            VHt = [Tl(VH[i], "VH%d" % i) for i in range(2)]
            CN = Tl(Hb[:, 14400:14912], "CN")
            KR = Tl(Hb[:, 14912:15040], "KR")
            QRS = Tl(Hb[:, 15040:15552], "QRS")
            OAT = Hb[:, 15552:16064].rearrange("p (h t) -> p h t", t=128)
            OATt = Tl(OAT, "OAT")
            knT = Tl(SCb[:, 0:2048], "knT")
            qnT = Tl(SCb[:, 2048:4096], "qnT")
            PB = [Tl(SCb[:, 4096 + i * 512: 4096 + (i + 1) * 512], "PB%d" % i) for i in range(4)]
            COS = Tl(SCb[:, 6144:7168].bitcast(F32), "COS")
            SIN = Tl(SCb[:, 7168:8192].bitcast(F32), "SIN")
            gbc = Tl(SCb[:, 8192:9216].bitcast(F32), "gbc")
            ANG = Tl(SCb[:, 9216:10240].bitcast(F32), "ANG")
            NF = Tl(SCb[:, 10240:11264].bitcast(F32), "NF")
            NI = SCb[:, 10240:11264].bitcast(I32)
            T1 = ANG
            T2 = NF
            fence("H", [cqTt, ckTt, krTt] + VHt + [CN, KR, QRS, OATt])
            fence("SC", [knT, qnT] + PB + [COS, SIN, gbc, ANG, NF])
            msm = G_SM[6]
            mrs = G_SM[7]
            posi, posf, invf, rec = MLA_SM
            allXT = list(XT)
            scale = 192.0 ** -0.5
            PI = 3.1415925
            TWO_PI = 6.283185307179586
            C1 = 6.28125
            C2 = TWO_PI - C1
            cos3 = COS[:].rearrange("p (t i) -> p t i", i=32)
            sin3 = SIN[:].rearrange("p (t i) -> p t i", i=32)
            P.dma("sp", posi[:], d_pos[:, :], writes=[posi])
            P.dma("sp", invf[:], d_invf[:, :], writes=[invf])
            P.dma("sp", gbc[:], d_mlan[j].partition_broadcast(128), writes=[gbc])
            P.dve(lambda e: e.tensor_copy(out=posf[:], in_=posi[:]), reads=[posi], writes=[posf])
            for tt in range(NT):
                P.dve(lambda e, tt=tt: e.tensor_scalar(out=ANG[:, tt * 32:(tt + 1) * 32], in0=invf[:], scalar1=posf[:, tt:tt + 1], scalar2=None, op0=ALU.mult),
                      reads=[invf, posf], writes=[ANG])
            P.dve(lambda e: e.tensor_scalar(out=NI, in0=ANG[:], scalar1=1.0 / TWO_PI, scalar2=None, op0=ALU.mult), reads=[ANG], writes=[NF])
            P.dve(lambda e: e.tensor_copy(out=NF[:], in_=NI), reads=[NF], writes=[NF])
            P.dve(lambda e: e.scalar_tensor_tensor(out=ANG[:], in0=NF[:], scalar=-C1, in1=ANG[:], op0=ALU.mult, op1=ALU.add), reads=[NF, ANG], writes=[ANG])
            P.dve(lambda e: e.scalar_tensor_tensor(out=ANG[:], in0=NF[:], scalar=-C2, in1=ANG[:], op0=ALU.mult, op1=ALU.add), reads=[NF, ANG], writes=[ANG])
            P.dve(lambda e: e.tensor_scalar(out=ANG[:], in0=ANG[:], scalar1=-PI, scalar2=PI, op0=ALU.max, op1=ALU.min), reads=[ANG], writes=[ANG])
            P.act(lambda e: e.activation(out=SIN[:], in_=ANG[:], func=AF.Sin), reads=[ANG], writes=[SIN])
            P.dve(lambda e: e.tensor_scalar(out=ANG[:], in0=ANG[:], scalar1=TWO_PI / 4, scalar2=None, op0=ALU.add), reads=[ANG], writes=[ANG])
            P.dve(lambda e: e.tensor_scalar(out=NF[:], in0=ANG[:], scalar1=PI, scalar2=None, op0=ALU.is_gt), reads=[ANG], writes=[NF])
            P.dve(lambda e: e.scalar_tensor_tensor(out=ANG[:], in0=NF[:], scalar=-TWO_PI, in1=ANG[:], op0=ALU.mult, op1=ALU.add), reads=[NF, ANG], writes=[ANG])
            P.dve(lambda e: e.tensor_scalar(out=ANG[:], in0=ANG[:], scalar1=-PI, scalar2=PI, op0=ALU.max, op1=ALU.min), reads=[ANG], writes=[ANG])
            P.act(lambda e: e.activation(out=COS[:], in_=ANG[:], func=AF.Sin), reads=[ANG], writes=[COS])
            load_ln(layer, 0)
            if dbg == ("cos", layer):
                P.dma("sp", d_dbg[0:128, 0:512], COS[:], reads=[COS])
                P.dma("sp", d_dbg[0:128, 512:1024], SIN[:], reads=[SIN])

            def rope(xa, xb, o1, o2, cb, sb_, reads, wt):
                n = 1
                for s_ in xa.shape[1:]:
                    n *= s_
                t1 = T1[:, 0:n]
                t2 = T2[:, 0:n]
                if len(xa.shape) == 3:
                    t1 = t1.rearrange("p (a b) -> p a b", b=xa.shape[2])
                    t2 = t2.rearrange("p (a b) -> p a b", b=xa.shape[2])
                P.dve(lambda e: e.tensor_tensor(out=t1, in0=xa, in1=cb, op=ALU.mult), reads=reads + [COS], writes=[T1])
                P.dve(lambda e: e.tensor_tensor(out=t2, in0=xb, in1=sb_, op=ALU.mult), reads=reads + [SIN], writes=[T2])
                P.dve(lambda e: e.tensor_tensor(out=o1, in0=t1, in1=t2, op=ALU.subtract), reads=[T1, T2], writes=[wt])
                P.dve(lambda e: e.tensor_tensor(out=t1, in0=xb, in1=cb, op=ALU.mult), reads=reads + [COS], writes=[T1])
                P.dve(lambda e: e.tensor_tensor(out=t2, in0=xa, in1=sb_, op=ALU.mult), reads=reads + [SIN], writes=[T2])
                P.dve(lambda e: e.tensor_tensor(out=o2, in0=t1, in1=t2, op=ALU.add), reads=[T1, T2], writes=[wt])

            wA = wload(wview(d_mlaw[j], 0, 512), [128, 8, 512])
            wB = wload(wview(d_mlaw[j], 512, 64), [128, 8, 64])
            CNs = [CN, QRS]
            KR2 = Tl(Hb[:, 15552:15680], "KR2")
            KR2.rd = list(OATt.rd)
            KRs = [KR, KR2]
            ss_l = [Tl(mla_f[:, 52 + 2 * k:54 + 2 * k], "mss%d" % k, True) for k in range(2)]
            tt_l = [Tl(mla_f[:, 56 + 2 * k:58 + 2 * k], "mtt%d" % k, True) for k in range(2)]
            rs_l = [Tl(mla_f[:, 60 + 2 * k:62 + 2 * k], "mrs%d" % k, True) for k in range(2)]

            def c_s1(tt):
                k = tt % 2
                cn, kr, ss, t_, rs = CNs[k], KRs[k], ss_l[k], tt_l[k], rs_l[k]
                b1 = ps1()
                for kc in range(KC):
                    P.pe(lambda e, kc=kc, b1=b1, tt=tt: e.matmul(PSt[:, b1, :], lhsT=XTb[:, kc, tt * 128:(tt + 1) * 128], rhs=wA[1][:, kc, :],
                                                             start=(kc == 0), stop=(kc == KC - 1)), reads=[XT[tt], wA[0]], writes=[PS[b1]])
                b2 = ps1()
                for kc in range(KC):
                    P.pe(lambda e, kc=kc, b2=b2, tt=tt: e.matmul(PSt[:, b2, 0:64], lhsT=XTb[:, kc, tt * 128:(tt + 1) * 128], rhs=wB[1][:, kc, :],
                                                             start=(kc == 0), stop=(kc == KC - 1)), reads=[XT[tt], wB[0]], writes=[PS[b2]])
                jk = nstg()
                for c in range(2):
                    P.act(lambda e, c=c, b1=b1, jk=jk, ss=ss: e.activation(out=jk[:, 0:256], in_=PSt[:, b1, c * 256:(c + 1) * 256], func=AF.Square,
                                                                       accum_out=ss[:, c:c + 1]), reads=[PS[b1]], writes=[jk, ss])
                P.dve(lambda e, ss=ss, t_=t_: e.tensor_scalar(out=t_[:, 0:2], in0=ss[:, 0:2], scalar1=1.0 / 256.0, scalar2=RMS_EPS, op0=ALU.mult, op1=ALU.add),
                      reads=[ss], writes=[t_])
                P.op("pool", lambda e, t_=t_, rs=rs: e.tensor_tensor(out=rs[:, 0:2], in0=t_[:, 0:2], in1=nhalf[:, 0:2], op=ALU.pow), reads=[t_, nhalf], writes=[rs])
                rope(PSt[:, b2, 0:32], PSt[:, b2, 32:64], kr[:, 0:32], kr[:, 32:64], cos3[:, tt, :], sin3[:, tt, :], [PS[b2]], kr)
                P.act(lambda e, kr=kr: e.copy(out=kr[:, 64:128], in_=kr[:, 0:64]), reads=[kr], writes=[kr])
                for c in range(2):
                    P.dve(lambda e, c=c, b1=b1, cn=cn, rs=rs: e.scalar_tensor_tensor(out=cn[:, c * 256:(c + 1) * 256], in0=PSt[:, b1, c * 256:(c + 1) * 256],
                                                                                scalar=rs[:, c:c + 1], in1=gbc[:, c * 256:(c + 1) * 256], op0=ALU.mult, op1=ALU.mult),
                          reads=[PS[b1], rs, gbc], writes=[cn])

            def c_s2(tt):
                k = tt % 2
                cn, kr = CNs[k], KRs[k]
                bt = ps1()
                pv = PSt[:, bt, :].bitcast(BF16)
                for c in range(4):
                    P.pe(lambda e, c=c, pv=pv, cn=cn: e.transpose(out=pv[:, c * 128:(c + 1) * 128], in_=cn[:, c * 128:(c + 1) * 128], identity=ident[:]),
                         reads=[cn, ident], writes=[PS[bt]])
                P.pe(lambda e, pv=pv, kr=kr: e.transpose(out=pv[:, 512:640], in_=kr[:], identity=ident[:]), reads=[kr, ident], writes=[PS[bt]])
                P.act(lambda e, pv=pv, tt=tt: e.copy(out=cqT[:, :, tt * 128:(tt + 1) * 128], in_=pv[:, 0:256].rearrange("p (a b) -> p a b", b=128)),
                      reads=[PS[bt]], writes=[cqTt])
                P.act(lambda e, pv=pv, tt=tt: e.copy(out=ckT[:, :, tt * 128:(tt + 1) * 128], in_=pv[:, 256:512].rearrange("p (a b) -> p a b", b=128)),
                      reads=[PS[bt]], writes=[ckTt])
                P.act(lambda e, pv=pv, tt=tt: e.copy(out=krT[:, tt * 128:(tt + 1) * 128], in_=pv[:, 512:640]), reads=[PS[bt]], writes=[krTt])

            c_s1(0)
            for tt in range(NT):
                if tt + 1 < NT:
                    c_s1(tt + 1)
                c_s2(tt)
            OATt.rd = OATt.rd + ([KR2.lw] if KR2.lw is not None else []) + KR2.rd
            prior = []
            for t in XT:
                if t.lw is not None:
                    prior.append(t.lw)
                prior.extend(t.rd)
            QRt.rd = list(prior)
            OAt.rd = list(prior)
            wqp, wqrv = walloc(1024)
            wqrv = wqrv.rearrange("p (a n) -> p a n", n=512)
            for rc in range(2):
                P.dma("pool", wqrv[:, rc, :].rearrange("p (h c) -> p h c", c=64),
                      d_mlauq[j][rc * 128:(rc + 1) * 128, :].rearrange("r (h c) -> r h c", c=192)[:, :, 128:192], writes=[wqp])
            wqr = (wqp, wqrv)
            QRSs = [QRS, CN]

            def q_s1(tt):
                qrs = QRSs[tt % 2]
                b1 = ps1()
                for rc in range(2):
                    P.pe(lambda e, rc=rc, b1=b1, tt=tt: e.matmul(PSt[:, b1, :], lhsT=cqT[:, rc, tt * 128:(tt + 1) * 128], rhs=wqrv[:, rc, :],
                                                             start=(rc == 0), stop=(rc == 1)), reads=[cqTt, wqr[0]], writes=[PS[b1]])
                p3 = PSt[:, b1, :].rearrange("p (h c) -> p h c", c=64)
                q3 = qrs[:].rearrange("p (h c) -> p h c", c=64)
                cb = cos3[:, tt, :].unsqueeze(1).to_broadcast([128, 8, 32])
                sb_ = sin3[:, tt, :].unsqueeze(1).to_broadcast([128, 8, 32])
                rope(p3[:, :, 0:32], p3[:, :, 32:64], q3[:, :, 0:32], q3[:, :, 32:64], cb, sb_, [PS[b1]], qrs)

            def q_s2(tt):
                qrs = QRSs[tt % 2]
                bt = ps1()
                pv = PSt[:, bt, :].bitcast(BF16)
                for c in range(4):
                    P.pe(lambda e, c=c, pv=pv, qrs=qrs: e.transpose(out=pv[:, c * 128:(c + 1) * 128], in_=qrs[:, c * 128:(c + 1) * 128], identity=ident[:]),
                         reads=[qrs, ident], writes=[PS[bt]])
                P.act(lambda e, pv=pv, tt=tt: e.copy(out=QR[:, :, tt * 128:(tt + 1) * 128], in_=pv[:, 0:512].rearrange("p (a b) -> p a b", b=128)),
                      reads=[PS[bt]], writes=[QRt])

            q_s1(0)
            for tt in range(NT):
                if tt + 1 < NT:
                    q_s1(tt + 1)
                q_s2(tt)
            if dbg == ("qr", layer):
                P.dma("sp", d_dbg[0:128, :].bitcast(BF16), XTf[:, 0:2048], reads=[QRt] + allXT)
                P.dma("sp", d_dbg[128:256, :].bitcast(BF16), krT[:, :], reads=[krTt])
                P.dma("sp", d_dbg[256:384, :].bitcast(BF16), cqT[:, 0, :], reads=[cqTt])
            Z2 = SCb[:, 6144:8192]
            Z2t = Tl(Z2, "Z2")
            Z2t.rd = [x for x in (COS.lw, SIN.lw) if x is not None] + COS.rd + SIN.rd
            P.dve(lambda e: e.memset(Z2[0:64, :], 0.0), writes=[Z2t])
            P.act(lambda e: e.copy(out=Z2[64:128, :], in_=krT[64:128, :]), reads=[krTt], writes=[Z2t])
            P.dve(lambda e: e.memset(krT[64:128, :], 0.0), reads=[Z2t], writes=[krTt])
            for i in range(2):
                P.dve(lambda e, i=i: e.memset(VH[i][:, :, 128:130], 1.0), writes=[VHt[i]])
            for hf in range(2):
                wo = wload(d_mlawo[j][hf * 512:(hf + 1) * 512, :].rearrange("(h d) n -> d h n", d=128), [128, 4, 1024])
                for hl in range(4):
                    h = hf * 4 + hl
                    vh, vht = VH[h % 2], VHt[h % 2]
                    wkv = wload(wview(d_mlaukv[j], h * 256, 256), [128, 2, 256])
                    wqn = wload(wview(d_mlauq[j], h * 192, 128), [128, 2, 128])
                    for g in range(4):
                        b = ps1()
                        for rc in range(2):
                            P.pe(lambda e, rc=rc, b=b, g=g, wkv=wkv: e.matmul(PSt[:, b, :], lhsT=wkv[1][:, rc, 0:128], rhs=ckT[:, rc, g * 512:(g + 1) * 512],
                                                                 start=(rc == 0), stop=(rc == 1)), reads=[wkv[0], ckTt], writes=[PS[b]])
                        P.act(lambda e, b=b, g=g: e.copy(out=knT[:, g * 512:(g + 1) * 512], in_=PSt[:, b, :]), reads=[PS[b]], writes=[knT])
                        b = ps1()
                        for rc in range(2):
                            P.pe(lambda e, rc=rc, b=b, g=g, wqn=wqn: e.matmul(PSt[:, b, :], lhsT=wqn[1][:, rc, :], rhs=cqT[:, rc, g * 512:(g + 1) * 512],
                                                                 start=(rc == 0), stop=(rc == 1)), reads=[wqn[0], cqTt], writes=[PS[b]])
                        P.act(lambda e, b=b, g=g: e.copy(out=qnT[:, g * 512:(g + 1) * 512], in_=PSt[:, b, :]), reads=[PS[b]], writes=[qnT])
                        b = ps1()
                        for t4 in range(4):
                            tt = 4 * g + t4
                            for rc in range(2):
                                P.pe(lambda e, rc=rc, b=b, tt=tt, t4=t4, wkv=wkv: e.matmul(PSt[:, b, t4 * 128:(t4 + 1) * 128], lhsT=ckT[:, rc, tt * 128:(tt + 1) * 128],
                                                                              rhs=wkv[1][:, rc, 128:256], start=(rc == 0), stop=(rc == 1)),
                                     reads=[wkv[0], ckTt], writes=[PS[b]])
                        P.act(lambda e, b=b, g=g, vh=vh: e.copy(out=vh[:, 4 * g:4 * g + 4, 0:128], in_=PSt[:, b, :].rearrange("p (a b) -> p a b", b=128)),
                              reads=[PS[b]], writes=[vht])
                    pb_ = (h % 2) * 64
                    hp = h // 2
                    for g in range(4):
                        ob = [ps1() for _ in range(4)]
                        C.resv = set(ob)
                        njk = 4 * g + 4

                        def emit_S(jk_, g=g):
                            n0 = max(0, jk_ - 4 * g) * 128
                            N = 512 - n0
                            bS = ps1()
                            P.pe(lambda e, bS=bS, jk_=jk_, g=g, n0=n0, N=N: e.matmul(PSt[:, bS, 0:N], lhsT=knT[:, jk_ * 128:(jk_ + 1) * 128],
                                                                                rhs=qnT[:, g * 512 + n0:(g + 1) * 512], start=True, stop=False),
                                 reads=[knT, qnT], writes=[PS[bS]])
                            kz = krT if pb_ == 0 else Z2
                            P.pe(lambda e, bS=bS, jk_=jk_, g=g, n0=n0, N=N, kz=kz, hp=hp: e.matmul(
                                PSt[:, bS, 0:N], lhsT=kz[:, jk_ * 128:(jk_ + 1) * 128], rhs=QR[:, hp, g * 512 + n0:(g + 1) * 512],
                                start=False, stop=True), reads=[krTt, Z2t, QRt], writes=[PS[bS]])
                            return bS, n0, N

                        LA = 3
                        pendq = [emit_S(i_) for i_ in range(min(LA, njk))]
                        for jk_ in range(njk):
                            bS, n0, N = pendq.pop(0)
                            if jk_ + LA < njk:
                                pendq.append(emit_S(jk_ + LA))
                            pbuf = PB[jk_ % 4]
                            P.act(lambda e, bS=bS, N=N, pbuf=pbuf: e.activation(out=pbuf[:, 0:N], in_=PSt[:, bS, 0:N], func=AF.Exp, scale=scale),
                                  reads=[PS[bS]], writes=[pbuf])
                            if jk_ >= 4 * g:
                                P.dve(lambda e, pbuf=pbuf: e.memset(pbuf[64:128, 0:64], 0.0), writes=[pbuf])
                            for qi in range(n0 // 128, 4):
                                c0 = qi * 128 - n0
                                P.pe(lambda e, qi=qi, c0=c0, pbuf=pbuf, jk_=jk_, vh=vh, ob=ob: e.matmul(
                                    PSt[:, ob[qi], 0:129], lhsT=pbuf[:, c0:c0 + 128], rhs=vh[:, jk_, 0:129], start=(jk_ == 0), stop=(jk_ == 4 * g + qi)),
                                    reads=[pbuf, vht], writes=[PS[ob[qi]]])
                        for qi in range(4):
                            tt = 4 * g + qi
                            rq = rec[qi]
                            P.dve(lambda e, qi=qi, ob=ob, rq=rq: e.reciprocal(out=rq[:, 0:1], in_=PSt[:, ob[qi], 128:129]), reads=[PS[ob[qi]]], writes=[rq])
                            P.act(lambda e, qi=qi, ob=ob, tt=tt, hl=hl, rq=rq: e.activation(out=OA[:, tt, hl * 128:(hl + 1) * 128], in_=PSt[:, ob[qi], 0:128],
                                                                                        func=AF.Copy, scale=rq[:, 0:1]),
                                  reads=[PS[ob[qi]], rq], writes=[OAt])
                        C.resv = set()
                for tt in range(NT):
                    bt = ps1()
                    pv = PSt[:, bt, :].bitcast(BF16)
                    for c in range(4):
                        P.pe(lambda e, c=c, pv=pv, tt=tt: e.transpose(out=pv[:, c * 128:(c + 1) * 128], in_=OA[:, tt, c * 128:(c + 1) * 128], identity=ident[:]),
                             reads=[OAt, ident], writes=[PS[bt]])
                    P.act(lambda e, pv=pv: e.copy(out=OAT[:, :, :], in_=pv[:, 0:512].rearrange("p (a b) -> p a b", b=128)), reads=[PS[bt]], writes=[OATt])
                    b2 = ps2()
                    for nh in range(2):
                        for c in range(4):
                            P.pe(lambda e, c=c, nh=nh, b2=b2, wo=wo: e.matmul(PSt[:, b2 + nh, :], lhsT=OAT[:, c, :], rhs=wo[1][:, c, nh * 512:(nh + 1) * 512],
                                                                   start=(c == 0), stop=(c == 3)), reads=[OATt, wo[0]], writes=[PS[b2 + nh]])
                    pin = PSt[:, b2:b2 + 2, :].rearrange("p a b -> p (a b)")
                    if hf == 0:
                        P.dve(lambda e, tt=tt, pin=pin: e.scalar_tensor_tensor(out=X[tt][:], in0=X[tt][:], scalar=ALPHA, in1=pin, op0=ALU.mult, op1=ALU.add),
                              reads=[X[tt], PS[b2], PS[b2 + 1]], writes=[X[tt]])
                    else:
                        P.dve(lambda e, tt=tt, pin=pin: e.tensor_tensor(out=X[tt][:], in0=X[tt][:], in1=pin, op=ALU.add),
                              reads=[X[tt], PS[b2], PS[b2 + 1]], writes=[X[tt]])
                        if dbg == ("h", layer):
                            P.dma("sp", d_dbg[tt * 128:(tt + 1) * 128, :], X[tt][:], reads=[X[tt]])
                        ln_inplace(tt)
            tail = [x for x in (QRt.lw, OAt.lw) if x is not None] + QRt.rd + OAt.rd
            for t in XT:
                t.rd = t.rd + tail
            if dbg == ("oa", layer):
                P.dma("sp", d_dbg[0:128, :].bitcast(BF16), XTf[:, 8192:10240], reads=[OAt] + allXT)
                P.dma("sp", d_dbg[128:256, :].bitcast(BF16), SCb[:, 0:2048], reads=[knT])
                P.dma("sp", d_dbg[256:384, :].bitcast(BF16), SCb[:, 2048:4096], reads=[qnT])
                P.dma("sp", d_dbg[384:512, :].bitcast(BF16), Hb[:, 10240 + 2080:10240 + 2080 + 2048], reads=VHt)
            for tt in range(NT):
                to_xt(tt)

        P.dma("sp", Xb[:, 0:8, :], d_x[0:1024, :].rearrange("(t q) d -> q t d", q=128), writes=X[0:8])
        P.dma("sp", Xb[:, 8:16, :], d_x[1024:2048, :].rearrange("(t q) d -> q t d", q=128), writes=X[8:16])
        for tt in range(NT):
            to_xt(tt)
        for li, layer in enumerate(layers):
            kind = layer % 3
            jj = layer // 3
            if kind == 0:
                gla_mixer(jj, layer)
            elif kind == 1:
                mla_mixer(jj, layer)
            else:
                conv_mixer(jj, layer)
            if dbg == ("a", layer):
                P.dma("sp", d_dbg.rearrange("(t q) d -> q t d", q=128), Xb[:, :, :], reads=X)
            down = mlp(layer)
            ple(layer, last=(li == len(layers) - 1), down=down)
        P.dma("sp", d_out[0:1024, :].rearrange("(t q) d -> q t d", q=128), Xb[:, 0:8, :], reads=X[0:8])
        P.dma("sp", d_out[1024:2048, :].rearrange("(t q) d -> q t d", q=128), Xb[:, 8:16, :], reads=X[8:16])
        P.emit(st)
    return nc


def host_consts():
    c = {}
    c["c_ident"] = np.eye(128, dtype=np.float32)
    s_ = np.arange(128)[:, None]
    t_ = np.arange(128)[None, :]
    mu = (t_ >= s_).astype(np.float32)
    ml = ((t_ < s_) & ((t_ // 64) == (s_ // 64))).astype(np.float32)
    c["c_gmask"] = np.concatenate([mu, ml], axis=1)
    rm = np.ones((128, 512), np.float32)
    rm[:, 0::128] = 0.0
    c["c_rmask"] = rm
    invf = (10000.0 ** (-np.arange(0, 32, dtype=np.float32) * np.float32(2.0 / 64))).astype(np.float32)
    c["c_invf"] = np.broadcast_to(invf[None, :], (128, 32)).copy()
    return c


def make_in_maps(inputs, xs):
    f = lambda a: np.ascontiguousarray(np.asarray(a, dtype=np.float32))
    shared = {
        "gla_w_in": f(inputs["gla_w_in"]), "gla_w_gate_up": f(inputs["gla_w_gate_up"]),
        "gla_b_gate_t": f(np.asarray(inputs["gla_b_gate"]).reshape(2, 4, 128).transpose(0, 2, 1)),
        "gla_norm_g_t": f(np.asarray(inputs["gla_norm_g"]).reshape(2, 2, 128).transpose(0, 2, 1)),
        "gla_w_out": f(inputs["gla_w_out"]),
        "mla_w_in": f(inputs["mla_w_in"]),
        "mla_norms": f(np.concatenate([np.asarray(inputs["mla_q_norm"]), np.asarray(inputs["mla_kv_norm"])], axis=1)),
        "mla_w_uq": f(inputs["mla_w_uq"]), "mla_w_ukv": f(inputs["mla_w_ukv"]), "mla_w_out": f(inputs["mla_w_out"]),
        "conv_w_in": f(inputs["conv_w_in"]),
        "conv_w_t": f(np.asarray(inputs["conv_w"]).reshape(1, 3, 8, 128).transpose(0, 3, 2, 1)),
        "conv_w_out": f(inputs["conv_w_out"]),
        "ln_g": f(inputs["ln_g"]), "ln_b": f(inputs["ln_b"]),
        "mlp_w1": f(inputs["mlp_w1"]), "mlp_w2": f(inputs["mlp_w2"]),
        "ple_w_gate": f(inputs["ple_w_gate"]), "ple_w_proj": f(inputs["ple_w_proj"]),
    }
    shared.update(host_consts())
    maps = []
    p = np.asarray(inputs["p"], dtype=np.float32)
    pos = np.asarray(inputs["positions"]).astype(np.int32)
    for c, xc in enumerate(xs):
        m = dict(shared)
        m["x"] = f(xc)
        m["p"] = np.ascontiguousarray(p[:, c])
        m["pos"] = np.ascontiguousarray(pos[c].reshape(NT, 128).T)
        maps.append(m)
    return maps


_NC_CACHE = {}


def kernel(**inputs):
    x = np.asarray(inputs["x"], dtype=np.float32)
    B = x.shape[0]
    key = "all"
    if key not in _NC_CACHE:
        _NC_CACHE[key] = build_nc([0, 1, 2, 3])
    nc = _NC_CACHE[key]
    maps = make_in_maps(inputs, [x[b] for b in range(B)])
    res = run_bass_kernel_spmd(nc, maps, core_ids=list(range(B)))
    return np.stack([np.asarray(r["out"], dtype=np.float32) for r in res.results], axis=0)
```

```python
import numpy as np
from contextlib import ExitStack
import concourse.bass as bass
import concourse.mybir as mybir
from concourse.bass_utils import run_bass_kernel_spmd

F32 = mybir.dt.float32
BF16 = mybir.dt.bfloat16
I32 = mybir.dt.int32
AF = mybir.ActivationFunctionType
ALU = mybir.AluOpType

ENGS = ("pe", "act", "dve", "pool", "sp")

S = 2048
D = 1024
NT = 16
KC = 8
DEPTH = 4
ALPHA = (2 * DEPTH) ** 0.25
LN_EPS = 1e-5
RMS_EPS = 1e-6
SAME_ENGINE_SYNC = False
RSTD_LATE = True
ST_DVE = 0
PREP_SPLIT = True
C_SKEW = True


class Tl:
    __slots__ = ("ap", "name", "lw", "rd", "small", "owner")

    def __init__(self, ap, name="", small=False):
        self.ap = ap
        self.name = name
        self.lw = None
        self.rd = []
        self.small = small
        self.owner = None

    def __getitem__(self, k):
        return self.ap[k]


class WPiece(list):
    oid = None


class Ins:
    __slots__ = ("eng", "fn", "deps", "pos", "sig", "sigval", "dma", "dsem", "dval")

    def __init__(self, eng, fn, dma=False):
        self.eng = eng
        self.fn = fn
        self.deps = []
        self.pos = -1
        self.sig = False
        self.sigval = 0
        self.dma = dma
        self.dsem = -1
        self.dval = 0


class Prog:
    def __init__(self, nc, n_hw_sems=6, n_sw_sems=8):
        self.nc = nc
        self.streams = {e: [] for e in ENGS}
        self.n_hw = n_hw_sems
        self.n_dma_sems = n_hw_sems + n_sw_sems
        self.rr_hw = 0
        self.rr_sw = 0
        self.dma_last = [None] * self.n_dma_sems
        self.dma_cnt = [0] * self.n_dma_sems

    @staticmethod
    def _flat(lst):
        out = []
        for t in lst:
            if isinstance(t, (list, tuple)):
                oid = getattr(t, "oid", None)
                for u in t:
                    if oid is not None and u.owner != oid:
                        raise RuntimeError("weight piece clobbered before use: %s" % u.name)
                    out.append(u)
            else:
                out.append(t)
        return out

    def op(self, eng, fn, reads=(), writes=(), dma=False):
        reads = self._flat(reads)
        writes = self._flat(writes)
        ins = Ins(eng, fn, dma)
        deps = []
        for t in reads:
            if t.lw is not None:
                deps.append((t.lw, t.small))
        for t in writes:
            if t.lw is not None:
                deps.append((t.lw, t.small))
            for r in t.rd:
                deps.append((r, t.small))
        if dma:
            if eng == "pool":
                s = self.n_hw + self.rr_sw
                self.rr_sw = (self.rr_sw + 1) % (self.n_dma_sems - self.n_hw)
            else:
                s = self.rr_hw
                self.rr_hw = (self.rr_hw + 1) % self.n_hw
            prev = self.dma_last[s]
            if prev is not None:
                deps.append((prev, True))
            self.dma_cnt[s] += 16
            ins.dsem = s
            ins.dval = self.dma_cnt[s]
            self.dma_last[s] = ins
        seen = set()
        best = {}
        for d, force in deps:
            if d is ins:
                continue
            if d.dma:
                if id(d) not in seen:
                    seen.add(id(d))
                    ins.deps.append(d)
                continue
            if d.eng == eng and not dma:
                if eng == "pe":
                    continue
                if not (force or SAME_ENGINE_SYNC):
                    continue
            if d.eng not in best or best[d.eng].pos < d.pos:
                best[d.eng] = d
        ins.deps.extend(best.values())
        ins.pos = len(self.streams[eng])
        self.streams[eng].append(ins)
        for t in reads:
            t.rd.append(ins)
        for t in writes:
            t.lw = ins
            t.rd = []
        return ins

    def pe(self, fn, reads=(), writes=()):
        return self.op("pe", fn, reads, writes)

    def act(self, fn, reads=(), writes=()):
        return self.op("act", fn, reads, writes)

    def dve(self, fn, reads=(), writes=()):
        return self.op("dve", fn, reads, writes)

    def dma(self, q, out_ap, in_ap, reads=(), writes=()):
        return self.op(q, lambda e: e.dma_start(out=out_ap, in_=in_ap), reads, writes, dma=True)

    def emit(self, stack):
        nc = self.nc
        esem = {e: stack.enter_context(nc.semaphore("s_" + e)) for e in ENGS}
        dsem = [stack.enter_context(nc.semaphore("d_%d" % i)) for i in range(self.n_dma_sems)]
        for e in ENGS:
            for ins in self.streams[e]:
                for d in ins.deps:
                    if not d.dma:
                        d.sig = True
        for e in ENGS:
            c = 0
            for ins in self.streams[e]:
                if ins.sig and not ins.dma:
                    c += 1
                    ins.sigval = c
        final_waits = [(i, self.dma_cnt[i]) for i in range(self.n_dma_sems) if self.dma_cnt[i] > 0]
        block = stack.enter_context(nc.Block())
        engobj = {"pe": "tensor", "act": "scalar", "dve": "vector", "pool": "gpsimd", "sp": "sync"}

        def make(e):
            def body(eng):
                seen_eng = {x: 0 for x in ENGS}
                seen_dma = [0] * self.n_dma_sems
                for ins in self.streams[e]:
                    for d in ins.deps:
                        if d.dma:
                            if seen_dma[d.dsem] < d.dval:
                                eng.wait_ge(dsem[d.dsem], d.dval)
                                seen_dma[d.dsem] = d.dval
                        else:
                            if seen_eng[d.eng] < d.sigval:
                                eng.wait_ge(esem[d.eng], d.sigval)
                                seen_eng[d.eng] = d.sigval
                    r = ins.fn(eng)
                    if ins.dma:
                        r.then_inc(dsem[ins.dsem], 16)
                    elif ins.sig:
                        r.then_inc(esem[e], 1)
                if e == "sp":
                    for i, v in final_waits:
                        if seen_dma[i] < v:
                            eng.wait_ge(dsem[i], v)
            return body

        for e in ENGS:
            getattr(block, engobj[e])(make(e))


WUNIT = 1024
NWUNIT = 16


class Ctx:
    pass


def build_nc(layers, dbg=None):
    nc = bass.Bass("TRN2", target_bir_lowering=False)
    dt = lambda name, shape, dtype=F32, kind="ExternalInput": nc.dram_tensor(name, shape, dtype, kind=kind).ap()
    d_x = dt("x", [S, D])
    d_p = dt("p", [DEPTH, S, 256])
    d_pos = dt("pos", [128, NT], I32)
    d_glaw = dt("gla_w_in", [2, D, 3088])
    d_glagu = dt("gla_w_gate_up", [2, 16, 512])
    d_glab = dt("gla_b_gate_t", [2, 128, 4])
    d_glang = dt("gla_norm_g_t", [2, 128, 2])
    d_glawo = dt("gla_w_out", [2, D, D])
    d_mlaw = dt("mla_w_in", [1, D, 576])
    d_mlan = dt("mla_norms", [1, 512])
    d_mlauq = dt("mla_w_uq", [1, 256, 1536])
    d_mlaukv = dt("mla_w_ukv", [1, 256, 2048])
    d_mlawo = dt("mla_w_out", [1, D, D])
    d_cwin = dt("conv_w_in", [1, D, 3072])
    d_cw = dt("conv_w_t", [1, 128, 8, 3])
    d_cwo = dt("conv_w_out", [1, D, D])
    d_lng = dt("ln_g", [DEPTH, 2, D])
    d_lnb = dt("ln_b", [DEPTH, 2, D])
    d_w1 = dt("mlp_w1", [DEPTH, D, 4 * D])
    d_w2 = dt("mlp_w2", [DEPTH, 4 * D, D])
    d_pwg = dt("ple_w_gate", [DEPTH, D, D])
    d_pwp = dt("ple_w_proj", [DEPTH, 256, D])
    d_ident = dt("c_ident", [128, 128])
    d_gmask = dt("c_gmask", [128, 256])
    d_rmask = dt("c_rmask", [128, 512])
    d_invf = dt("c_invf", [128, 32])
    d_out = dt("out", [S, D], F32, "ExternalOutput")
    d_dbg = dt("dbg", [S, D], F32, "ExternalOutput") if dbg else None

    with ExitStack() as st:
        P = Prog(nc)
        sb = lambda name, shape, dtype: st.enter_context(nc.sbuf_tensor(name, shape, dtype))

        Xb = sb("X", [128, NT, D], F32)
        X = [Tl(Xb[:, t, :], "X%d" % t) for t in range(NT)]
        XTb = sb("XT", [128, KC, S], BF16)
        XT = [Tl(XTb[:, :, t * 128:(t + 1) * 128], "XT%d" % t) for t in range(NT)]
        Hb = sb("H", [128, 16384], BF16)
        SCb = sb("SC", [128, 11264], BF16)
        Wb = sb("W", [128, NWUNIT * WUNIT], BF16)
        Wt = [Tl(Wb[:, i * WUNIT:(i + 1) * WUNIT], "W%d" % i) for i in range(NWUNIT)]
        C_w = {"rr": 0, "oid": 0}
        LNg = Tl(sb("LNg", [128, D], F32), "LNg")
        LNb = Tl(sb("LNb", [128, D], F32), "LNb")
        identf = Tl(sb("identf", [128, 128], F32))
        ident = Tl(sb("ident", [128, 128], BF16))
        xbf = [Tl(sb("xbf%d" % i, [128, D], BF16)) for i in range(2)]
        stg = [Tl(sb("stg%d" % i, [128, 512], F32)) for i in range(2)]
        st6s = [Tl(sb("st6_%d" % i, [128, 2, 6], F32), small=True) for i in range(4)]
        mvs = [Tl(sb("mv_%d" % i, [128, 8], F32), small=True) for i in range(4)]
        nhalf = Tl(sb("nhalf", [128, 8], F32), "nhalf")
        PSt = st.enter_context(nc.psum_tensor("ps", [128, 8, 512], F32))
        PS = [Tl(PSt[:, i, :], "PS%d" % i) for i in range(8)]
        C = Ctx()
        C.ps_rr = 0
        C.xb_rr = 0
        C.mv_rr = 0
        C.stg_rr = 0

        C.resv = set()

        def ps1():
            while C.ps_rr in C.resv:
                C.ps_rr = (C.ps_rr + 1) % 8
            i = C.ps_rr
            C.ps_rr = (C.ps_rr + 1) % 8
            return i

        def ps2():
            if C.ps_rr % 2:
                C.ps_rr = (C.ps_rr + 1) % 8
            while C.ps_rr in C.resv or (C.ps_rr + 1) in C.resv:
                C.ps_rr = (C.ps_rr + 2) % 8
            i = C.ps_rr
            C.ps_rr = (C.ps_rr + 2) % 8
            return i

        ARENA = {"H": [], "SC": []}

        def fence(arena, new_tiles):
            prior = []
            for t in ARENA[arena]:
                if t.lw is not None:
                    prior.append(t.lw)
                prior.extend(t.rd)
            for t in new_tiles:
                t.rd = list(prior)
            ARENA[arena] = list(new_tiles)

        def nstg():
            i = C.stg_rr
            C.stg_rr = (C.stg_rr + 1) % 2
            return stg[i]

        def walloc(n):
            nu = (n + WUNIT - 1) // WUNIT
            if C_w["rr"] + nu > NWUNIT:
                C_w["rr"] = 0
            u0 = C_w["rr"]
            C_w["rr"] = (u0 + nu) % NWUNIT
            C_w["oid"] += 1
            wp = WPiece(Wt[u0:u0 + nu])
            wp.oid = C_w["oid"]
            for u in wp:
                u.owner = wp.oid
            return wp, Wb[:, u0 * WUNIT:u0 * WUNIT + n]

        def wload(src_ap, shape):
            n = 1
            for s_ in shape[1:]:
                n *= s_
            wp, view = walloc(n)
            if len(shape) == 3:
                view = view.rearrange("p (a b) -> p a b", b=shape[2])
            elif len(shape) == 4:
                view = view.rearrange("p (a b c) -> p a b c", b=shape[2], c=shape[3])
            P.dma("pool", view, src_ap, writes=[wp])
            return wp, view

        def wview(dram2d, c0, ncols):
            return dram2d[:, c0:c0 + ncols].rearrange("(kc k) n -> k kc n", k=128)

        P.op("pool", lambda e: e.memset(nhalf[:], -0.5), writes=[nhalf])
        P.dma("sp", identf[:], d_ident[:, :], writes=[identf])
        P.dve(lambda e: e.tensor_copy(out=ident[:], in_=identf[:]), reads=[identf], writes=[ident])

        gsm_b = sb("gsm_b", [128, 1024], BF16)
        gsm_f = sb("gsm_f", [128, 64], F32)
        G_SM = (Tl(gsm_b[0:16, 0:512], "glrT"), Tl(gsm_b[0:16, 512:1024], "wgu"),
                Tl(gsm_f[:, 0:4], "bgt", True), Tl(gsm_f[:, 4:8], "nbg", True), Tl(gsm_f[:, 8:10], "ngt", True),
                Tl(gsm_f[:, 16:32], "eend", True), Tl(gsm_f[:, 32:40], "ssq", True), Tl(gsm_f[:, 40:48], "rstd", True),
                Tl(sb("k3T", [128, 512], BF16), "k3T"), Tl(sb("gmask", [128, 256], F32), "gmask"),
                Tl(sb("rmask", [128, 512], BF16), "rmask"), Tl(sb("wglr", [128, 8, 16], BF16), "wglr"))
        mla_i = sb("mla_i", [128, 16], I32)
        mla_f = sb("mla_f", [128, 64], F32)
        MLA_SM = (Tl(mla_i[:, :], "posi", True), Tl(mla_f[:, 0:16], "posf", True), Tl(mla_f[:, 16:48], "invf"),
                  [Tl(mla_f[:, 48 + i:49 + i], "rec%d" % i, True) for i in range(4)])
        P.dma("sp", G_SM[9][:], d_gmask[:, :], writes=[G_SM[9]])
        P.dma("pool", G_SM[10][:], d_rmask[:, :], writes=[G_SM[10]])

        def to_xt(tt, on_act=True, evac_act=False):
            xb = xbf[C.xb_rr]
            C.xb_rr ^= 1
            if on_act:
                P.act(lambda e: e.copy(out=xb[:], in_=X[tt][:]), reads=[X[tt]], writes=[xb])
            else:
                P.dve(lambda e: e.tensor_copy(out=xb[:], in_=X[tt][:]), reads=[X[tt]], writes=[xb])
            b = ps1()
            pv = PSt[:, b, :].bitcast(BF16)
            for kc in range(KC):
                P.pe(lambda e, kc=kc: e.transpose(out=pv[:, kc * 128:(kc + 1) * 128], in_=xb[:, kc * 128:(kc + 1) * 128],
                                                  identity=ident[:]), reads=[xb, ident], writes=[PS[b]])
            if evac_act:
                P.act(lambda e: e.copy(out=XT[tt][:], in_=pv.rearrange("p (a b) -> p a b", b=128)), reads=[PS[b]], writes=[XT[tt]])
            else:
                P.dve(lambda e: e.tensor_copy(out=XT[tt][:], in_=pv.rearrange("p (a b) -> p a b", b=128)),
                      reads=[PS[b]], writes=[XT[tt]])

        class XtQ:
            def __init__(self, delay=2):
                self.q = []
                self.delay = delay

            def push(self, tt):
                self.q.append(tt)
                while len(self.q) > self.delay:
                    to_xt(self.q.pop(0))

            def flush(self):
                while self.q:
                    to_xt(self.q.pop(0))

        def load_ln(layer, which):
            P.dma("sp", LNg[:], d_lng[layer, which, :].partition_broadcast(128), writes=[LNg])
            P.dma("sp", LNb[:], d_lnb[layer, which, :].partition_broadcast(128), writes=[LNb])

        def ln_inplace(tt):
            k = C.mv_rr
            C.mv_rr = (C.mv_rr + 1) % 4
            m, s6 = mvs[k], st6s[k]
            P.dve(lambda e: e.bn_stats(out=s6[:, 0, :], in_=X[tt][:, 0:512]), reads=[X[tt]], writes=[s6])
            P.dve(lambda e: e.bn_stats(out=s6[:, 1, :], in_=X[tt][:, 512:1024]), reads=[X[tt]], writes=[s6])
            P.dve(lambda e: e.bn_aggr(out=m[:, 0:2], in_=s6[:]), reads=[s6], writes=[m])
            P.op("pool", lambda e: e.tensor_scalar_add(out=m[:, 2:3], in0=m[:, 1:2], scalar1=LN_EPS), reads=[m], writes=[m])
            P.op("pool", lambda e: e.tensor_tensor(out=m[:, 3:4], in0=m[:, 2:3], in1=nhalf[:, 0:1], op=ALU.pow), reads=[m, nhalf], writes=[m])
            P.dve(lambda e: e.scalar_tensor_tensor(out=X[tt][:], in0=X[tt][:], scalar=m[:, 0:1], in1=LNg[:], op0=ALU.subtract, op1=ALU.mult),
                  reads=[X[tt], m, LNg], writes=[X[tt]])
            P.dve(lambda e: e.scalar_tensor_tensor(out=X[tt][:], in0=X[tt][:], scalar=m[:, 3:4], in1=LNb[:], op0=ALU.mult, op1=ALU.add),
                  reads=[X[tt], m, LNb], writes=[X[tt]])

        def resid_ln(tt, b2):
            P.dve(lambda e: e.scalar_tensor_tensor(out=X[tt][:], in0=X[tt][:], scalar=ALPHA, in1=PSt[:, b2:b2 + 2, :].rearrange("p a b -> p (a b)"),
                                                   op0=ALU.mult, op1=ALU.add), reads=[X[tt], PS[b2], PS[b2 + 1]], writes=[X[tt]])
            ln_inplace(tt)

        def dense_tok(tt, wv0, wv1, wt0, wt1, lhs_fn, nk, lhs_tiles):
            b2 = ps2()
            for half, (wv, wt) in enumerate(((wv0, wt0), (wv1, wt1))):
                for k in range(nk):
                    P.pe(lambda e, k=k, wv=wv, half=half: e.matmul(PSt[:, b2 + half, :], lhsT=lhs_fn(k), rhs=wv[:, k, :],
                                                                   start=(k == 0), stop=(k == nk - 1)),
                         reads=list(lhs_tiles) + [wt], writes=[PS[b2 + half]])
            return b2

        def mlp(layer):
            hT = Hb[:, :].rearrange("p (f t) -> p f t", t=S)
            hTl = [Tl(hT[:, f, :], "hT%d" % f) for f in range(8)]
            fence("H", hTl)
            w1 = d_w1[layer]
            w2 = d_w2[layer]
            load_ln(layer, 1)
            for fb in range(4):
                for half in range(2):
                    wt, wv = wload(wview(w1, fb * 1024 + half * 512, 512), [128, 8, 512])
                    for f4 in range(4):
                        f = half * 4 + f4
                        for g in range(4):
                            b = ps1()
                            for kc in range(KC):
                                P.pe(lambda e, kc=kc, f4=f4, g=g, wv=wv, b=b: e.matmul(
                                    PSt[:, b, :], lhsT=wv[:, kc, f4 * 128:(f4 + 1) * 128], rhs=XTb[:, kc, g * 512:(g + 1) * 512],
                                    start=(kc == 0), stop=(kc == KC - 1)),
                                    reads=[wt] + XT[4 * g:4 * g + 4], writes=[PS[b]])
                            r = nstg()
                            P.act(lambda e, r=r, b=b: e.activation(out=r[:], in_=PSt[:, b, :], func=AF.Relu), reads=[PS[b]], writes=[r])
                            P.dve(lambda e, r=r, f=f, g=g: e.tensor_tensor(out=hT[:, f, g * 512:(g + 1) * 512], in0=r[:], in1=r[:], op=ALU.mult),
                                  reads=[r], writes=[hTl[f]])
                wts = []
                for half in range(2):
                    wts.append(wload(w2[fb * 1024 + half * 512: fb * 1024 + (half + 1) * 512, :].rearrange("(fc f) n -> f fc n", f=128),
                                     [128, 4, 1024]))

                def down(tt, fb=fb, wts=wts):
                    b2 = ps2()
                    for nh in range(2):
                        for f in range(8):
                            wt, wv = wts[f // 4]
                            P.pe(lambda e, f=f, nh=nh, wv=wv, tt=tt, b2=b2: e.matmul(
                                PSt[:, b2 + nh, :], lhsT=hT[:, f, tt * 128:(tt + 1) * 128], rhs=wv[:, f % 4, nh * 512:(nh + 1) * 512],
                                start=(f == 0), stop=(f == 7)), reads=[hTl[f], wt], writes=[PS[b2 + nh]])
                    pin = PSt[:, b2:b2 + 2, :].rearrange("p a b -> p (a b)")
                    if fb == 0:
                        P.dve(lambda e, tt=tt, pin=pin: e.scalar_tensor_tensor(out=X[tt][:], in0=X[tt][:], scalar=ALPHA, in1=pin,
                                                                             op0=ALU.mult, op1=ALU.add),
                              reads=[X[tt], PS[b2], PS[b2 + 1]], writes=[X[tt]])
                    else:
                        P.dve(lambda e, tt=tt, pin=pin: e.tensor_tensor(out=X[tt][:], in0=X[tt][:], in1=pin, op=ALU.add),
                              reads=[X[tt], PS[b2], PS[b2 + 1]], writes=[X[tt]])
                    if fb == 3:
                        ln_inplace(tt)

                if fb < 3:
                    for tt in range(NT):
                        down(tt)
            return down

        def ple(layer, last, down):
            pbf = SCb[:, 0:4096].rearrange("p (t k) -> p t k", k=256)
            pbt = Tl(pbf, "pbf")
            pT = SCb[:, 4096:8192].rearrange("p (c t) -> p c t", t=S)
            pTt = Tl(pT, "pT")
            wpv = SCb[:, 8192:10240].rearrange("p (c n) -> p c n", n=1024)
            wpt = Tl(wpv, "wp")
            fence("SC", [pbt, pTt, wpt])
            P.dma("pool", pbf, d_p[layer].rearrange("(t q) k -> q t k", q=128), writes=[pbt])
            P.dma("pool", wpv, wview(d_pwp[layer], 0, 1024), writes=[wpt])
            for tt in range(NT):
                b = ps1()
                pv = PSt[:, b, :].bitcast(BF16)
                for c in range(2):
                    P.pe(lambda e, c=c, tt=tt, pv=pv: e.transpose(out=pv[:, c * 128:(c + 1) * 128], in_=pbf[:, tt, c * 128:(c + 1) * 128],
                                                             identity=ident[:]), reads=[pbt, ident], writes=[PS[b]])
                P.act(lambda e, tt=tt, pv=pv: e.copy(out=pT[:, :, tt * 128:(tt + 1) * 128], in_=pv[:, 0:256].rearrange("p (a b) -> p a b", b=128)),
                      reads=[PS[b]], writes=[pTt])
            wg = [wload(wview(d_pwg[layer], h * 512, 512), [128, 8, 512]) for h in range(2)]

            def ple_tile(tt):
                for half in range(2):
                    wt, wv = wg[half]
                    ba = ps1()
                    for kc in range(KC):
                        P.pe(lambda e, kc=kc, wv=wv, ba=ba, tt=tt: e.matmul(PSt[:, ba, :], lhsT=XTb[:, kc, tt * 128:(tt + 1) * 128], rhs=wv[:, kc, :],
                                                                        start=(kc == 0), stop=(kc == KC - 1)),
                             reads=[XT[tt], wt], writes=[PS[ba]])
                    bb = ps1()
                    for c in range(2):
                        P.pe(lambda e, c=c, bb=bb, tt=tt, half=half: e.matmul(PSt[:, bb, :], lhsT=pT[:, c, tt * 128:(tt + 1) * 128],
                                                                          rhs=wpv[:, c, half * 512:(half + 1) * 512], start=(c == 0), stop=(c == 1)),
                             reads=[pTt, wpt], writes=[PS[bb]])
                    sg = nstg()
                    P.act(lambda e, sg=sg, ba=ba: e.activation(out=sg[:], in_=PSt[:, ba, :], func=AF.Sigmoid), reads=[PS[ba]], writes=[sg])
                    P.dve(lambda e, sg=sg, bb=bb: e.tensor_tensor(out=sg[:], in0=sg[:], in1=PSt[:, bb, :], op=ALU.mult), reads=[sg, PS[bb]], writes=[sg])
                    P.dve(lambda e, sg=sg, tt=tt, half=half: e.tensor_tensor(out=X[tt][:, half * 512:(half + 1) * 512],
                                                                             in0=X[tt][:, half * 512:(half + 1) * 512], in1=sg[:], op=ALU.add),
                          reads=[sg, X[tt]], writes=[X[tt]])

            D1, D2, D3 = 2, 3, 5
            for i in range(NT + D3):
                if i < NT:
                    down(i)
                if 0 <= i - D1 < NT:
                    to_xt(i - D1, evac_act=True)
                if 0 <= i - D2 < NT:
                    ple_tile(i - D2)
                if not last and 0 <= i - D3 < NT:
                    to_xt(i - D3, evac_act=True)

        def conv_mixer(j, layer):
            YT = Hb[:, :].rearrange("p (c t) -> p c t", t=S)
            YTl = [Tl(YT[:, c, :], "YT%d" % c) for c in range(8)]
            CUf = SCb[:, 0:4352].bitcast(F32)
            CU = Tl(CUf, "CU")
            Zf = SCb[:, 4352:8448].bitcast(F32)
            Z = Tl(Zf, "Z")
            Bs = Tl(SCb[:, 8448:10496], "Bs")
            cw = Tl(SCb[:, 10496:10544].bitcast(F32).rearrange("p (c j) -> p c j", j=3), "cw", small=True)
            fence("H", YTl)
            fence("SC", [CU, Z, Bs, cw])
            P.dma("sp", cw[:], d_cw[j], writes=[cw])
            P.dve(lambda e: e.memset(CUf[:, 0:2], 0.0), writes=[CU])
            load_ln(layer, 0)
            win = d_cwin[j]
            for cg in range(4):
                wp = [wload(wview(win, sec * 1024 + cg * 256, 256), [128, 8, 256]) for sec in range(3)]
                for c4 in range(2):
                    cc = cg * 2 + c4
                    for g in range(4):
                        bs = []
                        for sec in range(3):
                            wt, wv = wp[sec]
                            b = ps1()
                            bs.append(b)
                            for kc in range(KC):
                                P.pe(lambda e, kc=kc, wv=wv, b=b, c4=c4, g=g: e.matmul(
                                    PSt[:, b, :], lhsT=wv[:, kc, c4 * 128:(c4 + 1) * 128], rhs=XTb[:, kc, g * 512:(g + 1) * 512],
                                    start=(kc == 0), stop=(kc == KC - 1)), reads=[wt] + XT[4 * g:4 * g + 4], writes=[PS[b]])
                        P.act(lambda e, b=bs[0], g=g: e.copy(out=Bs[:, g * 512:(g + 1) * 512], in_=PSt[:, b, :]), reads=[PS[bs[0]]], writes=[Bs])
                        cs = nstg()
                        P.act(lambda e, b=bs[1], cs=cs: e.copy(out=cs[:], in_=PSt[:, b, :]), reads=[PS[bs[1]]], writes=[cs])
                        P.dve(lambda e, b=bs[2], cs=cs, g=g: e.tensor_tensor(out=CUf[:, 2 + g * 512: 2 + (g + 1) * 512], in0=PSt[:, b, :], in1=cs[:], op=ALU.mult),
                              reads=[PS[bs[2]], cs], writes=[CU])
                    P.dve(lambda e, cc=cc: e.tensor_scalar(out=Zf[:, :], in0=CUf[:, 2:2050], scalar1=cw[:, cc, 2:3], scalar2=None, op0=ALU.mult),
                          reads=[CU, cw], writes=[Z])
                    P.dve(lambda e, cc=cc: e.scalar_tensor_tensor(out=Zf[:, :], in0=CUf[:, 1:2049], scalar=cw[:, cc, 1:2], in1=Zf[:, :], op0=ALU.mult, op1=ALU.add),
                          reads=[CU, cw, Z], writes=[Z])
                    P.dve(lambda e, cc=cc: e.scalar_tensor_tensor(out=Zf[:, :], in0=CUf[:, 0:2048], scalar=cw[:, cc, 0:1], in1=Zf[:, :], op0=ALU.mult, op1=ALU.add),
                          reads=[CU, cw, Z], writes=[Z])
                    P.dve(lambda e, cc=cc: e.tensor_tensor(out=YT[:, cc, :], in0=Zf[:, :], in1=Bs[:], op=ALU.mult), reads=[Z, Bs], writes=[YTl[cc]])
            wo = [wload(wview(d_cwo[j], h * 512, 512), [128, 8, 512]) for h in range(2)]
            xq = XtQ(2)
            for tt in range(NT):
                b2 = dense_tok(tt, wo[0][1], wo[1][1], wo[0][0], wo[1][0], lambda k, tt=tt: YT[:, k, tt * 128:(tt + 1) * 128], 8, YTl)
                resid_ln(tt, b2)
                xq.push(tt)
            xq.flush()

        def gla_mixer(j, layer):
            win = d_glaw[j]
            o = [0]

            def take(n):
                r = o[0]
                o[0] += n
                return r
            a0 = take(2048); Vh = [Hb[:, a0 + i * 1024: a0 + (i + 1) * 1024].rearrange("p (t n) -> p t n", n=256) for i in range(2)]
            Vht = [Tl(Vh[i], "Vh%d" % i) for i in range(2)]
            a0 = take(2048); Gh = [Hb[:, a0 + i * 1024: a0 + (i + 1) * 1024].rearrange("p (t n) -> p t n", n=256) for i in range(2)]
            Ght = [Tl(Gh[i], "Gh%d" % i) for i in range(2)]
            a0 = take(1024); K3 = [Hb[:, a0 + i * 512: a0 + (i + 1) * 512].rearrange("p (q d) -> p q d", d=128) for i in range(2)]
            K3t = [Tl(K3[i], "K3_%d" % i) for i in range(2)]
            a0 = take(512); STf = Tl(Hb[:, a0:a0 + 512].bitcast(F32), "STf")
            a0 = take(256); STb = Tl(Hb[:, a0:a0 + 256], "STb")
            a0 = take(512); OGh = [Tl(Hb[:, a0 + i * 256: a0 + (i + 1) * 256], "OGh%d" % i) for i in range(2)]
            a0 = take(512); OGT = [Tl(Hb[:, a0 + i * 256: a0 + (i + 1) * 256].rearrange("p (c t) -> p c t", t=128), "OGT%d" % i) for i in range(2)]
            a0 = take(512); S12 = [Tl(Hb[:, a0 + i * 256: a0 + (i + 1) * 256], "S12_%d" % i) for i in range(2)]
            a0 = take(2048); glrA = Tl(Hb[0:16, a0:a0 + 2048], "glrA")
            a0 = take(512); k3T = Tl(Hb[:, a0:a0 + 512], "k3T")
            assert o[0] <= 16384
            PR = [SCb[:, i * 2048:(i + 1) * 2048].rearrange("p (k t) -> p k t", t=512) for i in range(2)]
            PRt = [Tl(PR[i], "PR%d" % i) for i in range(2)]
            LT = Tl(SCb[:, 4096:5120].bitcast(F32), "LT")
            EP = Tl(SCb[:, 5120:6144].bitcast(F32), "EP")
            EM = Tl(SCb[:, 6144:7168].bitcast(F32), "EM")
            _glrT, wgu, bgt, nbg, ngt, eend, ssq, rstd, _k3T, gmask, rmask, wglr = G_SM
            ssq_l = [Tl(gsm_f[:, 32 + i:33 + i], "ssq%d" % i, True) for i in range(4)]
            sq_l = [Tl(gsm_f[:, 36 + i:37 + i], "sq%d" % i, True) for i in range(4)]
            rs_l = [Tl(gsm_f[:, 40 + i:41 + i], "rs%d" % i, True) for i in range(4)]
            if RSTD_LATE:
                for tt in range(NT):
                    P.act(lambda e, tt=tt: e.activation(out=X[tt][:], in_=X[tt][:], func=AF.Copy, scale=ALPHA), reads=[X[tt]], writes=[X[tt]])
            fence("H", Vht + Ght + K3t + [STf, STb] + OGh + OGT + S12 + [glrA, k3T])
            fence("SC", PRt + [LT, EP, EM])
            P.dma("pool", wgu[:], d_glagu[j], writes=[wgu])
            P.dma("sp", bgt[:], d_glab[j], writes=[bgt])
            P.dma("sp", ngt[:], d_glang[j], writes=[ngt])
            P.dma("pool", wglr[:], win[:, 3072:3088].rearrange("(kc k) n -> k kc n", k=128), writes=[wglr])
            P.dve(lambda e: e.tensor_scalar(out=nbg[:], in0=bgt[:], scalar1=-1.0, scalar2=None, op0=ALU.mult), reads=[bgt], writes=[nbg])
            load_ln(layer, 0)
            qs = 128.0 ** -0.5
            for g in range(4):
                b = ps1()
                for kc in range(KC):
                    P.pe(lambda e, kc=kc, b=b, g=g: e.matmul(PSt[0:16, b, :], lhsT=wglr[:, kc, :], rhs=XTb[:, kc, g * 512:(g + 1) * 512],
                                                         start=(kc == 0), stop=(kc == KC - 1)), reads=[wglr] + XT[4 * g:4 * g + 4], writes=[PS[b]])
                P.act(lambda e, b=b, g=g: e.copy(out=glrA[:, g * 512:(g + 1) * 512], in_=PSt[0:16, b, :]), reads=[PS[b]], writes=[glrA])
            gxq = XtQ(2)

            for h in range(4):
                wq = wload(wview(win, h * 128, 128), [128, 8, 128])
                wk = wload(wview(win, 512 + h * 128, 128), [128, 8, 128])
                wv = wload(wview(win, 1024 + h * 256, 256), [128, 8, 256])
                wr = wload(wview(win, 2048 + h * 256, 256), [128, 8, 256])
                wo = wload(d_glawo[j][h * 256:(h + 1) * 256, :].rearrange("(c k) n -> k c n", k=128), [128, 2, 1024])
                for c in range(2):
                    P.dve(lambda e, c=c, wo=wo: e.tensor_scalar(out=wo[1][:, c, :], in0=wo[1][:, c, :], scalar1=ngt[:, c:c + 1], scalar2=None, op0=ALU.mult),
                          reads=[wo[0], ngt], writes=[wo[0]])
                P.dve(lambda e: e.memset(STf[:], 0.0), writes=[STf])
                P.dve(lambda e: e.memset(STb[:], 0.0), writes=[STb])

                def prepA(g, h=h):
                    gb = g % 2
                    b = ps1()
                    P.pe(lambda e, b=b, h=h, g=g: e.matmul(PSt[:, b, :], lhsT=wgu[:, h * 128:(h + 1) * 128], rhs=glrA[:, g * 512:(g + 1) * 512], start=True, stop=True),
                         reads=[wgu, glrA], writes=[PS[b]])
                    P.act(lambda e, b=b, h=h: e.activation(out=LT[:], in_=PSt[:, b, :], func=AF.Exp, scale=-1.0, bias=nbg[:, h:h + 1]),
                          reads=[PS[b], nbg], writes=[LT])
                    P.act(lambda e: e.activation(out=LT[:], in_=LT[:], func=AF.Ln, bias=1.0), reads=[LT], writes=[LT])
                    P.dve(lambda e: e.tensor_scalar(out=LT[:], in0=LT[:], scalar1=-1.0 / 16.0, scalar2=None, op0=ALU.mult), reads=[LT], writes=[LT])
                    P.dve(lambda e: e.tensor_tensor_scan(out=LT[:], data0=rmask[:], data1=LT[:], initial=0.0, op0=ALU.mult, op1=ALU.add),
                          reads=[LT, rmask], writes=[LT])
                    P.act(lambda e: e.activation(out=EP[:], in_=LT[:], func=AF.Exp), reads=[LT], writes=[EP])
                    P.act(lambda e: e.activation(out=EM[:], in_=LT[:], func=AF.Exp, scale=-1.0), reads=[LT], writes=[EM])
                    P.dve(lambda e, gb=gb: e.tensor_copy(out=eend[:, gb * 4:(gb + 1) * 4], in_=EP[:].rearrange("p (q t) -> p q t", t=128)[:, :, 127]),
                          reads=[EP], writes=[eend])

                def prepB(g, h=h, wq=wq, wk=wk):
                    gb = g % 2
                    xts = XT[4 * g:4 * g + 4]
                    pr, prt = PR[gb], PRt[gb]
                    bq = ps1()
                    for kc in range(KC):
                        P.pe(lambda e, kc=kc, bq=bq, g=g, wq=wq: e.matmul(PSt[:, bq, :], lhsT=wq[1][:, kc, :], rhs=XTb[:, kc, g * 512:(g + 1) * 512],
                                                                      start=(kc == 0), stop=(kc == KC - 1)), reads=[wq[0]] + xts, writes=[PS[bq]])
                    bk = ps1()
                    for kc in range(KC):
                        P.pe(lambda e, kc=kc, bk=bk, g=g, wk=wk: e.matmul(PSt[:, bk, :], lhsT=wk[1][:, kc, :], rhs=XTb[:, kc, g * 512:(g + 1) * 512],
                                                                      start=(kc == 0), stop=(kc == KC - 1)), reads=[wk[0]] + xts, writes=[PS[bk]])
                    P.dve(lambda e, bq=bq, pr=pr: e.scalar_tensor_tensor(out=pr[:, 0, :], in0=PSt[:, bq, :], scalar=qs, in1=EP[:], op0=ALU.mult, op1=ALU.mult),
                          reads=[PS[bq], EP], writes=[prt])
                    P.dve(lambda e, bq=bq, pr=pr: e.scalar_tensor_tensor(out=pr[:, 1, :], in0=PSt[:, bq, :], scalar=qs, in1=EM[:], op0=ALU.mult, op1=ALU.mult),
                          reads=[PS[bq], EM], writes=[prt])
                    P.dve(lambda e, bk=bk, pr=pr: e.tensor_tensor(out=pr[:, 2, :], in0=PSt[:, bk, :], in1=EM[:], op=ALU.mult), reads=[PS[bk], EM], writes=[prt])
                    P.dve(lambda e, bk=bk, pr=pr: e.tensor_tensor(out=pr[:, 3, :], in0=PSt[:, bk, :], in1=EP[:], op=ALU.mult), reads=[PS[bk], EP], writes=[prt])
                    for q in range(4):
                        P.dve(lambda e, q=q, pr=pr, gb=gb: e.tensor_scalar(out=k3T[:, q * 128:(q + 1) * 128], in0=pr[:, 2, q * 128:(q + 1) * 128],
                                                                      scalar1=eend[:, gb * 4 + q:gb * 4 + q + 1], scalar2=None, op0=ALU.mult),
                              reads=[prt, eend], writes=[k3T])

                def prep_vr(g, t4, h=h, wv=wv, wr=wr):
                    gb = g % 2
                    vh, vht, gh, ght = Vh[gb], Vht[gb], Gh[gb], Ght[gb]
                    if True:
                        tt = 4 * g + t4
                        b = ps1()
                        for sec, w_ in ((0, wv), (1, wr)):
                            for kc in range(KC):
                                P.pe(lambda e, kc=kc, b=b, tt=tt, sec=sec, w_=w_: e.matmul(
                                    PSt[:, b, sec * 256:(sec + 1) * 256], lhsT=XTb[:, kc, tt * 128:(tt + 1) * 128], rhs=w_[1][:, kc, :],
                                    start=(kc == 0), stop=(kc == KC - 1)), reads=[XT[tt], w_[0]], writes=[PS[b]])
                        P.act(lambda e, b=b, t4=t4, vh=vh: e.copy(out=vh[:, t4, :], in_=PSt[:, b, 0:256]), reads=[PS[b]], writes=[vht])
                        P.act(lambda e, b=b, t4=t4, gh=gh: e.activation(out=gh[:, t4, :], in_=PSt[:, b, 256:512], func=AF.Silu), reads=[PS[b]], writes=[ght])

                def prep2(g):
                    k3, k3t = K3[g % 2], K3t[g % 2]
                    bt = ps1()
                    pv = PSt[:, bt, :].bitcast(BF16)
                    for q in range(4):
                        P.pe(lambda e, q=q, pv=pv: e.transpose(out=pv[:, q * 128:(q + 1) * 128], in_=k3T[:, q * 128:(q + 1) * 128], identity=ident[:]),
                             reads=[k3T, ident], writes=[PS[bt]])
                    P.act(lambda e, pv=pv, k3=k3: e.copy(out=k3[:, :, :], in_=pv[:, 0:512].rearrange("p (q d) -> p q d", d=128)), reads=[PS[bt]], writes=[k3t])

                def stageA(p):
                    g, q = p // 4, p % 4
                    pr, prt = PR[g % 2], PRt[g % 2]
                    s12 = S12[p % 2]
                    bab = ps1()
                    P.pe(lambda e, q=q, bab=bab, pr=pr: e.matmul(PSt[:, bab, 0:128], lhsT=pr[:, 2, q * 128:(q + 1) * 128], rhs=pr[:, 0, q * 128:(q + 1) * 128],
                                                             start=True, stop=True), reads=[prt], writes=[PS[bab]])
                    P.pe(lambda e, q=q, bab=bab, pr=pr: e.matmul(PSt[:, bab, 128:256], lhsT=pr[:, 3, q * 128:(q + 1) * 128], rhs=pr[:, 1, q * 128:(q + 1) * 128],
                                                             start=True, stop=True), reads=[prt], writes=[PS[bab]])
                    P.dve(lambda e, bab=bab, s12=s12: e.tensor_tensor(out=s12[:], in0=PSt[:, bab, 0:256], in1=gmask[:], op=ALU.mult),
                          reads=[PS[bab], gmask], writes=[s12])

                def stageB(p, h=h):
                    g, q = p // 4, p % 4
                    gb = g % 2
                    pr, prt, vh, vht, gh, ght, k3, k3t = PR[gb], PRt[gb], Vh[gb], Vht[gb], Gh[gb], Ght[gb], K3[gb], K3t[gb]
                    s12 = S12[p % 2]
                    og = OGh[p % 2]
                    bo = ps1()
                    oreg = PSt[:, bo, 0:256]
                    vq = vh[:, q, :]
                    P.pe(lambda e, s12=s12, oreg=oreg, vq=vq: e.matmul(oreg, lhsT=s12[:, 0:128], rhs=vq, start=True, stop=False), reads=[s12, vht], writes=[PS[bo]])
                    P.pe(lambda e, s12=s12, oreg=oreg, vq=vq: e.matmul(oreg, lhsT=s12[:, 128:256], rhs=vq, start=False, stop=False), reads=[s12, vht], writes=[PS[bo]])
                    P.pe(lambda e, q=q, oreg=oreg, pr=pr: e.matmul(oreg, lhsT=pr[:, 0, q * 128:(q + 1) * 128], rhs=STb[:], start=False, stop=True),
                         reads=[prt, STb], writes=[PS[bo]])
                    sreg = PSt[:, bo, 256:512]
                    P.pe(lambda e, q=q, sreg=sreg, vq=vq, k3=k3: e.matmul(sreg, lhsT=k3[:, q, :], rhs=vq, start=True, stop=True), reads=[k3t, vht], writes=[PS[bo]])
                    if ST_DVE == 2:
                        P.dve(lambda e, q=q, gb=gb, sreg=sreg: e.scalar_tensor_tensor(out=STb[:], in0=STf[:], scalar=eend[:, gb * 4 + q:gb * 4 + q + 1], in1=sreg,
                                                                                 op0=ALU.mult, op1=ALU.add), reads=[STf, eend, PS[bo]], writes=[STb])
                    P.dve(lambda e, q=q, gb=gb, sreg=sreg: e.scalar_tensor_tensor(out=STf[:], in0=STf[:], scalar=eend[:, gb * 4 + q:gb * 4 + q + 1], in1=sreg,
                                                                             op0=ALU.mult, op1=ALU.add), reads=[STf, eend, PS[bo]], writes=[STf])
                    if ST_DVE == 1:
                        P.dve(lambda e: e.tensor_copy(out=STb[:], in_=STf[:]), reads=[STf], writes=[STb])
                    elif ST_DVE == 0:
                        P.act(lambda e: e.copy(out=STb[:], in_=STf[:]), reads=[STf], writes=[STb])
                    if RSTD_LATE:
                        P.dve(lambda e, oreg=oreg, og=og, gh=gh, q=q: e.tensor_tensor(out=og[:], in0=oreg, in1=gh[:, q, :], op=ALU.mult),
                              reads=[PS[bo], ght], writes=[og])
                    jk = nstg()
                    sq_, sr_, rs_ = ssq_l[p % 4], sq_l[p % 4], rs_l[p % 4]
                    P.act(lambda e, oreg=oreg, jk=jk, sq_=sq_: e.activation(out=jk[:, 0:256], in_=oreg, func=AF.Square, accum_out=sq_[:, 0:1]),
                          reads=[PS[bo]], writes=[jk, sq_])
                    if not RSTD_LATE:
                        P.act(lambda e, sq_=sq_, sr_=sr_: e.activation(out=sr_[:, 0:1], in_=sq_[:, 0:1], func=AF.Sqrt, scale=1.0 / 256.0, bias=RMS_EPS),
                              reads=[sq_], writes=[sr_])
                    if not RSTD_LATE:
                        P.dve(lambda e, sr_=sr_, rs_=rs_: e.reciprocal(out=rs_[:, 0:1], in_=sr_[:, 0:1]), reads=[sr_], writes=[rs_])
                        P.dve(lambda e, oreg=oreg, og=og, gh=gh, q=q, rs_=rs_: e.scalar_tensor_tensor(out=og[:], in0=oreg, scalar=rs_[:, 0:1], in1=gh[:, q, :],
                                                                                               op0=ALU.mult, op1=ALU.mult),
                              reads=[PS[bo], rs_, ght], writes=[og])

                def stageC1(p):
                    og = OGh[p % 2]
                    ogt = OGT[p % 2]
                    if RSTD_LATE:
                        sq_, sr_, rs_ = ssq_l[p % 4], sq_l[p % 4], rs_l[p % 4]
                        P.dve(lambda e, sq_=sq_, sr_=sr_: e.tensor_scalar(out=sr_[:, 0:1], in0=sq_[:, 0:1], scalar1=1.0 / 256.0, scalar2=RMS_EPS,
                                                                       op0=ALU.mult, op1=ALU.add), reads=[sq_], writes=[sr_])
                        P.op("pool", lambda e, sr_=sr_, rs_=rs_: e.tensor_tensor(out=rs_[:, 0:1], in0=sr_[:, 0:1], in1=nhalf[:, 0:1], op=ALU.pow),
                             reads=[sr_, nhalf], writes=[rs_])
                    bt = ps1()
                    pv = PSt[:, bt, :].bitcast(BF16)
                    for c in range(2):
                        P.pe(lambda e, c=c, pv=pv, og=og: e.transpose(out=pv[:, c * 128:(c + 1) * 128], in_=og[:, c * 128:(c + 1) * 128], identity=ident[:]),
                             reads=[og, ident], writes=[PS[bt]])
                    P.act(lambda e, pv=pv, ogt=ogt: e.copy(out=ogt[:, :, :], in_=pv[:, 0:256].rearrange("p (a b) -> p a b", b=128)), reads=[PS[bt]], writes=[ogt])

                def stageC2(p, h=h, wo=wo):
                    tt = p
                    ogt = OGT[p % 2]
                    rs_ = rs_l[p % 4]
                    b2 = ps2()
                    for nh in range(2):
                        for c in range(2):
                            P.pe(lambda e, c=c, nh=nh, b2=b2, ogt=ogt, wo=wo: e.matmul(PSt[:, b2 + nh, :], lhsT=ogt[:, c, :], rhs=wo[1][:, c, nh * 512:(nh + 1) * 512],
                                                                                start=(c == 0), stop=(c == 1)), reads=[ogt, wo[0]], writes=[PS[b2 + nh]])
                    pin = PSt[:, b2:b2 + 2, :].rearrange("p a b -> p (a b)")
                    sr_ = sq_l[p % 4]
                    if RSTD_LATE:
                        P.dve(lambda e, tt=tt, pin=pin, rs_=rs_: e.scalar_tensor_tensor(out=X[tt][:], in0=pin, scalar=rs_[:, 0:1], in1=X[tt][:], op0=ALU.mult, op1=ALU.add),
                              reads=[X[tt], PS[b2], PS[b2 + 1], rs_], writes=[X[tt]])
                    elif h == 0:
                        P.dve(lambda e, tt=tt, pin=pin: e.scalar_tensor_tensor(out=X[tt][:], in0=X[tt][:], scalar=ALPHA, in1=pin, op0=ALU.mult, op1=ALU.add),
                              reads=[X[tt], PS[b2], PS[b2 + 1]], writes=[X[tt]])
                    else:
                        P.dve(lambda e, tt=tt, pin=pin: e.tensor_tensor(out=X[tt][:], in0=X[tt][:], in1=pin, op=ALU.add),
                              reads=[X[tt], PS[b2], PS[b2 + 1]], writes=[X[tt]])
                    if h == 3:
                        ln_inplace(tt)
                        gxq.push(tt)

                prepA(0)
                for t4 in range(4):
                    prep_vr(0, t4)
                prepB(0)
                prep2(0)
                stageA(0)
                for p in range(16):
                    gn = p // 4 + 1
                    if p + 4 < 16:
                        prep_vr(gn, p % 4)
                    if p + 1 < 16:
                        stageA(p + 1)
                    stageB(p)
                    if p >= 1:
                        stageC1(p - 1)
                    if p >= 2:
                        stageC2(p - 2)
                    if gn < 4:
                        if p % 4 == 0:
                            prepA(gn)
                        elif p % 4 == 1:
                            prepB(gn)
                        elif p % 4 == 2:
                            prep2(gn)
                stageC1(15)
                stageC2(14)
                stageC2(15)
            gxq.flush()

        def mla_mixer(j, layer):
            XTf = XTb[:, :, :].rearrange("p a b -> p (a b)")
            QR = XTf[:, 0:8192].rearrange("p (a t) -> p a t", t=S)
            QRt = Tl(QR, "QR")
            OA = XTf[:, 8192:16384].rearrange("p (t n) -> p t n", n=512)
            OAt = Tl(OA, "OA")
            cqT = Hb[:, 0:4096].rearrange("p (c t) -> p c t", t=S)
            cqTt = Tl(cqT, "cqT")
            ckT = Hb[:, 4096:8192].rearrange("p (c t) -> p c t", t=S)
            ckTt = Tl(ckT, "ckT")
            krT = Hb[:, 8192:10240]
            krTt = Tl(krT, "krT")
            VH = [Hb[:, 10240 + i * 2080: 10240 + (i + 1) * 2080].rearrange("p (t n) -> p t n", n=130) for i in range(2)]
            VHt = [Tl(VH[i], "VH%d" % i) for i in range(2)]
            CN = Tl(Hb[:, 14400:14912], "CN")
            KR = Tl(Hb[:, 14912:15040], "KR")
            QRS = Tl(Hb[:, 15040:15552], "QRS")
            OAT = Hb[:, 15552:16064].rearrange("p (h t) -> p h t", t=128)
            OATt = Tl(OAT, "OAT")
            knT = Tl(SCb[:, 0:2048], "knT")
            qnT = Tl(SCb[:, 2048:4096], "qnT")
            PB = [Tl(SCb[:, 4096 + i * 512: 4096 + (i + 1) * 512], "PB%d" % i) for i in range(4)]
            COS = Tl(SCb[:, 6144:7168].bitcast(F32), "COS")
            SIN = Tl(SCb[:, 7168:8192].bitcast(F32), "SIN")
            gbc = Tl(SCb[:, 8192:9216].bitcast(F32), "gbc")
            ANG = Tl(SCb[:, 9216:10240].bitcast(F32), "ANG")
            NF = Tl(SCb[:, 10240:11264].bitcast(F32), "NF")
            NI = SCb[:, 10240:11264].bitcast(I32)
            T1 = ANG
            T2 = NF
            fence("H", [cqTt, ckTt, krTt] + VHt + [CN, KR, QRS, OATt])
            fence("SC", [knT, qnT] + PB + [COS, SIN, gbc, ANG, NF])
            msm = G_SM[6]
            mrs = G_SM[7]
            posi, posf, invf, rec = MLA_SM
            allXT = list(XT)
            scale = 192.0 ** -0.5
            PI = 3.1415925
            TWO_PI = 6.283185307179586
            C1 = 6.28125
            C2 = TWO_PI - C1
            cos3 = COS[:].rearrange("p (t i) -> p t i", i=32)
            sin3 = SIN[:].rearrange("p (t i) -> p t i", i=32)
            P.dma("sp", posi[:], d_pos[:, :], writes=[posi])
            P.dma("sp", invf[:], d_invf[:, :], writes=[invf])
            P.dma("sp", gbc[:], d_mlan[j].partition_broadcast(128), writes=[gbc])
            P.dve(lambda e: e.tensor_copy(out=posf[:], in_=posi[:]), reads=[posi], writes=[posf])
            for tt in range(NT):
                P.dve(lambda e, tt=tt: e.tensor_scalar(out=ANG[:, tt * 32:(tt + 1) * 32], in0=invf[:], scalar1=posf[:, tt:tt + 1], scalar2=None, op0=ALU.mult),
                      reads=[invf, posf], writes=[ANG])
            P.dve(lambda e: e.tensor_scalar(out=NI, in0=ANG[:], scalar1=1.0 / TWO_PI, scalar2=None, op0=ALU.mult), reads=[ANG], writes=[NF])
            P.dve(lambda e: e.tensor_copy(out=NF[:], in_=NI), reads=[NF], writes=[NF])
            P.dve(lambda e: e.scalar_tensor_tensor(out=ANG[:], in0=NF[:], scalar=-C1, in1=ANG[:], op0=ALU.mult, op1=ALU.add), reads=[NF, ANG], writes=[ANG])
            P.dve(lambda e: e.scalar_tensor_tensor(out=ANG[:], in0=NF[:], scalar=-C2, in1=ANG[:], op0=ALU.mult, op1=ALU.add), reads=[NF, ANG], writes=[ANG])
            P.dve(lambda e: e.tensor_scalar(out=ANG[:], in0=ANG[:], scalar1=-PI, scalar2=PI, op0=ALU.max, op1=ALU.min), reads=[ANG], writes=[ANG])
            P.act(lambda e: e.activation(out=SIN[:], in_=ANG[:], func=AF.Sin), reads=[ANG], writes=[SIN])
            P.dve(lambda e: e.tensor_scalar(out=ANG[:], in0=ANG[:], scalar1=TWO_PI / 4, scalar2=None, op0=ALU.add), reads=[ANG], writes=[ANG])
            P.dve(lambda e: e.tensor_scalar(out=NF[:], in0=ANG[:], scalar1=PI, scalar2=None, op0=ALU.is_gt), reads=[ANG], writes=[NF])
            P.dve(lambda e: e.scalar_tensor_tensor(out=ANG[:], in0=NF[:], scalar=-TWO_PI, in1=ANG[:], op0=ALU.mult, op1=ALU.add), reads=[NF, ANG], writes=[ANG])
            P.dve(lambda e: e.tensor_scalar(out=ANG[:], in0=ANG[:], scalar1=-PI, scalar2=PI, op0=ALU.max, op1=ALU.min), reads=[ANG], writes=[ANG])
            P.act(lambda e: e.activation(out=COS[:], in_=ANG[:], func=AF.Sin), reads=[ANG], writes=[COS])
            load_ln(layer, 0)
            if dbg == ("cos", layer):
                P.dma("sp", d_dbg[0:128, 0:512], COS[:], reads=[COS])
                P.dma("sp", d_dbg[0:128, 512:1024], SIN[:], reads=[SIN])

            def rope(xa, xb, o1, o2, cb, sb_, reads, wt):
                n = 1
                for s_ in xa.shape[1:]:
                    n *= s_
                t1 = T1[:, 0:n]
                t2 = T2[:, 0:n]
                if len(xa.shape) == 3:
                    t1 = t1.rearrange("p (a b) -> p a b", b=xa.shape[2])
                    t2 = t2.rearrange("p (a b) -> p a b", b=xa.shape[2])
                P.dve(lambda e: e.tensor_tensor(out=t1, in0=xa, in1=cb, op=ALU.mult), reads=reads + [COS], writes=[T1])
                P.dve(lambda e: e.tensor_tensor(out=t2, in0=xb, in1=sb_, op=ALU.mult), reads=reads + [SIN], writes=[T2])
                P.dve(lambda e: e.tensor_tensor(out=o1, in0=t1, in1=t2, op=ALU.subtract), reads=[T1, T2], writes=[wt])
                P.dve(lambda e: e.tensor_tensor(out=t1, in0=xb, in1=cb, op=ALU.mult), reads=reads + [COS], writes=[T1])
                P.dve(lambda e: e.tensor_tensor(out=t2, in0=xa, in1=sb_, op=ALU.mult), reads=reads + [SIN], writes=[T2])
                P.dve(lambda e: e.tensor_tensor(out=o2, in0=t1, in1=t2, op=ALU.add), reads=[T1, T2], writes=[wt])

            wA = wload(wview(d_mlaw[j], 0, 512), [128, 8, 512])
            wB = wload(wview(d_mlaw[j], 512, 64), [128, 8, 64])
            CNs = [CN, QRS]
            KR2 = Tl(Hb[:, 15552:15680], "KR2")
            KR2.rd = list(OATt.rd)
            KRs = [KR, KR2]
            ss_l = [Tl(mla_f[:, 52 + 2 * k:54 + 2 * k], "mss%d" % k, True) for k in range(2)]
            tt_l = [Tl(mla_f[:, 56 + 2 * k:58 + 2 * k], "mtt%d" % k, True) for k in range(2)]
            rs_l = [Tl(mla_f[:, 60 + 2 * k:62 + 2 * k], "mrs%d" % k, True) for k in range(2)]

            def c_s1(tt):
                k = tt % 2
                cn, kr, ss, t_, rs = CNs[k], KRs[k], ss_l[k], tt_l[k], rs_l[k]
                b1 = ps1()
                for kc in range(KC):
                    P.pe(lambda e, kc=kc, b1=b1, tt=tt: e.matmul(PSt[:, b1, :], lhsT=XTb[:, kc, tt * 128:(tt + 1) * 128], rhs=wA[1][:, kc, :],
                                                             start=(kc == 0), stop=(kc == KC - 1)), reads=[XT[tt], wA[0]], writes=[PS[b1]])
                b2 = ps1()
                for kc in range(KC):
                    P.pe(lambda e, kc=kc, b2=b2, tt=tt: e.matmul(PSt[:, b2, 0:64], lhsT=XTb[:, kc, tt * 128:(tt + 1) * 128], rhs=wB[1][:, kc, :],
                                                             start=(kc == 0), stop=(kc == KC - 1)), reads=[XT[tt], wB[0]], writes=[PS[b2]])
                jk = nstg()
                for c in range(2):
                    P.act(lambda e, c=c, b1=b1, jk=jk, ss=ss: e.activation(out=jk[:, 0:256], in_=PSt[:, b1, c * 256:(c + 1) * 256], func=AF.Square,
                                                                       accum_out=ss[:, c:c + 1]), reads=[PS[b1]], writes=[jk, ss])
                P.dve(lambda e, ss=ss, t_=t_: e.tensor_scalar(out=t_[:, 0:2], in0=ss[:, 0:2], scalar1=1.0 / 256.0, scalar2=RMS_EPS, op0=ALU.mult, op1=ALU.add),
                      reads=[ss], writes=[t_])
                P.op("pool", lambda e, t_=t_, rs=rs: e.tensor_tensor(out=rs[:, 0:2], in0=t_[:, 0:2], in1=nhalf[:, 0:2], op=ALU.pow), reads=[t_, nhalf], writes=[rs])
                rope(PSt[:, b2, 0:32], PSt[:, b2, 32:64], kr[:, 0:32], kr[:, 32:64], cos3[:, tt, :], sin3[:, tt, :], [PS[b2]], kr)
                P.act(lambda e, kr=kr: e.copy(out=kr[:, 64:128], in_=kr[:, 0:64]), reads=[kr], writes=[kr])
                for c in range(2):
                    P.dve(lambda e, c=c, b1=b1, cn=cn, rs=rs: e.scalar_tensor_tensor(out=cn[:, c * 256:(c + 1) * 256], in0=PSt[:, b1, c * 256:(c + 1) * 256],
                                                                                scalar=rs[:, c:c + 1], in1=gbc[:, c * 256:(c + 1) * 256], op0=ALU.mult, op1=ALU.mult),
                          reads=[PS[b1], rs, gbc], writes=[cn])

            def c_s2(tt):
                k = tt % 2
                cn, kr = CNs[k], KRs[k]
                bt = ps1()
                pv = PSt[:, bt, :].bitcast(BF16)
                for c in range(4):
                    P.pe(lambda e, c=c, pv=pv, cn=cn: e.transpose(out=pv[:, c * 128:(c + 1) * 128], in_=cn[:, c * 128:(c + 1) * 128], identity=ident[:]),
                         reads=[cn, ident], writes=[PS[bt]])
                P.pe(lambda e, pv=pv, kr=kr: e.transpose(out=pv[:, 512:640], in_=kr[:], identity=ident[:]), reads=[kr, ident], writes=[PS[bt]])
                P.act(lambda e, pv=pv, tt=tt: e.copy(out=cqT[:, :, tt * 128:(tt + 1) * 128], in_=pv[:, 0:256].rearrange("p (a b) -> p a b", b=128)),
                      reads=[PS[bt]], writes=[cqTt])
                P.act(lambda e, pv=pv, tt=tt: e.copy(out=ckT[:, :, tt * 128:(tt + 1) * 128], in_=pv[:, 256:512].rearrange("p (a b) -> p a b", b=128)),
                      reads=[PS[bt]], writes=[ckTt])
                P.act(lambda e, pv=pv, tt=tt: e.copy(out=krT[:, tt * 128:(tt + 1) * 128], in_=pv[:, 512:640]), reads=[PS[bt]], writes=[krTt])

            c_s1(0)
            for tt in range(NT):
                if tt + 1 < NT:
                    c_s1(tt + 1)
                c_s2(tt)
            OATt.rd = OATt.rd + ([KR2.lw] if KR2.lw is not None else []) + KR2.rd
            prior = []
            for t in XT:
                if t.lw is not None:
                    prior.append(t.lw)
                prior.extend(t.rd)
            QRt.rd = list(prior)
            OAt.rd = list(prior)
            wqp, wqrv = walloc(1024)
            wqrv = wqrv.rearrange("p (a n) -> p a n", n=512)
            for rc in range(2):
                P.dma("pool", wqrv[:, rc, :].rearrange("p (h c) -> p h c", c=64),
                      d_mlauq[j][rc * 128:(rc + 1) * 128, :].rearrange("r (h c) -> r h c", c=192)[:, :, 128:192], writes=[wqp])
            wqr = (wqp, wqrv)
            QRSs = [QRS, CN]

            def q_s1(tt):
                qrs = QRSs[tt % 2]
                b1 = ps1()
                for rc in range(2):
                    P.pe(lambda e, rc=rc, b1=b1, tt=tt: e.matmul(PSt[:, b1, :], lhsT=cqT[:, rc, tt * 128:(tt + 1) * 128], rhs=wqrv[:, rc, :],
                                                             start=(rc == 0), stop=(rc == 1)), reads=[cqTt, wqr[0]], writes=[PS[b1]])
                p3 = PSt[:, b1, :].rearrange("p (h c) -> p h c", c=64)
                q3 = qrs[:].rearrange("p (h c) -> p h c", c=64)
                cb = cos3[:, tt, :].unsqueeze(1).to_broadcast([128, 8, 32])
                sb_ = sin3[:, tt, :].unsqueeze(1).to_broadcast([128, 8, 32])
                rope(p3[:, :, 0:32], p3[:, :, 32:64], q3[:, :, 0:32], q3[:, :, 32:64], cb, sb_, [PS[b1]], qrs)

            def q_s2(tt):
                qrs = QRSs[tt % 2]
                bt = ps1()
                pv = PSt[:, bt, :].bitcast(BF16)
                for c in range(4):
                    P.pe(lambda e, c=c, pv=pv, qrs=qrs: e.transpose(out=pv[:, c * 128:(c + 1) * 128], in_=qrs[:, c * 128:(c + 1) * 128], identity=ident[:]),
                         reads=[qrs, ident], writes=[PS[bt]])
                P.act(lambda e, pv=pv, tt=tt: e.copy(out=QR[:, :, tt * 128:(tt + 1) * 128], in_=pv[:, 0:512].rearrange("p (a b) -> p a b", b=128)),
                      reads=[PS[bt]], writes=[QRt])

            q_s1(0)
            for tt in range(NT):
                if tt + 1 < NT:
                    q_s1(tt + 1)
                q_s2(tt)
            if dbg == ("qr", layer):
                P.dma("sp", d_dbg[0:128, :].bitcast(BF16), XTf[:, 0:2048], reads=[QRt] + allXT)
                P.dma("sp", d_dbg[128:256, :].bitcast(BF16), krT[:, :], reads=[krTt])
                P.dma("sp", d_dbg[256:384, :].bitcast(BF16), cqT[:, 0, :], reads=[cqTt])
            Z2 = SCb[:, 6144:8192]
            Z2t = Tl(Z2, "Z2")
            Z2t.rd = [x for x in (COS.lw, SIN.lw) if x is not None] + COS.rd + SIN.rd
            P.dve(lambda e: e.memset(Z2[0:64, :], 0.0), writes=[Z2t])
            P.act(lambda e: e.copy(out=Z2[64:128, :], in_=krT[64:128, :]), reads=[krTt], writes=[Z2t])
            P.dve(lambda e: e.memset(krT[64:128, :], 0.0), reads=[Z2t], writes=[krTt])
            for i in range(2):
                P.dve(lambda e, i=i: e.memset(VH[i][:, :, 128:130], 1.0), writes=[VHt[i]])
            for hf in range(2):
                wo = wload(d_mlawo[j][hf * 512:(hf + 1) * 512, :].rearrange("(h d) n -> d h n", d=128), [128, 4, 1024])
                for hl in range(4):
                    h = hf * 4 + hl
                    vh, vht = VH[h % 2], VHt[h % 2]
                    wkv = wload(wview(d_mlaukv[j], h * 256, 256), [128, 2, 256])
                    wqn = wload(wview(d_mlauq[j], h * 192, 128), [128, 2, 128])
                    for g in range(4):
                        b = ps1()
                        for rc in range(2):
                            P.pe(lambda e, rc=rc, b=b, g=g, wkv=wkv: e.matmul(PSt[:, b, :], lhsT=wkv[1][:, rc, 0:128], rhs=ckT[:, rc, g * 512:(g + 1) * 512],
                                                                 start=(rc == 0), stop=(rc == 1)), reads=[wkv[0], ckTt], writes=[PS[b]])
                        P.act(lambda e, b=b, g=g: e.copy(out=knT[:, g * 512:(g + 1) * 512], in_=PSt[:, b, :]), reads=[PS[b]], writes=[knT])
                        b = ps1()
                        for rc in range(2):
                            P.pe(lambda e, rc=rc, b=b, g=g, wqn=wqn: e.matmul(PSt[:, b, :], lhsT=wqn[1][:, rc, :], rhs=cqT[:, rc, g * 512:(g + 1) * 512],
                                                                 start=(rc == 0), stop=(rc == 1)), reads=[wqn[0], cqTt], writes=[PS[b]])
                        P.act(lambda e, b=b, g=g: e.copy(out=qnT[:, g * 512:(g + 1) * 512], in_=PSt[:, b, :]), reads=[PS[b]], writes=[qnT])
                        b = ps1()
                        for t4 in range(4):
                            tt = 4 * g + t4
                            for rc in range(2):
                                P.pe(lambda e, rc=rc, b=b, tt=tt, t4=t4, wkv=wkv: e.matmul(PSt[:, b, t4 * 128:(t4 + 1) * 128], lhsT=ckT[:, rc, tt * 128:(tt + 1) * 128],
                                                                              rhs=wkv[1][:, rc, 128:256], start=(rc == 0), stop=(rc == 1)),
                                     reads=[wkv[0], ckTt], writes=[PS[b]])
                        P.act(lambda e, b=b, g=g, vh=vh: e.copy(out=vh[:, 4 * g:4 * g + 4, 0:128], in_=PSt[:, b, :].rearrange("p (a b) -> p a b", b=128)),
                              reads=[PS[b]], writes=[vht])
                    pb_ = (h % 2) * 64
                    hp = h // 2
                    for g in range(4):
                        ob = [ps1() for _ in range(4)]
                        C.resv = set(ob)
                        njk = 4 * g + 4

                        def emit_S(jk_, g=g):
                            n0 = max(0, jk_ - 4 * g) * 128
                            N = 512 - n0
                            bS = ps1()
                            P.pe(lambda e, bS=bS, jk_=jk_, g=g, n0=n0, N=N: e.matmul(PSt[:, bS, 0:N], lhsT=knT[:, jk_ * 128:(jk_ + 1) * 128],
                                                                                rhs=qnT[:, g * 512 + n0:(g + 1) * 512], start=True, stop=False),
                                 reads=[knT, qnT], writes=[PS[bS]])
                            kz = krT if pb_ == 0 else Z2
                            P.pe(lambda e, bS=bS, jk_=jk_, g=g, n0=n0, N=N, kz=kz, hp=hp: e.matmul(
                                PSt[:, bS, 0:N], lhsT=kz[:, jk_ * 128:(jk_ + 1) * 128], rhs=QR[:, hp, g * 512 + n0:(g + 1) * 512],
                                start=False, stop=True), reads=[krTt, Z2t, QRt], writes=[PS[bS]])
                            return bS, n0, N

                        LA = 3
                        pendq = [emit_S(i_) for i_ in range(min(LA, njk))]
                        for jk_ in range(njk):
                            bS, n0, N = pendq.pop(0)
                            if jk_ + LA < njk:
                                pendq.append(emit_S(jk_ + LA))
                            pbuf = PB[jk_ % 4]
                            P.act(lambda e, bS=bS, N=N, pbuf=pbuf: e.activation(out=pbuf[:, 0:N], in_=PSt[:, bS, 0:N], func=AF.Exp, scale=scale),
                                  reads=[PS[bS]], writes=[pbuf])
                            if jk_ >= 4 * g:
                                P.dve(lambda e, pbuf=pbuf: e.memset(pbuf[64:128, 0:64], 0.0), writes=[pbuf])
                            for qi in range(n0 // 128, 4):
                                c0 = qi * 128 - n0
                                P.pe(lambda e, qi=qi, c0=c0, pbuf=pbuf, jk_=jk_, vh=vh, ob=ob: e.matmul(
                                    PSt[:, ob[qi], 0:129], lhsT=pbuf[:, c0:c0 + 128], rhs=vh[:, jk_, 0:129], start=(jk_ == 0), stop=(jk_ == 4 * g + qi)),
                                    reads=[pbuf, vht], writes=[PS[ob[qi]]])
                        for qi in range(4):
                            tt = 4 * g + qi
                            rq = rec[qi]
                            P.dve(lambda e, qi=qi, ob=ob, rq=rq: e.reciprocal(out=rq[:, 0:1], in_=PSt[:, ob[qi], 128:129]), reads=[PS[ob[qi]]], writes=[rq])
                            P.act(lambda e, qi=qi, ob=ob, tt=tt, hl=hl, rq=rq: e.activation(out=OA[:, tt, hl * 128:(hl + 1) * 128], in_=PSt[:, ob[qi], 0:128],
                                                                                        func=AF.Copy, scale=rq[:, 0:1]),
                                  reads=[PS[ob[qi]], rq], writes=[OAt])
                        C.resv = set()
                for tt in range(NT):
                    bt = ps1()
                    pv = PSt[:, bt, :].bitcast(BF16)
                    for c in range(4):
                        P.pe(lambda e, c=c, pv=pv, tt=tt: e.transpose(out=pv[:, c * 128:(c + 1) * 128], in_=OA[:, tt, c * 128:(c + 1) * 128], identity=ident[:]),
                             reads=[OAt, ident], writes=[PS[bt]])
                    P.act(lambda e, pv=pv: e.copy(out=OAT[:, :, :], in_=pv[:, 0:512].rearrange("p (a b) -> p a b", b=128)), reads=[PS[bt]], writes=[OATt])
                    b2 = ps2()
                    for nh in range(2):
                        for c in range(4):
                            P.pe(lambda e, c=c, nh=nh, b2=b2, wo=wo: e.matmul(PSt[:, b2 + nh, :], lhsT=OAT[:, c, :], rhs=wo[1][:, c, nh * 512:(nh + 1) * 512],
                                                                   start=(c == 0), stop=(c == 3)), reads=[OATt, wo[0]], writes=[PS[b2 + nh]])
                    pin = PSt[:, b2:b2 + 2, :].rearrange("p a b -> p (a b)")
                    if hf == 0:
                        P.dve(lambda e, tt=tt, pin=pin: e.scalar_tensor_tensor(out=X[tt][:], in0=X[tt][:], scalar=ALPHA, in1=pin, op0=ALU.mult, op1=ALU.add),
                              reads=[X[tt], PS[b2], PS[b2 + 1]], writes=[X[tt]])
                    else:
                        P.dve(lambda e, tt=tt, pin=pin: e.tensor_tensor(out=X[tt][:], in0=X[tt][:], in1=pin, op=ALU.add),
                              reads=[X[tt], PS[b2], PS[b2 + 1]], writes=[X[tt]])
                        if dbg == ("h", layer):
                            P.dma("sp", d_dbg[tt * 128:(tt + 1) * 128, :], X[tt][:], reads=[X[tt]])
                        ln_inplace(tt)
            tail = [x for x in (QRt.lw, OAt.lw) if x is not None] + QRt.rd + OAt.rd
            for t in XT:
                t.rd = t.rd + tail
            if dbg == ("oa", layer):
                P.dma("sp", d_dbg[0:128, :].bitcast(BF16), XTf[:, 8192:10240], reads=[OAt] + allXT)
                P.dma("sp", d_dbg[128:256, :].bitcast(BF16), SCb[:, 0:2048], reads=[knT])
                P.dma("sp", d_dbg[256:384, :].bitcast(BF16), SCb[:, 2048:4096], reads=[qnT])
                P.dma("sp", d_dbg[384:512, :].bitcast(BF16), Hb[:, 10240 + 2080:10240 + 2080 + 2048], reads=VHt)
            for tt in range(NT):
                to_xt(tt)

        for i_ in range(8):
            P.dma("sp", Xb[:, 2 * i_:2 * i_ + 2, :], d_x[256 * i_:256 * (i_ + 1), :].rearrange("(t q) d -> q t d", q=128), writes=X[2 * i_:2 * i_ + 2])
        for tt in range(NT):
            to_xt(tt)
        for li, layer in enumerate(layers):
            kind = layer % 3
            jj = layer // 3
            if kind == 0:
                gla_mixer(jj, layer)
            elif kind == 1:
                mla_mixer(jj, layer)
            else:
                conv_mixer(jj, layer)
            if dbg == ("a", layer):
                P.dma("sp", d_dbg.rearrange("(t q) d -> q t d", q=128), Xb[:, :, :], reads=X)
            down = mlp(layer)
            ple(layer, last=(li == len(layers) - 1), down=down)
        P.dma("sp", d_out[0:1024, :].rearrange("(t q) d -> q t d", q=128), Xb[:, 0:8, :], reads=X[0:8])
        P.dma("sp", d_out[1024:2048, :].rearrange("(t q) d -> q t d", q=128), Xb[:, 8:16, :], reads=X[8:16])
        P.emit(st)
    return nc


def host_consts():
    c = {}
    c["c_ident"] = np.eye(128, dtype=np.float32)
    s_ = np.arange(128)[:, None]
    t_ = np.arange(128)[None, :]
    mu = (t_ >= s_).astype(np.float32)
    ml = ((t_ < s_) & ((t_ // 64) == (s_ // 64))).astype(np.float32)
    c["c_gmask"] = np.concatenate([mu, ml], axis=1)
    rm = np.ones((128, 512), np.float32)
    rm[:, 0::128] = 0.0
    c["c_rmask"] = rm
    invf = (10000.0 ** (-np.arange(0, 32, dtype=np.float32) * np.float32(2.0 / 64))).astype(np.float32)
    c["c_invf"] = np.broadcast_to(invf[None, :], (128, 32)).copy()
    return c


def make_in_maps(inputs, xs):
    f = lambda a: np.ascontiguousarray(np.asarray(a, dtype=np.float32))
    shared = {
        "gla_w_in": f(inputs["gla_w_in"]), "gla_w_gate_up": f(inputs["gla_w_gate_up"]),
        "gla_b_gate_t": f(np.asarray(inputs["gla_b_gate"]).reshape(2, 4, 128).transpose(0, 2, 1)),
        "gla_norm_g_t": f(np.asarray(inputs["gla_norm_g"]).reshape(2, 2, 128).transpose(0, 2, 1)),
        "gla_w_out": f(inputs["gla_w_out"]),
        "mla_w_in": f(inputs["mla_w_in"]),
        "mla_norms": f(np.concatenate([np.asarray(inputs["mla_q_norm"]), np.asarray(inputs["mla_kv_norm"])], axis=1)),
        "mla_w_uq": f(inputs["mla_w_uq"]), "mla_w_ukv": f(inputs["mla_w_ukv"]), "mla_w_out": f(inputs["mla_w_out"]),
        "conv_w_in": f(inputs["conv_w_in"]),
        "conv_w_t": f(np.asarray(inputs["conv_w"]).reshape(1, 3, 8, 128).transpose(0, 3, 2, 1)),
        "conv_w_out": f(inputs["conv_w_out"]),
        "ln_g": f(inputs["ln_g"]), "ln_b": f(inputs["ln_b"]),
        "mlp_w1": f(inputs["mlp_w1"]), "mlp_w2": f(inputs["mlp_w2"]),
        "ple_w_gate": f(inputs["ple_w_gate"]), "ple_w_proj": f(inputs["ple_w_proj"]),
    }
    shared.update(host_consts())
    maps = []
    p = np.asarray(inputs["p"], dtype=np.float32)
    pos = np.asarray(inputs["positions"]).astype(np.int32)
    for c, xc in enumerate(xs):
        m = dict(shared)
        m["x"] = f(xc)
        m["p"] = np.ascontiguousarray(p[:, c])
        m["pos"] = np.ascontiguousarray(pos[c].reshape(NT, 128).T)
        maps.append(m)
    return maps


_NC_CACHE = {}


def kernel(**inputs):
    x = np.asarray(inputs["x"], dtype=np.float32)
    B = x.shape[0]
    key = "all"
    if key not in _NC_CACHE:
        _NC_CACHE[key] = build_nc([0, 1, 2, 3])
    nc = _NC_CACHE[key]
    maps = make_in_maps(inputs, [x[b] for b in range(B)])
    res = run_bass_kernel_spmd(nc, maps, core_ids=list(range(B)))
    return np.stack([np.asarray(r["out"], dtype=np.float32) for r in res.results], axis=0)
```

```python
import numpy as np
from contextlib import ExitStack
import concourse.bass as bass
import concourse.mybir as mybir
from concourse.bass_utils import run_bass_kernel_spmd

F32 = mybir.dt.float32
BF16 = mybir.dt.bfloat16
I32 = mybir.dt.int32
AF = mybir.ActivationFunctionType
ALU = mybir.AluOpType

ENGS = ("pe", "act", "dve", "pool", "sp")

S = 2048
D = 1024
NT = 16
KC = 8
DEPTH = 4
ALPHA = (2 * DEPTH) ** 0.25
LN_EPS = 1e-5
RMS_EPS = 1e-6
SAME_ENGINE_SYNC = False
RSTD_LATE = True
ST_DVE = 0
PREP_SPLIT = True
C_SKEW = True


class Tl:
    __slots__ = ("ap", "name", "lw", "rd", "small", "owner", "last")

    def __init__(self, ap, name="", small=False):
        self.ap = ap
        self.name = name
        self.lw = None
        self.rd = []
        self.small = small
        self.owner = None
        self.last = 0

    def __getitem__(self, k):
        return self.ap[k]


class WPiece(list):
    oid = None


class Ins:
    __slots__ = ("eng", "fn", "deps", "pos", "sig", "sigval", "dma", "dsem", "dval")

    def __init__(self, eng, fn, dma=False):
        self.eng = eng
        self.fn = fn
        self.deps = []
        self.pos = -1
        self.sig = False
        self.sigval = 0
        self.dma = dma
        self.dsem = -1
        self.dval = 0


class Prog:
    def __init__(self, nc, n_hw_sems=6, n_sw_sems=8):
        self.nc = nc
        self.streams = {e: [] for e in ENGS}
        self.n_hw = n_hw_sems
        self.n_dma_sems = n_hw_sems + n_sw_sems
        self.rr_hw = 0
        self.rr_sw = 0
        self.dma_last = [None] * self.n_dma_sems
        self.dma_cnt = [0] * self.n_dma_sems
        self.clock = 0

    @staticmethod
    def _flat(lst):
        out = []
        for t in lst:
            if isinstance(t, (list, tuple)):
                oid = getattr(t, "oid", None)
                for u in t:
                    if oid is not None and u.owner != oid:
                        raise RuntimeError("weight piece clobbered before use: %s" % u.name)
                    out.append(u)
            else:
                out.append(t)
        return out

    def op(self, eng, fn, reads=(), writes=(), dma=False):
        reads = self._flat(reads)
        writes = self._flat(writes)
        self.clock += 1
        for t in reads:
            t.last = self.clock
        for t in writes:
            t.last = self.clock
        ins = Ins(eng, fn, dma)
        deps = []
        for t in reads:
            if t.lw is not None:
                deps.append((t.lw, t.small))
        for t in writes:
            if t.lw is not None:
                deps.append((t.lw, t.small))
            for r in t.rd:
                deps.append((r, t.small))
        if dma:
            if eng == "pool":
                s = self.n_hw + self.rr_sw
                self.rr_sw = (self.rr_sw + 1) % (self.n_dma_sems - self.n_hw)
            else:
                s = self.rr_hw
                self.rr_hw = (self.rr_hw + 1) % self.n_hw
            prev = self.dma_last[s]
            if prev is not None:
                deps.append((prev, True))
            self.dma_cnt[s] += 16
            ins.dsem = s
            ins.dval = self.dma_cnt[s]
            self.dma_last[s] = ins
        seen = set()
        best = {}
        for d, force in deps:
            if d is ins:
                continue
            if d.dma:
                if id(d) not in seen:
                    seen.add(id(d))
                    ins.deps.append(d)
                continue
            if d.eng == eng and not dma:
                if eng == "pe":
                    continue
                if not (force or SAME_ENGINE_SYNC):
                    continue
            if d.eng not in best or best[d.eng].pos < d.pos:
                best[d.eng] = d
        ins.deps.extend(best.values())
        ins.pos = len(self.streams[eng])
        self.streams[eng].append(ins)
        for t in reads:
            t.rd.append(ins)
        for t in writes:
            t.lw = ins
            t.rd = []
        return ins

    def pe(self, fn, reads=(), writes=()):
        return self.op("pe", fn, reads, writes)

    def act(self, fn, reads=(), writes=()):
        return self.op("act", fn, reads, writes)

    def dve(self, fn, reads=(), writes=()):
        return self.op("dve", fn, reads, writes)

    def dma(self, q, out_ap, in_ap, reads=(), writes=()):
        return self.op(q, lambda e: e.dma_start(out=out_ap, in_=in_ap), reads, writes, dma=True)

    def emit(self, stack):
        nc = self.nc
        esem = {e: stack.enter_context(nc.semaphore("s_" + e)) for e in ENGS}
        dsem = [stack.enter_context(nc.semaphore("d_%d" % i)) for i in range(self.n_dma_sems)]
        for e in ENGS:
            for ins in self.streams[e]:
                for d in ins.deps:
                    if not d.dma:
                        d.sig = True
        for e in ENGS:
            c = 0
            for ins in self.streams[e]:
                if ins.sig and not ins.dma:
                    c += 1
                    ins.sigval = c
        final_waits = [(i, self.dma_cnt[i]) for i in range(self.n_dma_sems) if self.dma_cnt[i] > 0]
        block = stack.enter_context(nc.Block())
        engobj = {"pe": "tensor", "act": "scalar", "dve": "vector", "pool": "gpsimd", "sp": "sync"}

        def make(e):
            def body(eng):
                seen_eng = {x: 0 for x in ENGS}
                seen_dma = [0] * self.n_dma_sems
                for ins in self.streams[e]:
                    for d in ins.deps:
                        if d.dma:
                            if seen_dma[d.dsem] < d.dval:
                                eng.wait_ge(dsem[d.dsem], d.dval)
                                seen_dma[d.dsem] = d.dval
                        else:
                            if seen_eng[d.eng] < d.sigval:
                                eng.wait_ge(esem[d.eng], d.sigval)
                                seen_eng[d.eng] = d.sigval
                    r = ins.fn(eng)
                    if ins.dma:
                        r.then_inc(dsem[ins.dsem], 16)
                    elif ins.sig:
                        r.then_inc(esem[e], 1)
                if e == "sp":
                    for i, v in final_waits:
                        if seen_dma[i] < v:
                            eng.wait_ge(dsem[i], v)
            return body

        for e in ENGS:
            getattr(block, engobj[e])(make(e))


WUNIT = 1024
NWUNIT = 16


class Ctx:
    pass


def build_nc(layers, dbg=None):
    nc = bass.Bass("TRN2", target_bir_lowering=False)
    dt = lambda name, shape, dtype=F32, kind="ExternalInput": nc.dram_tensor(name, shape, dtype, kind=kind).ap()
    d_x = dt("x", [S, D])
    d_p = dt("p", [DEPTH, S, 256])
    d_pos = dt("pos", [128, NT], I32)
    d_glaw = dt("gla_w_in", [2, D, 3088])
    d_glagu = dt("gla_w_gate_up", [2, 16, 512])
    d_glab = dt("gla_b_gate_t", [2, 128, 4])
    d_glang = dt("gla_norm_g_t", [2, 128, 2])
    d_glawo = dt("gla_w_out", [2, D, D])
    d_mlaw = dt("mla_w_in", [1, D, 576])
    d_mlan = dt("mla_norms", [1, 512])
    d_mlauq = dt("mla_w_uq", [1, 256, 1536])
    d_mlaukv = dt("mla_w_ukv", [1, 256, 2048])
    d_mlawo = dt("mla_w_out", [1, D, D])
    d_cwin = dt("conv_w_in", [1, D, 3072])
    d_cw = dt("conv_w_t", [1, 128, 8, 3])
    d_cwo = dt("conv_w_out", [1, D, D])
    d_lng = dt("ln_g", [DEPTH, 2, D])
    d_lnb = dt("ln_b", [DEPTH, 2, D])
    d_w1 = dt("mlp_w1", [DEPTH, D, 4 * D])
    d_w2 = dt("mlp_w2", [DEPTH, 4 * D, D])
    d_pwg = dt("ple_w_gate", [DEPTH, D, D])
    d_pwp = dt("ple_w_proj", [DEPTH, 256, D])
    d_ident = dt("c_ident", [128, 128])
    d_gmask = dt("c_gmask", [128, 256])
    d_rmask = dt("c_rmask", [128, 512])
    d_invf = dt("c_invf", [128, 32])
    d_out = dt("out", [S, D], F32, "ExternalOutput")
    d_dbg = dt("dbg", [S, D], F32, "ExternalOutput") if dbg else None

    with ExitStack() as st:
        P = Prog(nc)
        sb = lambda name, shape, dtype: st.enter_context(nc.sbuf_tensor(name, shape, dtype))

        Xb = sb("X", [128, NT, D], F32)
        X = [Tl(Xb[:, t, :], "X%d" % t) for t in range(NT)]
        XTb = sb("XT", [128, KC, S], BF16)
        XT = [Tl(XTb[:, :, t * 128:(t + 1) * 128], "XT%d" % t) for t in range(NT)]
        Hb = sb("H", [128, 16384], BF16)
        SCb = sb("SC", [128, 11264], BF16)
        Wb = sb("W", [128, NWUNIT * WUNIT], BF16)
        Wt = [Tl(Wb[:, i * WUNIT:(i + 1) * WUNIT], "W%d" % i) for i in range(NWUNIT)]
        C_w = {"rr": 0, "oid": 0}
        LNg = Tl(sb("LNg", [128, D], F32), "LNg")
        LNb = Tl(sb("LNb", [128, D], F32), "LNb")
        identf = Tl(sb("identf", [128, 128], F32))
        ident = Tl(sb("ident", [128, 128], BF16))
        xbf = [Tl(sb("xbf%d" % i, [128, D], BF16)) for i in range(2)]
        stg = [Tl(sb("stg%d" % i, [128, 512], F32)) for i in range(2)]
        st6s = [Tl(sb("st6_%d" % i, [128, 2, 6], F32), small=True) for i in range(4)]
        mvs = [Tl(sb("mv_%d" % i, [128, 8], F32), small=True) for i in range(4)]
        nhalf = Tl(sb("nhalf", [128, 8], F32), "nhalf")
        PSt = st.enter_context(nc.psum_tensor("ps", [128, 8, 512], F32))
        PS = [Tl(PSt[:, i, :], "PS%d" % i) for i in range(8)]
        C = Ctx()
        C.ps_rr = 0
        C.xb_rr = 0
        C.mv_rr = 0
        C.stg_rr = 0

        C.resv = set()

        def ps1():
            cand = [i for i in range(8) if i not in C.resv]
            i = min(cand, key=lambda k: PS[k].last)
            PS[i].last = P.clock + 1
            return i

        def ps2():
            cand = [i for i in range(0, 8, 2) if i not in C.resv and (i + 1) not in C.resv]
            i = min(cand, key=lambda k: max(PS[k].last, PS[k + 1].last))
            PS[i].last = P.clock + 1
            PS[i + 1].last = P.clock + 1
            return i

        ARENA = {"H": [], "SC": []}

        def fence(arena, new_tiles):
            prior = []
            for t in ARENA[arena]:
                if t.lw is not None:
                    prior.append(t.lw)
                prior.extend(t.rd)
            for t in new_tiles:
                t.rd = list(prior)
            ARENA[arena] = list(new_tiles)

        def nstg():
            i = C.stg_rr
            C.stg_rr = (C.stg_rr + 1) % 2
            return stg[i]

        def walloc(n):
            nu = (n + WUNIT - 1) // WUNIT
            if C_w["rr"] + nu > NWUNIT:
                C_w["rr"] = 0
            u0 = C_w["rr"]
            C_w["rr"] = (u0 + nu) % NWUNIT
            C_w["oid"] += 1
            wp = WPiece(Wt[u0:u0 + nu])
            wp.oid = C_w["oid"]
            for u in wp:
                u.owner = wp.oid
            return wp, Wb[:, u0 * WUNIT:u0 * WUNIT + n]

        def wload(src_ap, shape):
            n = 1
            for s_ in shape[1:]:
                n *= s_
            wp, view = walloc(n)
            if len(shape) == 3:
                view = view.rearrange("p (a b) -> p a b", b=shape[2])
            elif len(shape) == 4:
                view = view.rearrange("p (a b c) -> p a b c", b=shape[2], c=shape[3])
            P.dma("pool", view, src_ap, writes=[wp])
            return wp, view

        def wview(dram2d, c0, ncols):
            return dram2d[:, c0:c0 + ncols].rearrange("(kc k) n -> k kc n", k=128)

        P.op("pool", lambda e: e.memset(nhalf[:], -0.5), writes=[nhalf])
        P.dma("sp", identf[:], d_ident[:, :], writes=[identf])
        P.dve(lambda e: e.tensor_copy(out=ident[:], in_=identf[:]), reads=[identf], writes=[ident])

        gsm_b = sb("gsm_b", [128, 1024], BF16)
        gsm_f = sb("gsm_f", [128, 64], F32)
        G_SM = (Tl(gsm_b[0:16, 0:512], "glrT"), Tl(gsm_b[0:16, 512:1024], "wgu"),
                Tl(gsm_f[:, 0:4], "bgt", True), Tl(gsm_f[:, 4:8], "nbg", True), Tl(gsm_f[:, 8:10], "ngt", True),
                Tl(gsm_f[:, 16:32], "eend", True), Tl(gsm_f[:, 32:40], "ssq", True), Tl(gsm_f[:, 40:48], "rstd", True),
                Tl(sb("k3T", [128, 512], BF16), "k3T"), Tl(sb("gmask", [128, 256], F32), "gmask"),
                Tl(sb("rmask", [128, 512], BF16), "rmask"), Tl(sb("wglr", [128, 8, 16], BF16), "wglr"))
        mla_i = sb("mla_i", [128, 16], I32)
        mla_f = sb("mla_f", [128, 64], F32)
        MLA_SM = (Tl(mla_i[:, :], "posi", True), Tl(mla_f[:, 0:16], "posf", True), Tl(mla_f[:, 16:48], "invf"),
                  [Tl(mla_f[:, 48 + i:49 + i], "rec%d" % i, True) for i in range(4)])
        P.dma("sp", G_SM[9][:], d_gmask[:, :], writes=[G_SM[9]])
        P.dma("pool", G_SM[10][:], d_rmask[:, :], writes=[G_SM[10]])

        def to_xt(tt, on_act=True, evac_act=False):
            xb = xbf[C.xb_rr]
            C.xb_rr ^= 1
            if on_act:
                P.act(lambda e: e.copy(out=xb[:], in_=X[tt][:]), reads=[X[tt]], writes=[xb])
            else:
                P.dve(lambda e: e.tensor_copy(out=xb[:], in_=X[tt][:]), reads=[X[tt]], writes=[xb])
            b = ps1()
            pv = PSt[:, b, :].bitcast(BF16)
            for kc in range(KC):
                P.pe(lambda e, kc=kc: e.transpose(out=pv[:, kc * 128:(kc + 1) * 128], in_=xb[:, kc * 128:(kc + 1) * 128],
                                                  identity=ident[:]), reads=[xb, ident], writes=[PS[b]])
            if evac_act:
                P.act(lambda e: e.copy(out=XT[tt][:], in_=pv.rearrange("p (a b) -> p a b", b=128)), reads=[PS[b]], writes=[XT[tt]])
            else:
                P.dve(lambda e: e.tensor_copy(out=XT[tt][:], in_=pv.rearrange("p (a b) -> p a b", b=128)),
                      reads=[PS[b]], writes=[XT[tt]])

        class XtQ:
            def __init__(self, delay=2):
                self.q = []
                self.delay = delay

            def push(self, tt):
                self.q.append(tt)
                while len(self.q) > self.delay:
                    to_xt(self.q.pop(0))

            def flush(self):
                while self.q:
                    to_xt(self.q.pop(0))

        def load_ln(layer, which):
            P.dma("sp", LNg[:], d_lng[layer, which, :].partition_broadcast(128), writes=[LNg])
            P.dma("sp", LNb[:], d_lnb[layer, which, :].partition_broadcast(128), writes=[LNb])

        def ln_inplace(tt):
            k = C.mv_rr
            C.mv_rr = (C.mv_rr + 1) % 4
            m, s6 = mvs[k], st6s[k]
            P.dve(lambda e: e.bn_stats(out=s6[:, 0, :], in_=X[tt][:, 0:512]), reads=[X[tt]], writes=[s6])
            P.dve(lambda e: e.bn_stats(out=s6[:, 1, :], in_=X[tt][:, 512:1024]), reads=[X[tt]], writes=[s6])
            P.dve(lambda e: e.bn_aggr(out=m[:, 0:2], in_=s6[:]), reads=[s6], writes=[m])
            P.op("pool", lambda e: e.tensor_scalar_add(out=m[:, 2:3], in0=m[:, 1:2], scalar1=LN_EPS), reads=[m], writes=[m])
            P.op("pool", lambda e: e.tensor_tensor(out=m[:, 3:4], in0=m[:, 2:3], in1=nhalf[:, 0:1], op=ALU.pow), reads=[m, nhalf], writes=[m])
            P.dve(lambda e: e.scalar_tensor_tensor(out=X[tt][:], in0=X[tt][:], scalar=m[:, 0:1], in1=LNg[:], op0=ALU.subtract, op1=ALU.mult),
                  reads=[X[tt], m, LNg], writes=[X[tt]])
            P.dve(lambda e: e.scalar_tensor_tensor(out=X[tt][:], in0=X[tt][:], scalar=m[:, 3:4], in1=LNb[:], op0=ALU.mult, op1=ALU.add),
                  reads=[X[tt], m, LNb], writes=[X[tt]])

        def resid_ln(tt, b2):
            P.dve(lambda e: e.scalar_tensor_tensor(out=X[tt][:], in0=X[tt][:], scalar=ALPHA, in1=PSt[:, b2:b2 + 2, :].rearrange("p a b -> p (a b)"),
                                                   op0=ALU.mult, op1=ALU.add), reads=[X[tt], PS[b2], PS[b2 + 1]], writes=[X[tt]])
            ln_inplace(tt)

        def dense_tok(tt, wv0, wv1, wt0, wt1, lhs_fn, nk, lhs_tiles):
            b2 = ps2()
            for half, (wv, wt) in enumerate(((wv0, wt0), (wv1, wt1))):
                for k in range(nk):
                    P.pe(lambda e, k=k, wv=wv, half=half: e.matmul(PSt[:, b2 + half, :], lhsT=lhs_fn(k), rhs=wv[:, k, :],
                                                                   start=(k == 0), stop=(k == nk - 1)),
                         reads=list(lhs_tiles) + [wt], writes=[PS[b2 + half]])
            return b2

        def mlp(layer):
            hT = Hb[:, :].rearrange("p (f t) -> p f t", t=S)
            hTl = [Tl(hT[:, f, :], "hT%d" % f) for f in range(8)]
            fence("H", hTl)
            w1 = d_w1[layer]
            w2 = d_w2[layer]
            load_ln(layer, 1)
            for fb in range(4):
                for half in range(2):
                    wt, wv = wload(wview(w1, fb * 1024 + half * 512, 512), [128, 8, 512])
                    for f4 in range(4):
                        f = half * 4 + f4
                        for g in range(4):
                            b = ps1()
                            for kc in range(KC):
                                P.pe(lambda e, kc=kc, f4=f4, g=g, wv=wv, b=b: e.matmul(
                                    PSt[:, b, :], lhsT=wv[:, kc, f4 * 128:(f4 + 1) * 128], rhs=XTb[:, kc, g * 512:(g + 1) * 512],
                                    start=(kc == 0), stop=(kc == KC - 1)),
                                    reads=[wt] + XT[4 * g:4 * g + 4], writes=[PS[b]])
                            r = nstg()
                            P.act(lambda e, r=r, b=b: e.activation(out=r[:], in_=PSt[:, b, :], func=AF.Relu), reads=[PS[b]], writes=[r])
                            P.dve(lambda e, r=r, f=f, g=g: e.tensor_tensor(out=hT[:, f, g * 512:(g + 1) * 512], in0=r[:], in1=r[:], op=ALU.mult),
                                  reads=[r], writes=[hTl[f]])
                wts = []
                for half in range(2):
                    wts.append(wload(w2[fb * 1024 + half * 512: fb * 1024 + (half + 1) * 512, :].rearrange("(fc f) n -> f fc n", f=128),
                                     [128, 4, 1024]))

                def down(tt, fb=fb, wts=wts):
                    b2 = ps2()
                    for nh in range(2):
                        for f in range(8):
                            wt, wv = wts[f // 4]
                            P.pe(lambda e, f=f, nh=nh, wv=wv, tt=tt, b2=b2: e.matmul(
                                PSt[:, b2 + nh, :], lhsT=hT[:, f, tt * 128:(tt + 1) * 128], rhs=wv[:, f % 4, nh * 512:(nh + 1) * 512],
                                start=(f == 0), stop=(f == 7)), reads=[hTl[f], wt], writes=[PS[b2 + nh]])
                    pin = PSt[:, b2:b2 + 2, :].rearrange("p a b -> p (a b)")
                    if fb == 0:
                        P.dve(lambda e, tt=tt, pin=pin: e.scalar_tensor_tensor(out=X[tt][:], in0=X[tt][:], scalar=ALPHA, in1=pin,
                                                                             op0=ALU.mult, op1=ALU.add),
                              reads=[X[tt], PS[b2], PS[b2 + 1]], writes=[X[tt]])
                    else:
                        P.dve(lambda e, tt=tt, pin=pin: e.tensor_tensor(out=X[tt][:], in0=X[tt][:], in1=pin, op=ALU.add),
                              reads=[X[tt], PS[b2], PS[b2 + 1]], writes=[X[tt]])
                    if fb == 3:
                        ln_inplace(tt)

                if fb < 3:
                    for tt in range(NT):
                        down(tt)
            return down

        def ple(layer, last, down):
            pbf = SCb[:, 0:4096].rearrange("p (t k) -> p t k", k=256)
            pbt = Tl(pbf, "pbf")
            pT = SCb[:, 4096:8192].rearrange("p (c t) -> p c t", t=S)
            pTt = Tl(pT, "pT")
            wpv = SCb[:, 8192:10240].rearrange("p (c n) -> p c n", n=1024)
            wpt = Tl(wpv, "wp")
            fence("SC", [pbt, pTt, wpt])
            P.dma("pool", pbf, d_p[layer].rearrange("(t q) k -> q t k", q=128), writes=[pbt])
            P.dma("pool", wpv, wview(d_pwp[layer], 0, 1024), writes=[wpt])
            for tt in range(NT):
                b = ps1()
                pv = PSt[:, b, :].bitcast(BF16)
                for c in range(2):
                    P.pe(lambda e, c=c, tt=tt, pv=pv: e.transpose(out=pv[:, c * 128:(c + 1) * 128], in_=pbf[:, tt, c * 128:(c + 1) * 128],
                                                             identity=ident[:]), reads=[pbt, ident], writes=[PS[b]])
                P.act(lambda e, tt=tt, pv=pv: e.copy(out=pT[:, :, tt * 128:(tt + 1) * 128], in_=pv[:, 0:256].rearrange("p (a b) -> p a b", b=128)),
                      reads=[PS[b]], writes=[pTt])
            wg = [wload(wview(d_pwg[layer], h * 512, 512), [128, 8, 512]) for h in range(2)]

            def ple_tile(tt):
                for half in range(2):
                    wt, wv = wg[half]
                    ba = ps1()
                    for kc in range(KC):
                        P.pe(lambda e, kc=kc, wv=wv, ba=ba, tt=tt: e.matmul(PSt[:, ba, :], lhsT=XTb[:, kc, tt * 128:(tt + 1) * 128], rhs=wv[:, kc, :],
                                                                        start=(kc == 0), stop=(kc == KC - 1)),
                             reads=[XT[tt], wt], writes=[PS[ba]])
                    bb = ps1()
                    for c in range(2):
                        P.pe(lambda e, c=c, bb=bb, tt=tt, half=half: e.matmul(PSt[:, bb, :], lhsT=pT[:, c, tt * 128:(tt + 1) * 128],
                                                                          rhs=wpv[:, c, half * 512:(half + 1) * 512], start=(c == 0), stop=(c == 1)),
                             reads=[pTt, wpt], writes=[PS[bb]])
                    sg = nstg()
                    P.act(lambda e, sg=sg, ba=ba: e.activation(out=sg[:], in_=PSt[:, ba, :], func=AF.Sigmoid), reads=[PS[ba]], writes=[sg])
                    P.dve(lambda e, sg=sg, bb=bb: e.tensor_tensor(out=sg[:], in0=sg[:], in1=PSt[:, bb, :], op=ALU.mult), reads=[sg, PS[bb]], writes=[sg])
                    P.dve(lambda e, sg=sg, tt=tt, half=half: e.tensor_tensor(out=X[tt][:, half * 512:(half + 1) * 512],
                                                                             in0=X[tt][:, half * 512:(half + 1) * 512], in1=sg[:], op=ALU.add),
                          reads=[sg, X[tt]], writes=[X[tt]])

            D1, D2, D3 = 2, 3, 5
            for i in range(NT + D3):
                if i < NT:
                    down(i)
                if 0 <= i - D1 < NT:
                    to_xt(i - D1, evac_act=True)
                if 0 <= i - D2 < NT:
                    ple_tile(i - D2)
                if not last and 0 <= i - D3 < NT:
                    to_xt(i - D3, evac_act=True)

        def conv_mixer(j, layer):
            YT = Hb[:, :].rearrange("p (c t) -> p c t", t=S)
            YTl = [Tl(YT[:, c, :], "YT%d" % c) for c in range(8)]
            CUf = SCb[:, 0:4352].bitcast(F32)
            CU = Tl(CUf, "CU")
            Zf = SCb[:, 4352:8448].bitcast(F32)
            Z = Tl(Zf, "Z")
            Bs = Tl(SCb[:, 8448:10496], "Bs")
            cw = Tl(SCb[:, 10496:10544].bitcast(F32).rearrange("p (c j) -> p c j", j=3), "cw", small=True)
            fence("H", YTl)
            fence("SC", [CU, Z, Bs, cw])
            P.dma("sp", cw[:], d_cw[j], writes=[cw])
            P.dve(lambda e: e.memset(CUf[:, 0:2], 0.0), writes=[CU])
            load_ln(layer, 0)
            win = d_cwin[j]
            for cg in range(4):
                wp = [wload(wview(win, sec * 1024 + cg * 256, 256), [128, 8, 256]) for sec in range(3)]
                for c4 in range(2):
                    cc = cg * 2 + c4
                    for g in range(4):
                        bs = []
                        for sec in range(3):
                            wt, wv = wp[sec]
                            b = ps1()
                            bs.append(b)
                            for kc in range(KC):
                                P.pe(lambda e, kc=kc, wv=wv, b=b, c4=c4, g=g: e.matmul(
                                    PSt[:, b, :], lhsT=wv[:, kc, c4 * 128:(c4 + 1) * 128], rhs=XTb[:, kc, g * 512:(g + 1) * 512],
                                    start=(kc == 0), stop=(kc == KC - 1)), reads=[wt] + XT[4 * g:4 * g + 4], writes=[PS[b]])
                        P.act(lambda e, b=bs[0], g=g: e.copy(out=Bs[:, g * 512:(g + 1) * 512], in_=PSt[:, b, :]), reads=[PS[bs[0]]], writes=[Bs])
                        cs = nstg()
                        P.act(lambda e, b=bs[1], cs=cs: e.copy(out=cs[:], in_=PSt[:, b, :]), reads=[PS[bs[1]]], writes=[cs])
                        P.dve(lambda e, b=bs[2], cs=cs, g=g: e.tensor_tensor(out=CUf[:, 2 + g * 512: 2 + (g + 1) * 512], in0=PSt[:, b, :], in1=cs[:], op=ALU.mult),
                              reads=[PS[bs[2]], cs], writes=[CU])
                    P.dve(lambda e, cc=cc: e.tensor_scalar(out=Zf[:, :], in0=CUf[:, 2:2050], scalar1=cw[:, cc, 2:3], scalar2=None, op0=ALU.mult),
                          reads=[CU, cw], writes=[Z])
                    P.dve(lambda e, cc=cc: e.scalar_tensor_tensor(out=Zf[:, :], in0=CUf[:, 1:2049], scalar=cw[:, cc, 1:2], in1=Zf[:, :], op0=ALU.mult, op1=ALU.add),
                          reads=[CU, cw, Z], writes=[Z])
                    P.dve(lambda e, cc=cc: e.scalar_tensor_tensor(out=Zf[:, :], in0=CUf[:, 0:2048], scalar=cw[:, cc, 0:1], in1=Zf[:, :], op0=ALU.mult, op1=ALU.add),
                          reads=[CU, cw, Z], writes=[Z])
                    P.dve(lambda e, cc=cc: e.tensor_tensor(out=YT[:, cc, :], in0=Zf[:, :], in1=Bs[:], op=ALU.mult), reads=[Z, Bs], writes=[YTl[cc]])
            wo = [wload(wview(d_cwo[j], h * 512, 512), [128, 8, 512]) for h in range(2)]
            xq = XtQ(2)
            for tt in range(NT):
                b2 = dense_tok(tt, wo[0][1], wo[1][1], wo[0][0], wo[1][0], lambda k, tt=tt: YT[:, k, tt * 128:(tt + 1) * 128], 8, YTl)
                resid_ln(tt, b2)
                xq.push(tt)
            xq.flush()

        def gla_mixer(j, layer):
            win = d_glaw[j]
            o = [0]

            def take(n):
                r = o[0]
                o[0] += n
                return r
            a0 = take(2048); Vh = [Hb[:, a0 + i * 1024: a0 + (i + 1) * 1024].rearrange("p (t n) -> p t n", n=256) for i in range(2)]
            Vht = [Tl(Vh[i], "Vh%d" % i) for i in range(2)]
            a0 = take(2048); Gh = [Hb[:, a0 + i * 1024: a0 + (i + 1) * 1024].rearrange("p (t n) -> p t n", n=256) for i in range(2)]
            Ght = [Tl(Gh[i], "Gh%d" % i) for i in range(2)]
            a0 = take(1024); K3 = [Hb[:, a0 + i * 512: a0 + (i + 1) * 512].rearrange("p (q d) -> p q d", d=128) for i in range(2)]
            K3t = [Tl(K3[i], "K3_%d" % i) for i in range(2)]
            a0 = take(512); STf = Tl(Hb[:, a0:a0 + 512].bitcast(F32), "STf")
            a0 = take(256); STb = Tl(Hb[:, a0:a0 + 256], "STb")
            a0 = take(512); OGh = [Tl(Hb[:, a0 + i * 256: a0 + (i + 1) * 256], "OGh%d" % i) for i in range(2)]
            a0 = take(512); OGT = [Tl(Hb[:, a0 + i * 256: a0 + (i + 1) * 256].rearrange("p (c t) -> p c t", t=128), "OGT%d" % i) for i in range(2)]
            a0 = take(512); S12 = [Tl(Hb[:, a0 + i * 256: a0 + (i + 1) * 256], "S12_%d" % i) for i in range(2)]
            a0 = take(2048); glrA = Tl(Hb[0:16, a0:a0 + 2048], "glrA")
            a0 = take(512); k3T = Tl(Hb[:, a0:a0 + 512], "k3T")
            assert o[0] <= 16384
            PR = [SCb[:, i * 2048:(i + 1) * 2048].rearrange("p (k t) -> p k t", t=512) for i in range(2)]
            PRt = [Tl(PR[i], "PR%d" % i) for i in range(2)]
            LT = Tl(SCb[:, 4096:5120].bitcast(F32), "LT")
            EP = Tl(SCb[:, 5120:6144].bitcast(F32), "EP")
            EM = Tl(SCb[:, 6144:7168].bitcast(F32), "EM")
            _glrT, wgu, bgt, nbg, ngt, eend, ssq, rstd, _k3T, gmask, rmask, wglr = G_SM
            ssq_l = [Tl(gsm_f[:, 32 + i:33 + i], "ssq%d" % i, True) for i in range(4)]
            sq_l = [Tl(gsm_f[:, 36 + i:37 + i], "sq%d" % i, True) for i in range(4)]
            rs_l = [Tl(gsm_f[:, 40 + i:41 + i], "rs%d" % i, True) for i in range(4)]
            if RSTD_LATE:
                for tt in range(NT):
                    P.act(lambda e, tt=tt: e.activation(out=X[tt][:], in_=X[tt][:], func=AF.Copy, scale=ALPHA), reads=[X[tt]], writes=[X[tt]])
            fence("H", Vht + Ght + K3t + [STf, STb] + OGh + OGT + S12 + [glrA, k3T])
            fence("SC", PRt + [LT, EP, EM])
            P.dma("pool", wgu[:], d_glagu[j], writes=[wgu])
            P.dma("sp", bgt[:], d_glab[j], writes=[bgt])
            P.dma("sp", ngt[:], d_glang[j], writes=[ngt])
            P.dma("pool", wglr[:], win[:, 3072:3088].rearrange("(kc k) n -> k kc n", k=128), writes=[wglr])
            P.dve(lambda e: e.tensor_scalar(out=nbg[:], in0=bgt[:], scalar1=-1.0, scalar2=None, op0=ALU.mult), reads=[bgt], writes=[nbg])
            load_ln(layer, 0)
            qs = 128.0 ** -0.5
            for g in range(4):
                b = ps1()
                for kc in range(KC):
                    P.pe(lambda e, kc=kc, b=b, g=g: e.matmul(PSt[0:16, b, :], lhsT=wglr[:, kc, :], rhs=XTb[:, kc, g * 512:(g + 1) * 512],
                                                         start=(kc == 0), stop=(kc == KC - 1)), reads=[wglr] + XT[4 * g:4 * g + 4], writes=[PS[b]])
                P.act(lambda e, b=b, g=g: e.copy(out=glrA[:, g * 512:(g + 1) * 512], in_=PSt[0:16, b, :]), reads=[PS[b]], writes=[glrA])
            gxq = XtQ(2)

            for h in range(4):
                wq = wload(wview(win, h * 128, 128), [128, 8, 128])
                wk = wload(wview(win, 512 + h * 128, 128), [128, 8, 128])
                wv = wload(wview(win, 1024 + h * 256, 256), [128, 8, 256])
                wr = wload(wview(win, 2048 + h * 256, 256), [128, 8, 256])
                wo = wload(d_glawo[j][h * 256:(h + 1) * 256, :].rearrange("(c k) n -> k c n", k=128), [128, 2, 1024])
                for c in range(2):
                    P.dve(lambda e, c=c, wo=wo: e.tensor_scalar(out=wo[1][:, c, :], in0=wo[1][:, c, :], scalar1=ngt[:, c:c + 1], scalar2=None, op0=ALU.mult),
                          reads=[wo[0], ngt], writes=[wo[0]])
                P.dve(lambda e: e.memset(STf[:], 0.0), writes=[STf])
                P.dve(lambda e: e.memset(STb[:], 0.0), writes=[STb])

                def prepA(g, h=h):
                    gb = g % 2
                    b = ps1()
                    P.pe(lambda e, b=b, h=h, g=g: e.matmul(PSt[:, b, :], lhsT=wgu[:, h * 128:(h + 1) * 128], rhs=glrA[:, g * 512:(g + 1) * 512], start=True, stop=True),
                         reads=[wgu, glrA], writes=[PS[b]])
                    P.act(lambda e, b=b, h=h: e.activation(out=LT[:], in_=PSt[:, b, :], func=AF.Exp, scale=-1.0, bias=nbg[:, h:h + 1]),
                          reads=[PS[b], nbg], writes=[LT])
                    P.act(lambda e: e.activation(out=LT[:], in_=LT[:], func=AF.Ln, bias=1.0), reads=[LT], writes=[LT])

                def prepA2(g, h=h):
                    gb = g % 2
                    P.dve(lambda e: e.tensor_scalar(out=LT[:], in0=LT[:], scalar1=-1.0 / 16.0, scalar2=None, op0=ALU.mult), reads=[LT], writes=[LT])
                    P.dve(lambda e: e.tensor_tensor_scan(out=LT[:], data0=rmask[:], data1=LT[:], initial=0.0, op0=ALU.mult, op1=ALU.add),
                          reads=[LT, rmask], writes=[LT])
                    P.act(lambda e: e.activation(out=EP[:], in_=LT[:], func=AF.Exp), reads=[LT], writes=[EP])
                    P.act(lambda e: e.activation(out=EM[:], in_=LT[:], func=AF.Exp, scale=-1.0), reads=[LT], writes=[EM])
                    P.dve(lambda e, gb=gb: e.tensor_copy(out=eend[:, gb * 4:(gb + 1) * 4], in_=EP[:].rearrange("p (q t) -> p q t", t=128)[:, :, 127]),
                          reads=[EP], writes=[eend])

                def prepB(g, h=h, wq=wq, wk=wk):
                    gb = g % 2
                    xts = XT[4 * g:4 * g + 4]
                    pr, prt = PR[gb], PRt[gb]
                    bq = ps1()
                    for kc in range(KC):
                        P.pe(lambda e, kc=kc, bq=bq, g=g, wq=wq: e.matmul(PSt[:, bq, :], lhsT=wq[1][:, kc, :], rhs=XTb[:, kc, g * 512:(g + 1) * 512],
                                                                      start=(kc == 0), stop=(kc == KC - 1)), reads=[wq[0]] + xts, writes=[PS[bq]])
                    bk = ps1()
                    for kc in range(KC):
                        P.pe(lambda e, kc=kc, bk=bk, g=g, wk=wk: e.matmul(PSt[:, bk, :], lhsT=wk[1][:, kc, :], rhs=XTb[:, kc, g * 512:(g + 1) * 512],
                                                                      start=(kc == 0), stop=(kc == KC - 1)), reads=[wk[0]] + xts, writes=[PS[bk]])
                    P.dve(lambda e, bq=bq, pr=pr: e.scalar_tensor_tensor(out=pr[:, 0, :], in0=PSt[:, bq, :], scalar=qs, in1=EP[:], op0=ALU.mult, op1=ALU.mult),
                          reads=[PS[bq], EP], writes=[prt])
                    P.dve(lambda e, bq=bq, pr=pr: e.scalar_tensor_tensor(out=pr[:, 1, :], in0=PSt[:, bq, :], scalar=qs, in1=EM[:], op0=ALU.mult, op1=ALU.mult),
                          reads=[PS[bq], EM], writes=[prt])
                    P.dve(lambda e, bk=bk, pr=pr: e.tensor_tensor(out=pr[:, 2, :], in0=PSt[:, bk, :], in1=EM[:], op=ALU.mult), reads=[PS[bk], EM], writes=[prt])
                    P.dve(lambda e, bk=bk, pr=pr: e.tensor_tensor(out=pr[:, 3, :], in0=PSt[:, bk, :], in1=EP[:], op=ALU.mult), reads=[PS[bk], EP], writes=[prt])
                    for q in range(4):
                        P.dve(lambda e, q=q, pr=pr, gb=gb: e.tensor_scalar(out=k3T[:, q * 128:(q + 1) * 128], in0=pr[:, 2, q * 128:(q + 1) * 128],
                                                                      scalar1=eend[:, gb * 4 + q:gb * 4 + q + 1], scalar2=None, op0=ALU.mult),
                              reads=[prt, eend], writes=[k3T])

                def prep_vr(g, t4, h=h, wv=wv, wr=wr):
                    gb = g % 2
                    vh, vht, gh, ght = Vh[gb], Vht[gb], Gh[gb], Ght[gb]
                    if True:
                        tt = 4 * g + t4
                        b = ps1()
                        for sec, w_ in ((0, wv), (1, wr)):
                            for kc in range(KC):
                                P.pe(lambda e, kc=kc, b=b, tt=tt, sec=sec, w_=w_: e.matmul(
                                    PSt[:, b, sec * 256:(sec + 1) * 256], lhsT=XTb[:, kc, tt * 128:(tt + 1) * 128], rhs=w_[1][:, kc, :],
                                    start=(kc == 0), stop=(kc == KC - 1)), reads=[XT[tt], w_[0]], writes=[PS[b]])
                        P.act(lambda e, b=b, t4=t4, vh=vh: e.copy(out=vh[:, t4, :], in_=PSt[:, b, 0:256]), reads=[PS[b]], writes=[vht])
                        P.act(lambda e, b=b, t4=t4, gh=gh: e.activation(out=gh[:, t4, :], in_=PSt[:, b, 256:512], func=AF.Silu), reads=[PS[b]], writes=[ght])

                def prep2(g):
                    k3, k3t = K3[g % 2], K3t[g % 2]
                    bt = ps1()
                    pv = PSt[:, bt, :].bitcast(BF16)
                    for q in range(4):
                        P.pe(lambda e, q=q, pv=pv: e.transpose(out=pv[:, q * 128:(q + 1) * 128], in_=k3T[:, q * 128:(q + 1) * 128], identity=ident[:]),
                             reads=[k3T, ident], writes=[PS[bt]])
                    P.act(lambda e, pv=pv, k3=k3: e.copy(out=k3[:, :, :], in_=pv[:, 0:512].rearrange("p (q d) -> p q d", d=128)), reads=[PS[bt]], writes=[k3t])

                def stageA(p):
                    g, q = p // 4, p % 4
                    pr, prt = PR[g % 2], PRt[g % 2]
                    s12 = S12[p % 2]
                    bab = ps1()
                    P.pe(lambda e, q=q, bab=bab, pr=pr: e.matmul(PSt[:, bab, 0:128], lhsT=pr[:, 2, q * 128:(q + 1) * 128], rhs=pr[:, 0, q * 128:(q + 1) * 128],
                                                             start=True, stop=True), reads=[prt], writes=[PS[bab]])
                    P.pe(lambda e, q=q, bab=bab, pr=pr: e.matmul(PSt[:, bab, 128:256], lhsT=pr[:, 3, q * 128:(q + 1) * 128], rhs=pr[:, 1, q * 128:(q + 1) * 128],
                                                             start=True, stop=True), reads=[prt], writes=[PS[bab]])
                    P.dve(lambda e, bab=bab, s12=s12: e.tensor_tensor(out=s12[:], in0=PSt[:, bab, 0:256], in1=gmask[:], op=ALU.mult),
                          reads=[PS[bab], gmask], writes=[s12])

                def stageB(p, h=h):
                    g, q = p // 4, p % 4
                    gb = g % 2
                    pr, prt, vh, vht, gh, ght, k3, k3t = PR[gb], PRt[gb], Vh[gb], Vht[gb], Gh[gb], Ght[gb], K3[gb], K3t[gb]
                    s12 = S12[p % 2]
                    og = OGh[p % 2]
                    bo = ps1()
                    oreg = PSt[:, bo, 0:256]
                    vq = vh[:, q, :]
                    P.pe(lambda e, s12=s12, oreg=oreg, vq=vq: e.matmul(oreg, lhsT=s12[:, 0:128], rhs=vq, start=True, stop=False), reads=[s12, vht], writes=[PS[bo]])
                    P.pe(lambda e, s12=s12, oreg=oreg, vq=vq: e.matmul(oreg, lhsT=s12[:, 128:256], rhs=vq, start=False, stop=False), reads=[s12, vht], writes=[PS[bo]])
                    P.pe(lambda e, q=q, oreg=oreg, pr=pr: e.matmul(oreg, lhsT=pr[:, 0, q * 128:(q + 1) * 128], rhs=STb[:], start=False, stop=True),
                         reads=[prt, STb], writes=[PS[bo]])
                    sreg = PSt[:, bo, 256:512]
                    P.pe(lambda e, q=q, sreg=sreg, vq=vq, k3=k3: e.matmul(sreg, lhsT=k3[:, q, :], rhs=vq, start=True, stop=True), reads=[k3t, vht], writes=[PS[bo]])
                    if ST_DVE == 2:
                        P.dve(lambda e, q=q, gb=gb, sreg=sreg: e.scalar_tensor_tensor(out=STb[:], in0=STf[:], scalar=eend[:, gb * 4 + q:gb * 4 + q + 1], in1=sreg,
                                                                                 op0=ALU.mult, op1=ALU.add), reads=[STf, eend, PS[bo]], writes=[STb])
                    P.dve(lambda e, q=q, gb=gb, sreg=sreg: e.scalar_tensor_tensor(out=STf[:], in0=STf[:], scalar=eend[:, gb * 4 + q:gb * 4 + q + 1], in1=sreg,
                                                                             op0=ALU.mult, op1=ALU.add), reads=[STf, eend, PS[bo]], writes=[STf])
                    if ST_DVE == 1:
                        P.dve(lambda e: e.tensor_copy(out=STb[:], in_=STf[:]), reads=[STf], writes=[STb])
                    elif ST_DVE == 0:
                        P.act(lambda e: e.copy(out=STb[:], in_=STf[:]), reads=[STf], writes=[STb])
                    if RSTD_LATE:
                        P.dve(lambda e, oreg=oreg, og=og, gh=gh, q=q: e.tensor_tensor(out=og[:], in0=oreg, in1=gh[:, q, :], op=ALU.mult),
                              reads=[PS[bo], ght], writes=[og])
                    jk = nstg()
                    sq_, sr_, rs_ = ssq_l[p % 4], sq_l[p % 4], rs_l[p % 4]
                    P.act(lambda e, oreg=oreg, jk=jk, sq_=sq_: e.activation(out=jk[:, 0:256], in_=oreg, func=AF.Square, accum_out=sq_[:, 0:1]),
                          reads=[PS[bo]], writes=[jk, sq_])
                    if not RSTD_LATE:
                        P.act(lambda e, sq_=sq_, sr_=sr_: e.activation(out=sr_[:, 0:1], in_=sq_[:, 0:1], func=AF.Sqrt, scale=1.0 / 256.0, bias=RMS_EPS),
                              reads=[sq_], writes=[sr_])
                    if not RSTD_LATE:
                        P.dve(lambda e, sr_=sr_, rs_=rs_: e.reciprocal(out=rs_[:, 0:1], in_=sr_[:, 0:1]), reads=[sr_], writes=[rs_])
                        P.dve(lambda e, oreg=oreg, og=og, gh=gh, q=q, rs_=rs_: e.scalar_tensor_tensor(out=og[:], in0=oreg, scalar=rs_[:, 0:1], in1=gh[:, q, :],
                                                                                               op0=ALU.mult, op1=ALU.mult),
                              reads=[PS[bo], rs_, ght], writes=[og])

                def stageC1(p):
                    og = OGh[p % 2]
                    ogt = OGT[p % 2]
                    if RSTD_LATE:
                        sq_, sr_, rs_ = ssq_l[p % 4], sq_l[p % 4], rs_l[p % 4]
                        P.dve(lambda e, sq_=sq_, sr_=sr_: e.tensor_scalar(out=sr_[:, 0:1], in0=sq_[:, 0:1], scalar1=1.0 / 256.0, scalar2=RMS_EPS,
                                                                       op0=ALU.mult, op1=ALU.add), reads=[sq_], writes=[sr_])
                        P.op("pool", lambda e, sr_=sr_, rs_=rs_: e.tensor_tensor(out=rs_[:, 0:1], in0=sr_[:, 0:1], in1=nhalf[:, 0:1], op=ALU.pow),
                             reads=[sr_, nhalf], writes=[rs_])
                    bt = ps1()
                    pv = PSt[:, bt, :].bitcast(BF16)
                    for c in range(2):
                        P.pe(lambda e, c=c, pv=pv, og=og: e.transpose(out=pv[:, c * 128:(c + 1) * 128], in_=og[:, c * 128:(c + 1) * 128], identity=ident[:]),
                             reads=[og, ident], writes=[PS[bt]])
                    P.act(lambda e, pv=pv, ogt=ogt: e.copy(out=ogt[:, :, :], in_=pv[:, 0:256].rearrange("p (a b) -> p a b", b=128)), reads=[PS[bt]], writes=[ogt])

                def stageC2(p, h=h, wo=wo):
                    tt = p
                    ogt = OGT[p % 2]
                    rs_ = rs_l[p % 4]
                    b2 = ps2()
                    for nh in range(2):
                        for c in range(2):
                            P.pe(lambda e, c=c, nh=nh, b2=b2, ogt=ogt, wo=wo: e.matmul(PSt[:, b2 + nh, :], lhsT=ogt[:, c, :], rhs=wo[1][:, c, nh * 512:(nh + 1) * 512],
                                                                                start=(c == 0), stop=(c == 1)), reads=[ogt, wo[0]], writes=[PS[b2 + nh]])
                    pin = PSt[:, b2:b2 + 2, :].rearrange("p a b -> p (a b)")
                    sr_ = sq_l[p % 4]
                    if RSTD_LATE:
                        P.dve(lambda e, tt=tt, pin=pin, rs_=rs_: e.scalar_tensor_tensor(out=X[tt][:], in0=pin, scalar=rs_[:, 0:1], in1=X[tt][:], op0=ALU.mult, op1=ALU.add),
                              reads=[X[tt], PS[b2], PS[b2 + 1], rs_], writes=[X[tt]])
                    elif h == 0:
                        P.dve(lambda e, tt=tt, pin=pin: e.scalar_tensor_tensor(out=X[tt][:], in0=X[tt][:], scalar=ALPHA, in1=pin, op0=ALU.mult, op1=ALU.add),
                              reads=[X[tt], PS[b2], PS[b2 + 1]], writes=[X[tt]])
                    else:
                        P.dve(lambda e, tt=tt, pin=pin: e.tensor_tensor(out=X[tt][:], in0=X[tt][:], in1=pin, op=ALU.add),
                              reads=[X[tt], PS[b2], PS[b2 + 1]], writes=[X[tt]])
                    if h == 3:
                        ln_inplace(tt)
                        gxq.push(tt)

                prepA(0)
                prepA2(0)
                for t4 in range(4):
                    prep_vr(0, t4)
                prepB(0)
                prep2(0)
                stageA(0)
                for p in range(16):
                    gn = p // 4 + 1
                    if p + 4 < 16:
                        prep_vr(gn, p % 4)
                    if p + 1 < 16:
                        stageA(p + 1)
                    stageB(p)
                    if p >= 1:
                        stageC1(p - 1)
                    if p >= 2:
                        stageC2(p - 2)
                    if gn < 4:
                        if p % 4 == 0:
                            prepA(gn)
                        elif p % 4 == 1:
                            prepA2(gn)
                        elif p % 4 == 2:
                            prepB(gn)
                        elif p % 4 == 3:
                            prep2(gn)
                stageC1(15)
                stageC2(14)
                stageC2(15)
            gxq.flush()

        def mla_mixer(j, layer):
            XTf = XTb[:, :, :].rearrange("p a b -> p (a b)")
            QR = XTf[:, 0:8192].rearrange("p (a t) -> p a t", t=S)
            QRt = Tl(QR, "QR")
            OA = XTf[:, 8192:16384].rearrange("p (t n) -> p t n", n=512)
            OAt = Tl(OA, "OA")
            cqT = Hb[:, 0:4096].rearrange("p (c t) -> p c t", t=S)
            cqTt = Tl(cqT, "cqT")
            ckT = Hb[:, 4096:8192].rearrange("p (c t) -> p c t", t=S)
            ckTt = Tl(ckT, "ckT")
            krT = Hb[:, 8192:10240]
            krTt = Tl(krT, "krT")
            VH = [Hb[:, 10240 + i * 2080: 10240 + (i + 1) * 2080].rearrange("p (t n) -> p t n", n=130) for i in range(2)]
            VHt = [Tl(VH[i], "VH%d" % i) for i in range(2)]
            CN = Tl(Hb[:, 14400:14912], "CN")
            KR = Tl(Hb[:, 14912:15040], "KR")
            QRS = Tl(Hb[:, 15040:15552], "QRS")
            OAT = Hb[:, 15552:16064].rearrange("p (h t) -> p h t", t=128)
            OATt = Tl(OAT, "OAT")
            knT = Tl(SCb[:, 0:2048], "knT")
            qnT = Tl(SCb[:, 2048:4096], "qnT")
            PB = [Tl(SCb[:, 4096 + i * 512: 4096 + (i + 1) * 512], "PB%d" % i) for i in range(4)]
            COS = Tl(SCb[:, 6144:7168].bitcast(F32), "COS")
            SIN = Tl(SCb[:, 7168:8192].bitcast(F32), "SIN")
            gbc = Tl(SCb[:, 8192:9216].bitcast(F32), "gbc")
            ANG = Tl(SCb[:, 9216:10240].bitcast(F32), "ANG")
            NF = Tl(SCb[:, 10240:11264].bitcast(F32), "NF")
            NI = SCb[:, 10240:11264].bitcast(I32)
            T1 = ANG
            T2 = NF
            fence("H", [cqTt, ckTt, krTt] + VHt + [CN, KR, QRS, OATt])
            fence("SC", [knT, qnT] + PB + [COS, SIN, gbc, ANG, NF])
            msm = G_SM[6]
            mrs = G_SM[7]
            posi, posf, invf, rec = MLA_SM
            allXT = list(XT)
            scale = 192.0 ** -0.5
            PI = 3.1415925
            TWO_PI = 6.283185307179586
            C1 = 6.28125
            C2 = TWO_PI - C1
            cos3 = COS[:].rearrange("p (t i) -> p t i", i=32)
            sin3 = SIN[:].rearrange("p (t i) -> p t i", i=32)
            P.dma("sp", posi[:], d_pos[:, :], writes=[posi])
            P.dma("sp", invf[:], d_invf[:, :], writes=[invf])
            P.dma("sp", gbc[:], d_mlan[j].partition_broadcast(128), writes=[gbc])
            P.dve(lambda e: e.tensor_copy(out=posf[:], in_=posi[:]), reads=[posi], writes=[posf])
            for tt in range(NT):
                P.dve(lambda e, tt=tt: e.tensor_scalar(out=ANG[:, tt * 32:(tt + 1) * 32], in0=invf[:], scalar1=posf[:, tt:tt + 1], scalar2=None, op0=ALU.mult),
                      reads=[invf, posf], writes=[ANG])
            P.dve(lambda e: e.tensor_scalar(out=NI, in0=ANG[:], scalar1=1.0 / TWO_PI, scalar2=None, op0=ALU.mult), reads=[ANG], writes=[NF])
            P.dve(lambda e: e.tensor_copy(out=NF[:], in_=NI), reads=[NF], writes=[NF])
            P.dve(lambda e: e.scalar_tensor_tensor(out=ANG[:], in0=NF[:], scalar=-C1, in1=ANG[:], op0=ALU.mult, op1=ALU.add), reads=[NF, ANG], writes=[ANG])
            P.dve(lambda e: e.scalar_tensor_tensor(out=ANG[:], in0=NF[:], scalar=-C2, in1=ANG[:], op0=ALU.mult, op1=ALU.add), reads=[NF, ANG], writes=[ANG])
            P.dve(lambda e: e.tensor_scalar(out=ANG[:], in0=ANG[:], scalar1=-PI, scalar2=PI, op0=ALU.max, op1=ALU.min), reads=[ANG], writes=[ANG])
            P.act(lambda e: e.activation(out=SIN[:], in_=ANG[:], func=AF.Sin), reads=[ANG], writes=[SIN])
            P.dve(lambda e: e.tensor_scalar(out=ANG[:], in0=ANG[:], scalar1=TWO_PI / 4, scalar2=None, op0=ALU.add), reads=[ANG], writes=[ANG])
            P.dve(lambda e: e.tensor_scalar(out=NF[:], in0=ANG[:], scalar1=PI, scalar2=None, op0=ALU.is_gt), reads=[ANG], writes=[NF])
            P.dve(lambda e: e.scalar_tensor_tensor(out=ANG[:], in0=NF[:], scalar=-TWO_PI, in1=ANG[:], op0=ALU.mult, op1=ALU.add), reads=[NF, ANG], writes=[ANG])
            P.dve(lambda e: e.tensor_scalar(out=ANG[:], in0=ANG[:], scalar1=-PI, scalar2=PI, op0=ALU.max, op1=ALU.min), reads=[ANG], writes=[ANG])
            P.act(lambda e: e.activation(out=COS[:], in_=ANG[:], func=AF.Sin), reads=[ANG], writes=[COS])
            load_ln(layer, 0)
            if dbg == ("cos", layer):
                P.dma("sp", d_dbg[0:128, 0:512], COS[:], reads=[COS])
                P.dma("sp", d_dbg[0:128, 512:1024], SIN[:], reads=[SIN])

            def rope(xa, xb, o1, o2, cb, sb_, reads, wt):
                n = 1
                for s_ in xa.shape[1:]:
                    n *= s_
                t1 = T1[:, 0:n]
                t2 = T2[:, 0:n]
                if len(xa.shape) == 3:
                    t1 = t1.rearrange("p (a b) -> p a b", b=xa.shape[2])
                    t2 = t2.rearrange("p (a b) -> p a b", b=xa.shape[2])
                P.dve(lambda e: e.tensor_tensor(out=t1, in0=xa, in1=cb, op=ALU.mult), reads=reads + [COS], writes=[T1])
                P.dve(lambda e: e.tensor_tensor(out=t2, in0=xb, in1=sb_, op=ALU.mult), reads=reads + [SIN], writes=[T2])
                P.dve(lambda e: e.tensor_tensor(out=o1, in0=t1, in1=t2, op=ALU.subtract), reads=[T1, T2], writes=[wt])
                P.dve(lambda e: e.tensor_tensor(out=t1, in0=xb, in1=cb, op=ALU.mult), reads=reads + [COS], writes=[T1])
                P.dve(lambda e: e.tensor_tensor(out=t2, in0=xa, in1=sb_, op=ALU.mult), reads=reads + [SIN], writes=[T2])
                P.dve(lambda e: e.tensor_tensor(out=o2, in0=t1, in1=t2, op=ALU.add), reads=[T1, T2], writes=[wt])

            wA = wload(wview(d_mlaw[j], 0, 512), [128, 8, 512])
            wB = wload(wview(d_mlaw[j], 512, 64), [128, 8, 64])
            CNs = [CN, QRS]
            KR2 = Tl(Hb[:, 15552:15680], "KR2")
            KR2.rd = list(OATt.rd)
            KRs = [KR, KR2]
            ss_l = [Tl(mla_f[:, 52 + 2 * k:54 + 2 * k], "mss%d" % k, True) for k in range(2)]
            tt_l = [Tl(mla_f[:, 56 + 2 * k:58 + 2 * k], "mtt%d" % k, True) for k in range(2)]
            rs_l = [Tl(mla_f[:, 60 + 2 * k:62 + 2 * k], "mrs%d" % k, True) for k in range(2)]

            def c_s1(tt):
                k = tt % 2
                cn, kr, ss, t_, rs = CNs[k], KRs[k], ss_l[k], tt_l[k], rs_l[k]
                b1 = ps1()
                for kc in range(KC):
                    P.pe(lambda e, kc=kc, b1=b1, tt=tt: e.matmul(PSt[:, b1, :], lhsT=XTb[:, kc, tt * 128:(tt + 1) * 128], rhs=wA[1][:, kc, :],
                                                             start=(kc == 0), stop=(kc == KC - 1)), reads=[XT[tt], wA[0]], writes=[PS[b1]])
                b2 = ps1()
                for kc in range(KC):
                    P.pe(lambda e, kc=kc, b2=b2, tt=tt: e.matmul(PSt[:, b2, 0:64], lhsT=XTb[:, kc, tt * 128:(tt + 1) * 128], rhs=wB[1][:, kc, :],
                                                             start=(kc == 0), stop=(kc == KC - 1)), reads=[XT[tt], wB[0]], writes=[PS[b2]])
                jk = nstg()
                for c in range(2):
                    P.act(lambda e, c=c, b1=b1, jk=jk, ss=ss: e.activation(out=jk[:, 0:256], in_=PSt[:, b1, c * 256:(c + 1) * 256], func=AF.Square,
                                                                       accum_out=ss[:, c:c + 1]), reads=[PS[b1]], writes=[jk, ss])
                P.dve(lambda e, ss=ss, t_=t_: e.tensor_scalar(out=t_[:, 0:2], in0=ss[:, 0:2], scalar1=1.0 / 256.0, scalar2=RMS_EPS, op0=ALU.mult, op1=ALU.add),
                      reads=[ss], writes=[t_])
                P.op("pool", lambda e, t_=t_, rs=rs: e.tensor_tensor(out=rs[:, 0:2], in0=t_[:, 0:2], in1=nhalf[:, 0:2], op=ALU.pow), reads=[t_, nhalf], writes=[rs])
                rope(PSt[:, b2, 0:32], PSt[:, b2, 32:64], kr[:, 0:32], kr[:, 32:64], cos3[:, tt, :], sin3[:, tt, :], [PS[b2]], kr)
                P.act(lambda e, kr=kr: e.copy(out=kr[:, 64:128], in_=kr[:, 0:64]), reads=[kr], writes=[kr])
                for c in range(2):
                    P.dve(lambda e, c=c, b1=b1, cn=cn, rs=rs: e.scalar_tensor_tensor(out=cn[:, c * 256:(c + 1) * 256], in0=PSt[:, b1, c * 256:(c + 1) * 256],
                                                                                scalar=rs[:, c:c + 1], in1=gbc[:, c * 256:(c + 1) * 256], op0=ALU.mult, op1=ALU.mult),
                          reads=[PS[b1], rs, gbc], writes=[cn])

            def c_s2(tt):
                k = tt % 2
                cn, kr = CNs[k], KRs[k]
                bt = ps1()
                pv = PSt[:, bt, :].bitcast(BF16)
                for c in range(4):
                    P.pe(lambda e, c=c, pv=pv, cn=cn: e.transpose(out=pv[:, c * 128:(c + 1) * 128], in_=cn[:, c * 128:(c + 1) * 128], identity=ident[:]),
                         reads=[cn, ident], writes=[PS[bt]])
                P.pe(lambda e, pv=pv, kr=kr: e.transpose(out=pv[:, 512:640], in_=kr[:], identity=ident[:]), reads=[kr, ident], writes=[PS[bt]])
                P.act(lambda e, pv=pv, tt=tt: e.copy(out=cqT[:, :, tt * 128:(tt + 1) * 128], in_=pv[:, 0:256].rearrange("p (a b) -> p a b", b=128)),
                      reads=[PS[bt]], writes=[cqTt])
                P.act(lambda e, pv=pv, tt=tt: e.copy(out=ckT[:, :, tt * 128:(tt + 1) * 128], in_=pv[:, 256:512].rearrange("p (a b) -> p a b", b=128)),
                      reads=[PS[bt]], writes=[ckTt])
                P.act(lambda e, pv=pv, tt=tt: e.copy(out=krT[:, tt * 128:(tt + 1) * 128], in_=pv[:, 512:640]), reads=[PS[bt]], writes=[krTt])

            c_s1(0)
            for tt in range(NT):
                if tt + 1 < NT:
                    c_s1(tt + 1)
                c_s2(tt)
            OATt.rd = OATt.rd + ([KR2.lw] if KR2.lw is not None else []) + KR2.rd
            prior = []
            for t in XT:
                if t.lw is not None:
                    prior.append(t.lw)
                prior.extend(t.rd)
            QRt.rd = list(prior)
            OAt.rd = list(prior)
            wqp, wqrv = walloc(1024)
            wqrv = wqrv.rearrange("p (a n) -> p a n", n=512)
            for rc in range(2):
                P.dma("pool", wqrv[:, rc, :].rearrange("p (h c) -> p h c", c=64),
                      d_mlauq[j][rc * 128:(rc + 1) * 128, :].rearrange("r (h c) -> r h c", c=192)[:, :, 128:192], writes=[wqp])
            wqr = (wqp, wqrv)
            QRSs = [QRS, CN]

            def q_s1(tt):
                qrs = QRSs[tt % 2]
                b1 = ps1()
                for rc in range(2):
                    P.pe(lambda e, rc=rc, b1=b1, tt=tt: e.matmul(PSt[:, b1, :], lhsT=cqT[:, rc, tt * 128:(tt + 1) * 128], rhs=wqrv[:, rc, :],
                                                             start=(rc == 0), stop=(rc == 1)), reads=[cqTt, wqr[0]], writes=[PS[b1]])
                p3 = PSt[:, b1, :].rearrange("p (h c) -> p h c", c=64)
                q3 = qrs[:].rearrange("p (h c) -> p h c", c=64)
                cb = cos3[:, tt, :].unsqueeze(1).to_broadcast([128, 8, 32])
                sb_ = sin3[:, tt, :].unsqueeze(1).to_broadcast([128, 8, 32])
                rope(p3[:, :, 0:32], p3[:, :, 32:64], q3[:, :, 0:32], q3[:, :, 32:64], cb, sb_, [PS[b1]], qrs)

            def q_s2(tt):
                qrs = QRSs[tt % 2]
                bt = ps1()
                pv = PSt[:, bt, :].bitcast(BF16)
                for c in range(4):
                    P.pe(lambda e, c=c, pv=pv, qrs=qrs: e.transpose(out=pv[:, c * 128:(c + 1) * 128], in_=qrs[:, c * 128:(c + 1) * 128], identity=ident[:]),
                         reads=[qrs, ident], writes=[PS[bt]])
                P.act(lambda e, pv=pv, tt=tt: e.copy(out=QR[:, :, tt * 128:(tt + 1) * 128], in_=pv[:, 0:512].rearrange("p (a b) -> p a b", b=128)),
                      reads=[PS[bt]], writes=[QRt])

            q_s1(0)
            for tt in range(NT):
                if tt + 1 < NT:
                    q_s1(tt + 1)
                q_s2(tt)
            if dbg == ("qr", layer):
                P.dma("sp", d_dbg[0:128, :].bitcast(BF16), XTf[:, 0:2048], reads=[QRt] + allXT)
                P.dma("sp", d_dbg[128:256, :].bitcast(BF16), krT[:, :], reads=[krTt])
                P.dma("sp", d_dbg[256:384, :].bitcast(BF16), cqT[:, 0, :], reads=[cqTt])
            Z2 = SCb[:, 6144:8192]
            Z2t = Tl(Z2, "Z2")
            Z2t.rd = [x for x in (COS.lw, SIN.lw) if x is not None] + COS.rd + SIN.rd
            P.dve(lambda e: e.memset(Z2[0:64, :], 0.0), writes=[Z2t])
            P.act(lambda e: e.copy(out=Z2[64:128, :], in_=krT[64:128, :]), reads=[krTt], writes=[Z2t])
            P.dve(lambda e: e.memset(krT[64:128, :], 0.0), reads=[Z2t], writes=[krTt])
            for i in range(2):
                P.dve(lambda e, i=i: e.memset(VH[i][:, :, 128:130], 1.0), writes=[VHt[i]])
            for hf in range(2):
                wo = wload(d_mlawo[j][hf * 512:(hf + 1) * 512, :].rearrange("(h d) n -> d h n", d=128), [128, 4, 1024])
                for hl in range(4):
                    h = hf * 4 + hl
                    vh, vht = VH[h % 2], VHt[h % 2]
                    wkv = wload(wview(d_mlaukv[j], h * 256, 256), [128, 2, 256])
                    wqn = wload(wview(d_mlauq[j], h * 192, 128), [128, 2, 128])
                    for g in range(4):
                        b = ps1()
                        for rc in range(2):
                            P.pe(lambda e, rc=rc, b=b, g=g, wkv=wkv: e.matmul(PSt[:, b, :], lhsT=wkv[1][:, rc, 0:128], rhs=ckT[:, rc, g * 512:(g + 1) * 512],
                                                                 start=(rc == 0), stop=(rc == 1)), reads=[wkv[0], ckTt], writes=[PS[b]])
                        P.act(lambda e, b=b, g=g: e.copy(out=knT[:, g * 512:(g + 1) * 512], in_=PSt[:, b, :]), reads=[PS[b]], writes=[knT])
                        b = ps1()
                        for rc in range(2):
                            P.pe(lambda e, rc=rc, b=b, g=g, wqn=wqn: e.matmul(PSt[:, b, :], lhsT=wqn[1][:, rc, :], rhs=cqT[:, rc, g * 512:(g + 1) * 512],
                                                                 start=(rc == 0), stop=(rc == 1)), reads=[wqn[0], cqTt], writes=[PS[b]])
                        P.act(lambda e, b=b, g=g: e.copy(out=qnT[:, g * 512:(g + 1) * 512], in_=PSt[:, b, :]), reads=[PS[b]], writes=[qnT])
                        b = ps1()
                        for t4 in range(4):
                            tt = 4 * g + t4
                            for rc in range(2):
                                P.pe(lambda e, rc=rc, b=b, tt=tt, t4=t4, wkv=wkv: e.matmul(PSt[:, b, t4 * 128:(t4 + 1) * 128], lhsT=ckT[:, rc, tt * 128:(tt + 1) * 128],
                                                                              rhs=wkv[1][:, rc, 128:256], start=(rc == 0), stop=(rc == 1)),
                                     reads=[wkv[0], ckTt], writes=[PS[b]])
                        P.act(lambda e, b=b, g=g, vh=vh: e.copy(out=vh[:, 4 * g:4 * g + 4, 0:128], in_=PSt[:, b, :].rearrange("p (a b) -> p a b", b=128)),
                              reads=[PS[b]], writes=[vht])
                    pb_ = (h % 2) * 64
                    hp = h // 2
                    for g in range(4):
                        ob = [ps1() for _ in range(4)]
                        C.resv = set(ob)
                        njk = 4 * g + 4

                        def emit_S(jk_, g=g):
                            n0 = max(0, jk_ - 4 * g) * 128
                            N = 512 - n0
                            bS = ps1()
                            C.resv.add(bS)
                            P.pe(lambda e, bS=bS, jk_=jk_, g=g, n0=n0, N=N: e.matmul(PSt[:, bS, 0:N], lhsT=knT[:, jk_ * 128:(jk_ + 1) * 128],
                                                                                rhs=qnT[:, g * 512 + n0:(g + 1) * 512], start=True, stop=False),
                                 reads=[knT, qnT], writes=[PS[bS]])
                            kz = krT if pb_ == 0 else Z2
                            P.pe(lambda e, bS=bS, jk_=jk_, g=g, n0=n0, N=N, kz=kz, hp=hp: e.matmul(
                                PSt[:, bS, 0:N], lhsT=kz[:, jk_ * 128:(jk_ + 1) * 128], rhs=QR[:, hp, g * 512 + n0:(g + 1) * 512],
                                start=False, stop=True), reads=[krTt, Z2t, QRt], writes=[PS[bS]])
                            return bS, n0, N

                        LA = 3
                        pendq = [emit_S(i_) for i_ in range(min(LA, njk))]
                        for jk_ in range(njk):
                            bS, n0, N = pendq.pop(0)
                            if jk_ + LA < njk:
                                pendq.append(emit_S(jk_ + LA))
                            pbuf = PB[jk_ % 4]
                            P.act(lambda e, bS=bS, N=N, pbuf=pbuf: e.activation(out=pbuf[:, 0:N], in_=PSt[:, bS, 0:N], func=AF.Exp, scale=scale),
                                  reads=[PS[bS]], writes=[pbuf])
                            C.resv.discard(bS)
                            if jk_ >= 4 * g:
                                P.dve(lambda e, pbuf=pbuf: e.memset(pbuf[64:128, 0:64], 0.0), writes=[pbuf])
                            for qi in range(n0 // 128, 4):
                                c0 = qi * 128 - n0
                                P.pe(lambda e, qi=qi, c0=c0, pbuf=pbuf, jk_=jk_, vh=vh, ob=ob: e.matmul(
                                    PSt[:, ob[qi], 0:129], lhsT=pbuf[:, c0:c0 + 128], rhs=vh[:, jk_, 0:129], start=(jk_ == 0), stop=(jk_ == 4 * g + qi)),
                                    reads=[pbuf, vht], writes=[PS[ob[qi]]])
                        for qi in range(4):
                            tt = 4 * g + qi
                            rq = rec[qi]
                            P.dve(lambda e, qi=qi, ob=ob, rq=rq: e.reciprocal(out=rq[:, 0:1], in_=PSt[:, ob[qi], 128:129]), reads=[PS[ob[qi]]], writes=[rq])
                            P.act(lambda e, qi=qi, ob=ob, tt=tt, hl=hl, rq=rq: e.activation(out=OA[:, tt, hl * 128:(hl + 1) * 128], in_=PSt[:, ob[qi], 0:128],
                                                                                        func=AF.Copy, scale=rq[:, 0:1]),
                                  reads=[PS[ob[qi]], rq], writes=[OAt])
                        C.resv = set()
                for tt in range(NT):
                    bt = ps1()
                    pv = PSt[:, bt, :].bitcast(BF16)
                    for c in range(4):
                        P.pe(lambda e, c=c, pv=pv, tt=tt: e.transpose(out=pv[:, c * 128:(c + 1) * 128], in_=OA[:, tt, c * 128:(c + 1) * 128], identity=ident[:]),
                             reads=[OAt, ident], writes=[PS[bt]])
                    P.act(lambda e, pv=pv: e.copy(out=OAT[:, :, :], in_=pv[:, 0:512].rearrange("p (a b) -> p a b", b=128)), reads=[PS[bt]], writes=[OATt])
                    b2 = ps2()
                    for nh in range(2):
                        for c in range(4):
                            P.pe(lambda e, c=c, nh=nh, b2=b2, wo=wo: e.matmul(PSt[:, b2 + nh, :], lhsT=OAT[:, c, :], rhs=wo[1][:, c, nh * 512:(nh + 1) * 512],
                                                                   start=(c == 0), stop=(c == 3)), reads=[OATt, wo[0]], writes=[PS[b2 + nh]])
                    pin = PSt[:, b2:b2 + 2, :].rearrange("p a b -> p (a b)")
                    if hf == 0:
                        P.dve(lambda e, tt=tt, pin=pin: e.scalar_tensor_tensor(out=X[tt][:], in0=X[tt][:], scalar=ALPHA, in1=pin, op0=ALU.mult, op1=ALU.add),
                              reads=[X[tt], PS[b2], PS[b2 + 1]], writes=[X[tt]])
                    else:
                        P.dve(lambda e, tt=tt, pin=pin: e.tensor_tensor(out=X[tt][:], in0=X[tt][:], in1=pin, op=ALU.add),
                              reads=[X[tt], PS[b2], PS[b2 + 1]], writes=[X[tt]])
                        if dbg == ("h", layer):
                            P.dma("sp", d_dbg[tt * 128:(tt + 1) * 128, :], X[tt][:], reads=[X[tt]])
                        ln_inplace(tt)
            tail = [x for x in (QRt.lw, OAt.lw) if x is not None] + QRt.rd + OAt.rd
            for t in XT:
                t.rd = t.rd + tail
            if dbg == ("oa", layer):
                P.dma("sp", d_dbg[0:128, :].bitcast(BF16), XTf[:, 8192:10240], reads=[OAt] + allXT)
                P.dma("sp", d_dbg[128:256, :].bitcast(BF16), SCb[:, 0:2048], reads=[knT])
                P.dma("sp", d_dbg[256:384, :].bitcast(BF16), SCb[:, 2048:4096], reads=[qnT])
                P.dma("sp", d_dbg[384:512, :].bitcast(BF16), Hb[:, 10240 + 2080:10240 + 2080 + 2048], reads=VHt)
            for tt in range(NT):
                to_xt(tt)

        for i_ in range(8):
            P.dma("sp", Xb[:, 2 * i_:2 * i_ + 2, :], d_x[256 * i_:256 * (i_ + 1), :].rearrange("(t q) d -> q t d", q=128), writes=X[2 * i_:2 * i_ + 2])
        for tt in range(NT):
            to_xt(tt)
        for li, layer in enumerate(layers):
            kind = layer % 3
            jj = layer // 3
            if kind == 0:
                gla_mixer(jj, layer)
            elif kind == 1:
                mla_mixer(jj, layer)
            else:
                conv_mixer(jj, layer)
            if dbg == ("a", layer):
                P.dma("sp", d_dbg.rearrange("(t q) d -> q t d", q=128), Xb[:, :, :], reads=X)
            down = mlp(layer)
            ple(layer, last=(li == len(layers) - 1), down=down)
        P.dma("sp", d_out[0:1024, :].rearrange("(t q) d -> q t d", q=128), Xb[:, 0:8, :], reads=X[0:8])
        P.dma("sp", d_out[1024:2048, :].rearrange("(t q) d -> q t d", q=128), Xb[:, 8:16, :], reads=X[8:16])
        P.emit(st)
    return nc


def host_consts():
    c = {}
    c["c_ident"] = np.eye(128, dtype=np.float32)
    s_ = np.arange(128)[:, None]
    t_ = np.arange(128)[None, :]
    mu = (t_ >= s_).astype(np.float32)
    ml = ((t_ < s_) & ((t_ // 64) == (s_ // 64))).astype(np.float32)
    c["c_gmask"] = np.concatenate([mu, ml], axis=1)
    rm = np.ones((128, 512), np.float32)
    rm[:, 0::128] = 0.0
    c["c_rmask"] = rm
    invf = (10000.0 ** (-np.arange(0, 32, dtype=np.float32) * np.float32(2.0 / 64))).astype(np.float32)
    c["c_invf"] = np.broadcast_to(invf[None, :], (128, 32)).copy()
    return c


def make_in_maps(inputs, xs):
    f = lambda a: np.ascontiguousarray(np.asarray(a, dtype=np.float32))
    shared = {
        "gla_w_in": f(inputs["gla_w_in"]), "gla_w_gate_up": f(inputs["gla_w_gate_up"]),
        "gla_b_gate_t": f(np.asarray(inputs["gla_b_gate"]).reshape(2, 4, 128).transpose(0, 2, 1)),
        "gla_norm_g_t": f(np.asarray(inputs["gla_norm_g"]).reshape(2, 2, 128).transpose(0, 2, 1)),
        "gla_w_out": f(inputs["gla_w_out"]),
        "mla_w_in": f(inputs["mla_w_in"]),
        "mla_norms": f(np.concatenate([np.asarray(inputs["mla_q_norm"]), np.asarray(inputs["mla_kv_norm"])], axis=1)),
        "mla_w_uq": f(inputs["mla_w_uq"]), "mla_w_ukv": f(inputs["mla_w_ukv"]), "mla_w_out": f(inputs["mla_w_out"]),
        "conv_w_in": f(inputs["conv_w_in"]),
        "conv_w_t": f(np.asarray(inputs["conv_w"]).reshape(1, 3, 8, 128).transpose(0, 3, 2, 1)),
        "conv_w_out": f(inputs["conv_w_out"]),
        "ln_g": f(inputs["ln_g"]), "ln_b": f(inputs["ln_b"]),
        "mlp_w1": f(inputs["mlp_w1"]), "mlp_w2": f(inputs["mlp_w2"]),
        "ple_w_gate": f(inputs["ple_w_gate"]), "ple_w_proj": f(inputs["ple_w_proj"]),
    }
    shared.update(host_consts())
    maps = []
    p = np.asarray(inputs["p"], dtype=np.float32)
    pos = np.asarray(inputs["positions"]).astype(np.int32)
    for c, xc in enumerate(xs):
        m = dict(shared)
        m["x"] = f(xc)
        m["p"] = np.ascontiguousarray(p[:, c])
        m["pos"] = np.ascontiguousarray(pos[c].reshape(NT, 128).T)
        maps.append(m)
    return maps


_NC_CACHE = {}


def kernel(**inputs):
    x = np.asarray(inputs["x"], dtype=np.float32)
    B = x.shape[0]
    key = "all"
    if key not in _NC_CACHE:
        _NC_CACHE[key] = build_nc([0, 1, 2, 3])
    nc = _NC_CACHE[key]
    maps = make_in_maps(inputs, [x[b] for b in range(B)])
    res = run_bass_kernel_spmd(nc, maps, core_ids=list(range(B)))
    return np.stack([np.asarray(r["out"], dtype=np.float32) for r in res.results], axis=0)
```
